# Optimizing a Trainium2 kernel written in Bass

```python
import jax, jax.numpy as jnp
from jax import lax
import numpy as np

D_MODEL = 2048
BATCH = 16
SEQ = 2048
DEPTH = 1
DEC_BATCH = 4
DEC_SEQ = 2048
PAST_LEN = 128

MIX_WIDTH = D_MODEL
RWKV_WIDTH = MIX_WIDTH // 2
CONV_WIDTH = MIX_WIDTH - RWKV_WIDTH
HEAD_SIZE = 64
RWKV_HEADS = RWKV_WIDTH // HEAD_SIZE
CONV_TAPS = 3
DECAY_RANK = 64
AAA_RANK = 64
GATE_RANK = 160
RWKV_COLS = 3 * RWKV_WIDTH + 2 * DECAY_RANK + AAA_RANK + GATE_RANK
CONV_COLS = 3 * CONV_WIDTH
IN_COLS = RWKV_COLS + CONV_COLS
D_FF = -(-8 * D_MODEL // (3 * 256)) * 256
N_MOD = 6
RMS_EPS = 1e-6
LNX_EPS = 64e-5

kernel_name = "hymba_rwkv7_shortconv_adaln_encoder"


def rms_norm(x, g):
    xf = x.astype(jnp.float32)
    y = xf * lax.rsqrt(jnp.mean(xf * xf, axis=-1, keepdims=True) + RMS_EPS)
    return y.astype(x.dtype) * g


def bidir_token_shift(h):
    prev = jnp.pad(h[:, :-1], ((0, 0), (1, 0), (0, 0)))
    nxt = jnp.pad(h[:, 1:], ((0, 0), (0, 1), (0, 0)))
    even = (jnp.arange(h.shape[-1]) % 2) == 0
    return jnp.where(even, prev, nxt)


def rwkv7_step(S, inp):
    r_t, w_t, k_t, v_t, kk_t, b_t = inp
    sa = jnp.einsum('dbhij,dbhj->dbhi', S, -kk_t)
    S = S * w_t[..., None, :] + sa[..., :, None] * b_t[..., None, :] + v_t[..., :, None] * k_t[..., None, :]
    y = jnp.einsum('dbhij,dbhj->dbhi', S, r_t)
    return S, y


def rwkv7_time_mix(h_r, h_k, h_v, h_wf, h_wb, h_a, h_g,
                   w0_decay, w2_decay, a0, a2, g2_gate, k_k, k_a, r_k, lnx_g, lnx_b):
    f32 = jnp.float32
    bsz, T, C = h_r.shape
    heads = lambda t: t.reshape(bsz, T, RWKV_HEADS, HEAD_SIZE)
    w_lora = jnp.stack([jnp.tanh(h_wf) @ w2_decay[0], jnp.tanh(h_wb) @ w2_decay[1]])
    w_log = -jax.nn.softplus(-(w0_decay[:, None, None, :] + w_lora).astype(f32)) - 0.5
    decay = jnp.exp(-jnp.exp(w_log)).reshape(2, bsz, T, RWKV_HEADS, HEAD_SIZE)
    a = jax.nn.sigmoid(a0 + h_a @ a2)
    g = jax.nn.sigmoid(h_g) @ g2_gate
    kk = heads(h_k * k_k).astype(f32)
    kk = kk / jnp.maximum(jnp.sqrt(jnp.sum(kk * kk, axis=-1, keepdims=True)), 1e-12)
    k = h_k * (1 + (a - 1) * k_a)
    r = heads(h_r).astype(f32)
    k = heads(k).astype(f32)
    v = heads(h_v).astype(f32)
    a = heads(a).astype(f32)

    def both(t):
        return jnp.moveaxis(jnp.stack([t, t[:, ::-1]]), 2, 0)

    dec_s = jnp.moveaxis(jnp.stack([decay[0], decay[1][:, ::-1]]), 2, 0)
    xs = (both(r), dec_s, both(k), both(v), both(kk), both(kk * a))
    S0 = jnp.zeros((2, bsz, RWKV_HEADS, HEAD_SIZE, HEAD_SIZE), f32)
    _, ys = lax.scan(rwkv7_step, S0, xs)
    y = jnp.moveaxis(ys[:, 0], 0, 1) + jnp.moveaxis(ys[::-1, 1], 0, 1)
    mean = jnp.mean(y, axis=-1, keepdims=True)
    var = jnp.mean(jnp.square(y - mean), axis=-1, keepdims=True)
    y = ((y - mean) * lax.rsqrt(var + LNX_EPS)).reshape(bsz, T, C) * lnx_g + lnx_b
    bonus = jnp.sum(r * k * r_k, axis=-1, keepdims=True) * v
    y = y + bonus.reshape(bsz, T, C)
    return (y * g).astype(h_r.dtype)


def short_conv_mixer(h_b, h_c, h_x, conv_w):
    z = h_c * h_x
    T = z.shape[1]
    zp = jnp.pad(z, ((0, 0), (1, 1), (0, 0)))
    conv = zp[:, :T] * conv_w[0] + zp[:, 1:T + 1] * conv_w[1] + zp[:, 2:] * conv_w[2]
    return h_b * conv


def encoder_layer(x, c, w_ada, b_ada, norm1_g, w_in, mu_shift, w0_decay, w2_decay, a0, a2, g2_gate,
                  k_k, k_a, r_k, lnx_g, lnx_b, conv_w, w_out, norm2_g, w_ffn_gate, w_ffn_up, w_ffn_down):
    mod = jax.nn.silu(c) @ w_ada + b_ada
    sh1, sc1, gt1, sh2, sc2, gt2 = jnp.split(mod[:, None, :], N_MOD, axis=-1)
    hn = rms_norm(x, norm1_g) * (1 + sc1) + sh1
    h = hn @ w_in
    h_rw, h_cv = h[..., :RWKV_COLS], h[..., RWKV_COLS:]
    h_rw = h_rw + mu_shift * (bidir_token_shift(h_rw) - h_rw)
    cuts = [RWKV_WIDTH, 2 * RWKV_WIDTH, 3 * RWKV_WIDTH, 3 * RWKV_WIDTH + DECAY_RANK,
            3 * RWKV_WIDTH + 2 * DECAY_RANK, 3 * RWKV_WIDTH + 2 * DECAY_RANK + AAA_RANK]
    h_r, h_k, h_v, h_wf, h_wb, h_a, h_g = jnp.split(h_rw, cuts, axis=-1)
    h_b, h_c, h_x = jnp.split(h_cv, 3, axis=-1)
    y_rwkv = rwkv7_time_mix(h_r, h_k, h_v, h_wf, h_wb, h_a, h_g, w0_decay, w2_decay, a0, a2, g2_gate,
                            k_k, k_a, r_k, lnx_g, lnx_b)
    y_conv = short_conv_mixer(h_b, h_c, h_x, conv_w)
    x = x + gt1 * (jnp.concatenate([y_rwkv, y_conv], axis=-1) @ w_out)
    hn = rms_norm(x, norm2_g) * (1 + sc2) + sh2
    ff = (jax.nn.silu(hn @ w_ffn_gate) * (hn @ w_ffn_up)) @ w_ffn_down
    return x + gt2 * ff


def setup_inputs(seed: int = 0) -> dict:
    key = jax.random.key(seed)
    ks = jax.random.split(key, 32)
    L, D, f32 = DEPTH, D_MODEL, jnp.float32
    nrm = lambda k, shape, s: jax.random.normal(k, shape, f32) * s
    return {
        "x_prompt": nrm(ks[0], (BATCH, SEQ, D), 1.0),
        "x_sample": nrm(ks[1], (DEC_BATCH, DEC_SEQ, D), 1.0),
        "c_prompt": nrm(ks[2], (BATCH, D), 1.0),
        "c_sample": nrm(ks[3], (DEC_BATCH, D), 1.0),
        "w_ada": nrm(ks[4], (L, D, N_MOD * D), 0.5 * D ** -0.5),
        "b_ada": nrm(ks[5], (L, N_MOD * D), 0.02),
        "norm1_g": 1.0 + nrm(ks[6], (L, D), 0.02),
        "w_in": nrm(ks[7], (L, D, IN_COLS), D ** -0.5),
        "mu_shift": jax.random.uniform(ks[8], (L, RWKV_COLS), f32, 0.0, 1.0),
        "w0_decay": jax.random.uniform(ks[9], (L, 2, RWKV_WIDTH), f32, -5.0, 0.5),
        "w2_decay": nrm(ks[10], (L, 2, DECAY_RANK, RWKV_WIDTH), 0.1 * DECAY_RANK ** -0.5),
        "a0": nrm(ks[11], (L, RWKV_WIDTH), 0.1),
        "a2": nrm(ks[12], (L, AAA_RANK, RWKV_WIDTH), 0.5 * AAA_RANK ** -0.5),
        "g2_gate": nrm(ks[13], (L, GATE_RANK, RWKV_WIDTH), GATE_RANK ** -0.5),
        "k_k": 0.85 + nrm(ks[14], (L, RWKV_WIDTH), 0.02),
        "k_a": 1.0 + nrm(ks[15], (L, RWKV_WIDTH), 0.02),
        "r_k": nrm(ks[16], (L, RWKV_HEADS, HEAD_SIZE), 0.1),
        "lnx_g": 1.0 + nrm(ks[17], (L, RWKV_WIDTH), 0.02),
        "lnx_b": nrm(ks[18], (L, RWKV_WIDTH), 0.02),
        "conv_w": nrm(ks[19], (L, CONV_TAPS, CONV_WIDTH), CONV_TAPS ** -0.5),
        "w_out": nrm(ks[20], (L, MIX_WIDTH, D), MIX_WIDTH ** -0.5),
        "norm2_g": 1.0 + nrm(ks[21], (L, D), 0.02),
        "w_ffn_gate": nrm(ks[22], (L, D, D_FF), D ** -0.5),
        "w_ffn_up": nrm(ks[23], (L, D, D_FF), D ** -0.5),
        "w_ffn_down": nrm(ks[24], (L, D_FF, D), D_FF ** -0.5),
        "norm_f_g": 1.0 + nrm(ks[25], (D,), 0.02),
    }


def reference(x_prompt, x_sample, c_prompt, c_sample, w_ada, b_ada, norm1_g, w_in, mu_shift, w0_decay,
              w2_decay, a0, a2, g2_gate, k_k, k_a, r_k, lnx_g, lnx_b, conv_w, w_out, norm2_g,
              w_ffn_gate, w_ffn_up, w_ffn_down, norm_f_g):
    y_p, y_s = x_prompt, x_sample
    for l in range(DEPTH):
        layer = (w_ada[l], b_ada[l], norm1_g[l], w_in[l], mu_shift[l], w0_decay[l], w2_decay[l], a0[l],
                 a2[l], g2_gate[l], k_k[l], k_a[l], r_k[l], lnx_g[l], lnx_b[l], conv_w[l], w_out[l],
                 norm2_g[l], w_ffn_gate[l], w_ffn_up[l], w_ffn_down[l])
        y_p = encoder_layer(y_p, c_prompt, *layer)
        y_s = encoder_layer(y_s, c_sample, *layer)
    y_prompt = rms_norm(y_p, norm_f_g)
    y_sample = rms_norm(y_s, norm_f_g)
    return (y_prompt, y_sample)
```

```python
import math
from contextlib import ExitStack
import numpy as np
import concourse.bass as bass
import concourse.mybir as mybir
from concourse.bass_utils import run_bass_kernel_spmd

F32 = mybir.dt.float32
BF16 = mybir.dt.bfloat16
AF = mybir.ActivationFunctionType
ALU = mybir.AluOpType
AX = mybir.AxisListType

D = 2048
KC = 16
RW = 1024
RWC = 3424
IN_COLS = 6496
DFF = 5632
FC = 44
NMOD = 6
RMS_EPS = 1e-6
LNX_EPS = 64e-5
C0 = -math.exp(-0.5)

GROUPS = []
for i in range(8):
    GROUPS.append([(i * 128, 128, 0)])
for i in range(8):
    GROUPS.append([(1024 + i * 128, 128, 0)])
for i in range(8):
    GROUPS.append([(2048 + i * 128, 128, 0)])
G_WFWB = len(GROUPS); GROUPS.append([(3072, 128, 0)])
G_AG1 = len(GROUPS); GROUPS.append([(3200, 64, 0), (3392, 32, 64), (3392, 32, 96)])
G_G0 = len(GROUPS); GROUPS.append([(3264, 128, 0)])
G_CONV = len(GROUPS)
for j in range(3):
    for i in range(8):
        GROUPS.append([(RWC + j * 1024 + i * 128, 128, 0)])
NG = len(GROUPS)


class Prog:
    ENGS = ("pe", "act", "dve", "pool", "sp")

    def __init__(self):
        self.ops = []
        self.last_w = {}
        self.readers = {}
        self.chan_cnt = {}

    def add(self, eng, fn, r=(), w=(), chan=None):
        i = len(self.ops)
        deps = set()
        for t in r:
            lw = self.last_w.get(t)
            if lw is not None:
                deps.add(lw)
        for t in w:
            lw = self.last_w.get(t)
            if lw is not None:
                deps.add(lw)
            for rd in self.readers.get(t, ()):
                deps.add(rd)
        for t in r:
            self.readers.setdefault(t, []).append(i)
        for t in w:
            self.last_w[t] = i
            self.readers[t] = []
        cval = None
        if chan is not None:
            self.chan_cnt[chan] = self.chan_cnt.get(chan, 0) + 1
            cval = 16 * self.chan_cnt[chan]
        import sys as _s
        fr = _s._getframe(1)
        self.ops.append(dict(eng=eng, fn=fn, deps=deps, chan=chan, cval=cval, sig=False, seq=0, tag=fr.f_lineno))
        return i

    def barrier(self):
        last = {}
        for i, op in enumerate(self.ops):
            if op["fn"] is None:
                continue
            if op["chan"] is not None:
                last[("c", op["chan"])] = i
            else:
                last[("e", op["eng"])] = i
        deps = set(last.values())
        for e in self.ENGS:
            self.ops.append(dict(eng=e, fn=None, deps=set(deps), chan=None, cval=None, sig=False, seq=0))
        self.last_w = {}
        self.readers = {}

    def finish(self):
        self.barrier()

    def simulate(self):
        import bisect
        ops = self.ops
        for op in ops:
            for d in op["deps"]:
                if ops[d]["chan"] is None:
                    ops[d]["sig"] = True
        cnt = {e: 0 for e in self.ENGS}
        for op in ops:
            if op["chan"] is None and op["sig"] and op["fn"] is not None:
                cnt[op["eng"]] += 1
                op["seq"] = cnt[op["eng"]]
        chan_ops = {}
        for i, op in enumerate(ops):
            op["idx"] = i
            if op["chan"] is not None:
                chan_ops.setdefault(op["chan"], []).append(i)
        per = {e: [op for op in ops if op["eng"] == e] for e in self.ENGS}
        pos = {e: 0 for e in self.ENGS}
        sem = {}
        progress = True
        while progress:
            progress = False
            for e in self.ENGS:
                while pos[e] < len(per[e]):
                    op = per[e][pos[e]]
                    ok = True
                    for d in op["deps"]:
                        dop = ops[d]
                        if dop["chan"] is not None:
                            key = ("c", dop["chan"])
                            if str(dop["chan"]).startswith("pcs"):
                                val = dop["cval"]
                            else:
                                val = 16 * bisect.bisect_left(chan_ops[dop["chan"]], op["idx"])
                        else:
                            if dop["eng"] == e and e == "pe":
                                continue
                            if dop["fn"] is None:
                                print("DEP ON NONE OP", op["idx"], d)
                            key = ("e", dop["eng"]); val = dop["seq"]
                        if sem.get(key, 0) < val:
                            ok = False
                            blk = (key, val, sem.get(key, 0), d)
                            break
                    if not ok:
                        op["blk"] = blk
                        break
                    if op["fn"] is not None:
                        if op["chan"] is not None:
                            sem[("c", op["chan"])] = sem.get(("c", op["chan"]), 0) + 16
                        elif op["sig"]:
                            sem[("e", e)] = sem.get(("e", e), 0) + 1
                    pos[e] += 1
                    progress = True
        stuck = {e: (pos[e], len(per[e])) for e in self.ENGS if pos[e] < len(per[e])}
        if stuck:
            print("DEADLOCK", stuck)
            for e in stuck:
                op = per[e][pos[e]]
                print(e, "op idx", op["idx"], "blocked on", op.get("blk"), "tag", op.get("tag"))
        else:
            print("simulate: no deadlock;", {e: len(per[e]) for e in self.ENGS})
        return not stuck

    def emit(self, nc, es):
        import os
        if os.environ.get("KSIM"):
            self.simulate()
        ops = self.ops
        for op in ops:
            for d in op["deps"]:
                if ops[d]["chan"] is None:
                    ops[d]["sig"] = True
        cnt = {e: 0 for e in self.ENGS}
        for op in ops:
            if op["chan"] is None and op["sig"]:
                cnt[op["eng"]] += 1
                op["seq"] = cnt[op["eng"]]
        sems = {}
        for e in self.ENGS:
            sems[("e", e)] = es.enter_context(nc.semaphore("sem_" + e))
        for c in self.chan_cnt:
            sems[("c", c)] = es.enter_context(nc.semaphore("ch_" + str(c)))
        block = es.enter_context(nc.Block())
        for i, op in enumerate(ops):
            op["idx"] = i
        per = {e: [op for op in ops if op["eng"] == e] for e in self.ENGS}
        import bisect
        chan_ops = {}
        for i, op in enumerate(ops):
            if op["chan"] is not None:
                chan_ops.setdefault(op["chan"], []).append(i)

        def run(eng_name, e):
            waited = {}
            for op in per[eng_name]:
                need = {}
                for d in op["deps"]:
                    dop = ops[d]
                    if dop["chan"] is not None:
                        key = ("c", dop["chan"])
                        if str(dop["chan"]).startswith("pcs"):
                            val = dop["cval"]
                        else:
                            lst = chan_ops[dop["chan"]]
                            k = bisect.bisect_left(lst, op["idx"])
                            val = 16 * k
                    else:
                        if dop["eng"] == eng_name and eng_name == "pe":
                            continue
                        key = ("e", dop["eng"]); val = dop["seq"]
                    if need.get(key, 0) < val:
                        need[key] = val
                for key, val in need.items():
                    if waited.get(key, 0) >= val:
                        continue
                    e.wait_ge(sems[key], val)
                    waited[key] = val
                if op["fn"] is None:
                    continue
                ins = op["fn"](e)
                if op["chan"] is not None:
                    ins.then_inc(sems[("c", op["chan"])], 16)
                elif op["sig"]:
                    ins.then_inc(sems[("e", eng_name)], 1)

        @block.tensor
        def _(e):
            run("pe", e)

        @block.scalar
        def _(e):
            run("act", e)

        @block.vector
        def _(e):
            run("dve", e)

        @block.gpsimd
        def _(e):
            run("pool", e)

        @block.sync
        def _(e):
            run("sp", e)


def build_program(NSEQ, T, do_rwkv=True, stage=99):
    NT = T // 128
    NQ = T // 512
    assert T % 512 == 0
    nc = bass.Bass("TRN2", target_bir_lowering=False)
    P = Prog()
    es = ExitStack()

    def din(name, shape):
        return nc.dram_tensor(name, list(shape), F32, kind="ExternalInput").ap()

    xs = din("xs", [NSEQ, T, D])
    cs = din("cs", [NSEQ, D])
    w_ada = din("w_ada", [D, NMOD * D])
    b_ada = din("b_ada", [NMOD * D])
    norm1_g = din("norm1_g", [D])
    w_in = din("w_in", [D, IN_COLS])
    mu_shift = din("mu_shift", [RWC])
    w0_decay = din("w0_decay", [2, RW])
    w2_decay = din("w2_decay", [2, 64, RW])
    a0 = din("a0", [RW])
    a2 = din("a2", [64, RW])
    g2_gate = din("g2_gate", [160, RW])
    k_k = din("k_k", [RW])
    k_a = din("k_a", [RW])
    r_k = din("r_k", [RW])
    lnx_g = din("lnx_g", [RW])
    lnx_b = din("lnx_b", [RW])
    conv_w = din("conv_w", [3, RW])
    w_out = din("w_out", [D, D])
    norm2_g = din("norm2_g", [D])
    w_ffn_gate = din("w_ffn_gate", [D, DFF])
    w_ffn_up = din("w_ffn_up", [D, DFF])
    w_ffn_down = din("w_ffn_down", [DFF, D])
    norm_f_g = din("norm_f_g", [D])
    ys = nc.dram_tensor("ys", [NSEQ, T, D], F32, kind="ExternalOutput").ap()

    wd_b = nc.dram_tensor("wd_b", [4, 4, 128, 11 * 512], BF16).ap()
    win_b = nc.dram_tensor("win_b", [NG, 128, KC * 128], BF16).ap()
    wg_b = nc.dram_tensor("wg_b", [FC, 128, KC * 128], BF16).ap()
    wu_b = nc.dram_tensor("wu_b", [FC, 128, KC * 128], BF16).ap()
    wout_b = nc.dram_tensor("wout_b", [4, 128, KC * 512], BF16).ap()
    mixT_d = nc.dram_tensor("mixT_d", [NSEQ, 16, 128, T], BF16).ap()

    def sb(name, shape, dt):
        return es.enter_context(nc.sbuf_tensor(name, list(shape), dt))

    id_f = sb("id_f", [128, 128], F32)
    id_b = sb("id_b", [128, 128], BF16)
    ones_f = sb("ones_f", [128, 128], F32)
    blk_f = sb("blk_f", [128, 128], F32)
    MA = [sb("MA%d" % d, [128, 512], F32) for d in range(2)]
    MX = [sb("MX%d" % d, [128, 256], F32) for d in range(2)]
    TRI = [sb("TRI%d" % d, [128, 256], F32) for d in range(2)]
    n1g = sb("n1g", [128, 16], F32)
    n2g = sb("n2g", [128, 16], F32)
    bada = sb("bada", [128, 96], F32)
    modT = sb("modT", [128, NSEQ, 96], F32)
    s1c = sb("s1c", [128, NSEQ, 16], F32)
    s2c = sb("s2c", [128, NSEQ, 16], F32)
    mu_c = sb("mu_c", [128, 27], F32)
    om_c = sb("om_c", [128, 27], F32)
    mue_c = sb("mue_c", [128, 27], F32)
    muo_c = sb("muo_c", [128, 27], F32)
    muag = sb("muag", [128, 4], F32)
    par8 = {nm: sb("p_" + nm, [128, 8], F32) for nm in ("a0", "k_k", "k_a", "r_k", "lnx_g", "lnx_b", "omka", "cw0", "cw1", "cw2")}
    evn = sb("evn", [128, 1], F32)
    mug0 = sb("mug0", [128, 4], F32)
    odd = sb("odd", [128, 1], F32)
    stat = sb("stat", [128, 8], F32)
    epsc = sb("epsc", [128, 2], F32)

    ARENA = 190 * 1024 // 4
    arena = sb("arena", [128, ARENA], F32)
    apos = [0]

    def carve(nbytes, dt, shape):
        n32 = (nbytes + 3) // 4
        n32 = (n32 + 7) // 8 * 8
        a = arena[:, apos[0]:apos[0] + n32]
        apos[0] += n32
        assert apos[0] <= ARENA, (apos[0], ARENA)
        if dt == BF16:
            a = a.bitcast(BF16)
        v = a
        if len(shape) == 3:
            v = a[:, 0:shape[1] * shape[2]].rearrange("p (a b) -> p a b", b=shape[2])
        elif len(shape) == 2:
            v = a[:, 0:shape[1]]
        return v

    PS = [es.enter_context(nc.psum_tensor("ps%d" % i, [128, 512], F32)) for i in range(8)]

    def psb(i):
        return ("ps", i)

    def pool_op(fn, r=(), w=()):
        return P.add("pool", fn, r, w)

    def mk_mask(ap, kind, tok="const"):
        mt = ("mask", id(ap), kind, len(P.ops))
        pool_op(lambda e: e.memset(ap, 1.0), w=(mt,))

        def f(e):
            if kind == "SL":
                return e.affine_select(out=ap, in_=ap, pattern=[[-1, 128]], compare_op=ALU.is_gt, fill=0.0, base=0, channel_multiplier=1)
            if kind == "SU":
                return e.affine_select(out=ap, in_=ap, pattern=[[1, 128]], compare_op=ALU.is_gt, fill=0.0, base=0, channel_multiplier=-1)
            if kind == "IU":
                return e.affine_select(out=ap, in_=ap, pattern=[[1, 128]], compare_op=ALU.is_ge, fill=0.0, base=0, channel_multiplier=-1)
            if kind == "IL":
                return e.affine_select(out=ap, in_=ap, pattern=[[-1, 128]], compare_op=ALU.is_ge, fill=0.0, base=0, channel_multiplier=1)
        pool_op(f, r=(mt,), w=(mt, tok))

    pool_op(lambda e: e.memset(id_f[:], 0.0), w=("id_f0",))
    pool_op(lambda e: e.affine_select(out=id_f[:], in_=id_f[:], pattern=[[-1, 128]], compare_op=ALU.not_equal, fill=1.0, base=0, channel_multiplier=1),
            r=("id_f0",), w=("id_f0", "const"))

    def f_ident(e):
        e.memset(ones_f[:], 1.0)
        e.memset(epsc[:, 0:1], RMS_EPS)
        return e.memset(epsc[:, 1:2], LNX_EPS)
    pool_op(f_ident, w=("const_b",))
    pool_op(lambda e: e.memset(blk_f[:], 0.0), w=("blk0",))

    def f_blk(e):
        e.memset(blk_f[0:64, 0:64], 1.0)
        return e.memset(blk_f[64:128, 64:128], 1.0)
    pool_op(f_blk, r=("blk0",), w=("blk0", "const_c"))
    pool_op(lambda e: e.tensor_copy(out=id_b[:], in_=id_f[:]), r=("const",), w=("const2",))
    for d, (ks, ki, kx) in enumerate((("SU", "IU", "SL"), ("SL", "IL", "SU"))):
        mk_mask(MA[d][:, 0:128], ks); mk_mask(MA[d][:, 128:256], ks)
        mk_mask(MA[d][:, 256:384], ki); mk_mask(MA[d][:, 384:512], ki)
        mk_mask(MX[d][:, 0:128], kx); mk_mask(MX[d][:, 128:256], kx)
    mk_mask(TRI[0][:, 0:128], "IU", tok=("tri", 0)); mk_mask(TRI[0][:, 128:256], "SU", tok=("tri", 0))
    mk_mask(TRI[1][:, 0:128], "IL", tok=("tri", 1)); mk_mask(TRI[1][:, 128:256], "SL", tok=("tri", 1))
    pool_op(lambda e: e.tensor_scalar(out=TRI[0][0:64, :], in0=TRI[0][0:64, :], scalar1=-1.0, scalar2=None, op0=ALU.add), r=(("tri", 0),), w=(("tri", 0),))
    pool_op(lambda e: e.tensor_scalar(out=TRI[1][64:128, :], in0=TRI[1][64:128, :], scalar1=-1.0, scalar2=None, op0=ALU.add), r=(("tri", 1),), w=(("tri", 1),))
    pool_op(lambda e: e.tensor_scalar(out=TRI[0][:], in0=TRI[0][:], scalar1=C0, scalar2=None, op0=ALU.mult), r=(("tri", 0),), w=(("tri", 0),))
    pool_op(lambda e: e.tensor_scalar(out=TRI[1][:], in0=TRI[1][:], scalar1=C0, scalar2=None, op0=ALU.mult), r=(("tri", 1),), w=(("tri", 1), "const3"))

    def f_par(e):
        idv = id_f[:].rearrange("p (a b) -> p a b", b=2)
        e.tensor_reduce(out=evn[:], in_=idv[:, :, 0], axis=AX.X, op=ALU.add)
        return e.tensor_reduce(out=odd[:], in_=idv[:, :, 1], axis=AX.X, op=ALU.add)
    P.add("dve", f_par, r=("const",), w=("const4",))

    def early(k):
        if stage == k:
            P.finish()
            P.emit(nc, es)
            es.close()
            return True
        return False
    if early(0):
        return nc
    pcn = {"i": 0}
    PCD = 4

    def precast(dst, src, tok, chan):
        k = pcn["i"] % PCD
        pcn["i"] += 1
        P.add("pool", lambda e: e.dma_start(out=dst, in_=src), r=(), w=(tok, ("pcring", k)), chan="pcs%d" % k)

    for g, parts in enumerate(GROUPS):
        dstg = win_b[g].rearrange("p (kc m) -> p kc m", m=128)
        for (c0, wd, d0) in parts:
            src = w_in[:, c0:c0 + wd].rearrange("(kc p) m -> p kc m", p=128)
            precast(dstg[:, :, d0:d0 + wd], src, ("scr", "win", g, d0), "pc_win")

    if early(1):
        return nc
    stg = sb("stg", [128, 128], F32)

    def load_cols(vec, n, dst_cols, tag):
        rows = n // 128
        rem = n - rows * 128
        nr = rows + (1 if rem else 0)
        P.add("pool", lambda e: e.memset(stg[:], 0.0), w=("stg",))
        if rows:
            P.add("sp", lambda e: e.dma_start(out=stg[0:rows, :], in_=vec[0:rows * 128].rearrange("(r c) -> r c", c=128)),
                  w=("stg",), chan="misc")
        if rem:
            P.add("sp", lambda e: e.dma_start(out=stg[rows:rows + 1, 0:rem], in_=vec[rows * 128:n].rearrange("(r c) -> r c", r=1)),
                  w=("stg",), chan="misc")
        P.add("pe", lambda e: e.transpose(PS[0][:, 0:128], stg[:], id_f[:]), r=("stg", "const"), w=(psb(0),))
        P.add("dve", lambda e: e.tensor_copy(out=dst_cols, in_=PS[0][:, 0:nr]), r=(psb(0),), w=("par", tag))

    load_cols(norm1_g, D, n1g[:], "n1g")
    load_cols(norm2_g, D, n2g[:], "n2g")
    load_cols(b_ada, NMOD * D, bada[:], "bada")
    load_cols(mu_shift, RWC, mu_c[:], "mu")
    for nm, v in (("a0", a0), ("k_k", k_k), ("k_a", k_a), ("r_k", r_k), ("lnx_g", lnx_g), ("lnx_b", lnx_b)):
        load_cols(v, RW, par8[nm][:], nm)
    for j in range(3):
        load_cols(conv_w[j], RW, par8["cw%d" % j][:], "cw%d" % j)
    P.add("pool", lambda e: e.memset(muag[:], 0.0), w=("muag",))
    P.add("sp", lambda e: e.dma_start(out=muag[0:64, 0:1], in_=mu_shift[3200:3264].rearrange("(p o) -> p o", o=1), allow_slow_non_contiguous=True),
          w=("muag",), chan="c_muag")
    P.add("sp", lambda e: e.dma_start(out=muag[64:96, 0:1], in_=mu_shift[3392:3424].rearrange("(p o) -> p o", o=1), allow_slow_non_contiguous=True),
          w=("muag",), chan="c_muag")

    P.add("sp", lambda e: e.dma_start(out=mug0[:, 0:1], in_=mu_shift[3264:3392].rearrange("(p o) -> p o", o=1), allow_slow_non_contiguous=True),
          w=("mug0",), chan="c_mug0")

    def f_mu(e):
        e.tensor_scalar(out=mug0[:, 1:2], in0=mug0[:, 0:1], scalar1=-1.0, scalar2=1.0, op0=ALU.mult, op1=ALU.add)
        e.tensor_scalar(out=mug0[:, 2:3], in0=mug0[:, 0:1], scalar1=evn[:, 0:1], scalar2=None, op0=ALU.mult)
        e.tensor_scalar(out=mug0[:, 3:4], in0=mug0[:, 0:1], scalar1=odd[:, 0:1], scalar2=None, op0=ALU.mult)
        e.tensor_scalar(out=om_c[:], in0=mu_c[:], scalar1=-1.0, scalar2=1.0, op0=ALU.mult, op1=ALU.add)
        e.tensor_scalar(out=mue_c[:], in0=mu_c[:], scalar1=evn[:, 0:1], scalar2=None, op0=ALU.mult)
        e.tensor_scalar(out=muo_c[:], in0=mu_c[:], scalar1=odd[:, 0:1], scalar2=None, op0=ALU.mult)
        e.tensor_scalar(out=muag[:, 1:2], in0=muag[:, 0:1], scalar1=-1.0, scalar2=1.0, op0=ALU.mult, op1=ALU.add)
        e.tensor_scalar(out=muag[:, 2:3], in0=muag[:, 0:1], scalar1=evn[:, 0:1], scalar2=None, op0=ALU.mult)
        e.tensor_scalar(out=muag[:, 3:4], in0=muag[:, 0:1], scalar1=odd[:, 0:1], scalar2=None, op0=ALU.mult)
        return e.tensor_scalar(out=par8["omka"][:], in0=par8["k_a"][:], scalar1=-1.0, scalar2=1.0, op0=ALU.mult, op1=ALU.add)
    P.add("dve", f_mu, r=(("par", "mu"), ("par", "k_a"), "muag", "mug0", "const3", "const4"), w=("mud",))

    if early(2):
        return nc
    apos[0] = 0
    cT = carve(KC * NSEQ * 2, BF16, [128, KC, NSEQ])
    crow = carve(D * 4, F32, [128, D])
    WA_G = 512
    wa = [carve(KC * WA_G * 2, BF16, [128, KC, WA_G]) for _ in range(2)]
    P.add("sp", lambda e: e.dma_start(out=crow[0:NSEQ, :], in_=cs), w=("crow",), chan="c_crow")
    for kc in range(KC):
        P.add("pe", lambda e, kc=kc: e.transpose(PS[1][:, kc * NSEQ:(kc + 1) * NSEQ], crow[0:NSEQ, kc * 128:(kc + 1) * 128], id_f[0:NSEQ, 0:NSEQ]),
              r=("crow", "const"), w=(psb(1),))
    P.add("act", lambda e: e.activation(out=cT[:].rearrange("p a b -> p (a b)"), in_=PS[1][:, 0:KC * NSEQ], func=AF.Silu), r=(psb(1),), w=("cT",))
    NWA = NMOD * D // WA_G
    for gi in range(NWA):
        slot = gi % 2
        P.add("pool", lambda e, gi=gi, slot=slot: e.dma_start(
            out=wa[slot][:], in_=w_ada[:, gi * WA_G:(gi + 1) * WA_G].rearrange("(kc p) m -> p kc m", p=128)),
            w=(("wa", slot),), chan="wa%d" % slot)
        bank = 2 + (gi % 2)
        for jj in range(WA_G // 128):
            for kc in range(KC):
                P.add("pe", lambda e, jj=jj, kc=kc, slot=slot, bank=bank: e.matmul(
                    PS[bank][:, jj * NSEQ:(jj + 1) * NSEQ], wa[slot][:, kc, jj * 128:(jj + 1) * 128], cT[:, kc, :],
                    start=(kc == 0), stop=(kc == KC - 1)), r=(("wa", slot), "cT"), w=(psb(bank),))
        nj = WA_G // 128
        for s in range(NSEQ):
            P.add("dve", lambda e, gi=gi, s=s, bank=bank, nj=nj: e.tensor_tensor(
                out=modT[:, s, gi * nj:(gi + 1) * nj],
                in0=PS[bank][:, 0:nj * NSEQ].rearrange("p (j s) -> p j s", s=NSEQ)[:, :, s],
                in1=bada[:, gi * nj:(gi + 1) * nj], op=ALU.add), r=(psb(bank), ("par", "bada")), w=("modT",))

    def f_s12(e):
        for s in range(NSEQ):
            e.scalar_tensor_tensor(out=s1c[:, s, :], in0=modT[:, s, 16:32], scalar=1.0, in1=n1g[:], op0=ALU.add, op1=ALU.mult)
            r_ = e.scalar_tensor_tensor(out=s2c[:, s, :], in0=modT[:, s, 64:80], scalar=1.0, in1=n2g[:], op0=ALU.add, op1=ALU.mult)
        return r_
    P.add("dve", f_s12, r=("modT", ("par", "n1g"), ("par", "n2g")), w=("s12",))

    if early(3):
        return nc
    for dg in range(4):
        dst = wout_b[dg].rearrange("p (kc n) -> p kc n", n=512)
        for half in range(2):
            src = w_out[half * 1024:(half + 1) * 1024, dg * 512:(dg + 1) * 512].rearrange("(kc p) n -> p kc n", p=128)
            precast(dst[:, half * 8:(half + 1) * 8, :], src, ("scr", "wout", dg, half), "pc_wout")
    if early(31):
        return nc
    import os as _os
    for fb in range(0 if _os.environ.get("SKIPGU") is None else FC, FC):
        for (wsrc, wdst, nm) in ((w_ffn_gate, wg_b, "wg"), (w_ffn_up, wu_b, "wu")):
            src = wsrc[:, fb * 128:(fb + 1) * 128].rearrange("(kc p) m -> p kc m", p=128)
            precast(wdst[fb].rearrange("p (kc m) -> p kc m", m=128), src, ("scr", nm, fb), "pc_" + nm)
    if early(32):
        return nc
    for dg in range(4):
        for fq in range(4):
            src = w_ffn_down[fq * 1408:(fq + 1) * 1408, dg * 512:(dg + 1) * 512].rearrange("(fc p) n -> p fc n", p=128)
            precast(wd_b[dg, fq].rearrange("p (fc n) -> p fc n", n=512), src, ("scr", "wd", dg, fq), "pc_wd")

    P.barrier()
    if early(4):
        return nc

    rr = {"i": 0}

    def evac_eng():
        import os
        if os.environ.get("EVAC"):
            return os.environ["EVAC"]
        return "dve"

    def rstd_from_ss(ss_col, out_col, scale, eps_idx):
        P.add("act", lambda e: e.activation(out=out_col, in_=ss_col, func=AF.Sqrt, bias=epsc[:, eps_idx:eps_idx + 1], scale=scale),
              r=("stat",), w=("stat",))
        P.add("dve", lambda e: e.reciprocal(out=out_col, in_=out_col), r=("stat",), w=("stat",))

    def norm_transpose(src_row, rtok, s_cols, sh_cols, dstT, dtoks, ttok, w_tok_fn, bank0=4):
        xn, xn_tok, junk = ttok
        P.add("act", lambda e: e.activation(out=junk, in_=src_row, func=AF.Square, accum_out=stat[:, 0:1]), r=(rtok,), w=("stat", "junk"))
        rstd_from_ss(stat[:, 0:1], stat[:, 1:2], 1.0 / D, 0)
        if stage == 81 and bank0 == 6:
            return
        P.add("dve", lambda e: e.tensor_scalar(out=xn, in0=src_row, scalar1=stat[:, 1:2], scalar2=None, op0=ALU.mult), r=(rtok, "stat"), w=(xn_tok,))
        if stage == 82 and bank0 == 6:
            return
        for kq in range(4):
            bank = bank0 + (kq % 2)
            for k4 in range(4):
                kc = kq * 4 + k4
                P.add("pe", lambda e, kc=kc, k4=k4, bank=bank: e.transpose(PS[bank][:, k4 * 128:(k4 + 1) * 128], xn[:, kc * 128:(kc + 1) * 128], id_f[:]),
                      r=(xn_tok,), w=(psb(bank),))
            if stage == 83 and bank0 == 6:
                continue
            for k4 in range(4):
                kc = kq * 4 + k4
                eng = evac_eng()
                if eng == "dve":
                    P.add("dve", lambda e, kc=kc, k4=k4, bank=bank: e.tensor_scalar(
                        out=dstT[:, kc, dtoks], in0=PS[bank][:, k4 * 128:(k4 + 1) * 128], scalar1=s_cols[:, kc:kc + 1], scalar2=sh_cols[:, kc:kc + 1],
                        op0=ALU.mult, op1=ALU.add), r=(psb(bank),), w=(w_tok_fn(kc),))
                else:
                    P.add("act", lambda e, kc=kc, k4=k4, bank=bank: e.activation(
                        out=dstT[:, kc, dtoks], in_=PS[bank][:, k4 * 128:(k4 + 1) * 128], func=AF.Identity,
                        bias=sh_cols[:, kc:kc + 1], scale=s_cols[:, kc:kc + 1]), r=(psb(bank),), w=(w_tok_fn(kc),))

    def chk(n):
        if stage == n:
            raise StopIteration

    import os as _os2
    for _i in range(int(_os2.environ.get("PEPAD", "0"))):
        P.add("pe", lambda e: e.matmul(PS[7][:, 0:128], id_b[:], id_b[:], start=True, stop=True), w=(psb(7),))

    def do_seq(s):
        apos[0] = 0
        hn1T = carve(KC * T * 2, BF16, [128, KC, T])
        a_mark = apos[0]
        xrow = [carve(D * 4, F32, [128, D]) for _ in range(2)]
        xn = carve(D * 4, F32, [128, D])
        junk = carve(D * 2, BF16, [128, D])
        for ti in range(NT):
            slot = ti % 2
            P.add("sp", lambda e, ti=ti, slot=slot: e.dma_start(out=xrow[slot][:], in_=xs[s, ti * 128:(ti + 1) * 128, :]),
                  w=(("xrow", slot),), chan="xr%d" % slot)
            norm_transpose(xrow[slot][:], ("xrow", slot), s1c[:, s, :], modT[:, s, 0:16], hn1T, slice(ti * 128, (ti + 1) * 128),
                           (xn[:], "xn", junk[:]), lambda kc, ti=ti: ("hn1T", ti // 4))
        P.barrier()
        if stage == 5:
            raise StopIteration

        apos[0] = a_mark
        rawp = carve((T + 2) * 4, F32, [128, T + 2])
        tA = carve(T * 4, F32, [128, T])
        tB = carve(T * 4, F32, [128, T])
        yacc = carve(T * 4, F32, [128, T])
        r_bf = carve(T * 2, BF16, [128, T])
        k_bf = carve(T * 2, BF16, [128, T])
        v_bf = carve(T * 2, BF16, [128, T])
        ka_bf = carve(T * 2, BF16, [128, T])
        nb_bf = carve(T * 2, BF16, [128, T])
        mixo = carve(T * 2, BF16, [128, T])
        lo_w = carve(T * 2, BF16, [128, T])
        lo_a = carve(T * 2, BF16, [128, T])
        lo_g = carve(T * 2, BF16, [128, T])
        w2b = carve(RW * 2, BF16, [128, RW])
        a2g = carve(RW * 2, BF16, [128, RW])
        g2a = carve(RW * 2, BF16, [128, RW])
        w0b = carve(256 * 4, F32, [128, 256])
        LW = [tA[:, 0:NT * 128].rearrange("p (a b) -> p a b", b=128), rawp[:, 0:NT * 128].rearrange("p (a b) -> p a b", b=128)]
        LWT = ["tA", "rawp"]
        vpad = carve(NT * 256 * 2, BF16, [128, NT, 256])
        win_t = [carve(KC * 128 * 2, BF16, [128, KC, 128]) for _ in range(2)]
        UB = {}
        for d in range(2):
            UB[d] = dict(
                E1=[carve(256 * 4, F32, [128, 256]) for _ in range(2)], Em=carve(128 * 4, F32, [128, 128]),
                fm=carve(512 * 2, BF16, [128, 4, 128]),
                rh=carve(256 * 2, BF16, [128, 2, 128]),
                kh=carve(256 * 2, BF16, [128, 2, 128]),
                tk=carve(768 * 2, BF16, [128, 3, 256]),
                A=carve(1024 * 2, BF16, [128, 2, 512]),
                X0=carve(256 * 2, BF16, [128, 2, 128]),
                XZ=[carve(512 * 2, BF16, [128, 4, 128]) for _ in range(2)],
                Pm=[carve(256 * 2, BF16, [128, 2, 128]) for _ in range(2)],
                akv=carve(256 * 2, BF16, [128, 256]), wtT=carve(128 * 2, BF16, [128, 128]),
                upad=carve(256 * 2, BF16, [128, 256]),
            )
        HN = [carve(128 * 4, F32, [128, 128]) for _ in range(2)]
        HB = [carve(128 * 2, BF16, [128, 128]) for _ in range(2)]
        GC = [carve(4 * 4, F32, [128, 4]) for _ in range(2)]
        tq = [carve(512 * 4, F32, [128, 512]) for _ in range(2)]

        def zero_pads(e):
            e.memset(rawp[:, 0:1], 0.0)
            return e.memset(rawp[:, T + 1:T + 2], 0.0)

        def zero_padded(e):
            e.memset(vpad[:].rearrange("p a b -> p (a b)"), 0.0)
            for d in range(2):
                e.memset(UB[d]["tk"][:].rearrange("p a b -> p (a b)"), 0.0)
                e.memset(UB[d]["akv"][:], 0.0)
                e.memset(UB[d]["rh"][:].rearrange("p a b -> p (a b)"), 0.0)
                e.memset(UB[d]["kh"][:].rearrange("p a b -> p (a b)"), 0.0)
                r_ = e.memset(UB[d]["upad"][:], 0.0)
            return r_
        P.add("pool", zero_padded, w=("vpad",) + tuple(("ub", d, nm) for d in range(2) for nm in ("tk", "akv", "upad", "fmh")))

        P.add("pool", lambda e: e.dma_start(out=w2b[0:64, :], in_=w2_decay[0]), w=(("lw_w", 0),), chan="lora")
        P.add("pool", lambda e: e.dma_start(out=w2b[64:128, :], in_=w2_decay[1]), w=(("lw_w", 1),), chan="lora")
        P.add("pool", lambda e: e.dma_start(out=a2g[0:64, :], in_=a2), w=(("lw_w", 2),), chan="lora")
        P.add("pool", lambda e: e.dma_start(out=a2g[64:96, :], in_=g2_gate[128:160, :]), w=(("lw_w", 3),), chan="lora")
        P.add("pool", lambda e: e.dma_start(out=g2a[:], in_=g2_gate[0:128, :]), w=(("lw_w", 4),), chan="lora")
        LWW = tuple(("lw_w", i) for i in range(5))

        wslot = {"i": 0}

        def project(g, consume):
            slot = wslot["i"] % 2
            wslot["i"] += 1
            rt = tuple(("scr", "win", g, d0) for (_, _, d0) in GROUPS[g])
            P.add("sp", lambda e: e.dma_start(out=win_t[slot][:].rearrange("p a b -> p (a b)"), in_=win_b[g]),
                  r=rt, w=(("win_t", slot),), chan="win%d" % slot)
            for q in range(NQ):
                bank = q % 4
                for kc in range(KC):
                    P.add("pe", lambda e, kc=kc, q=q, bank=bank: e.matmul(PS[bank][:], win_t[slot][:, kc, :], hn1T[:, kc, q * 512:(q + 1) * 512],
                                                                         start=(kc == 0), stop=(kc == KC - 1)),
                          r=(("win_t", slot),), w=(psb(bank),))
                consume(q, PS[bank][:], psb(bank))

        def to_raw(q, ps, ptok):
            eng = evac_eng()
            dst = rawp[:, 1 + q * 512:1 + (q + 1) * 512]
            if eng == "dve":
                P.add("dve", lambda e: e.tensor_copy(out=dst, in_=ps), r=(ptok,), w=("rawp",))
            else:
                P.add("act", lambda e: e.activation(out=dst, in_=ps, func=AF.Copy), r=(ptok,), w=("rawp",))

        def shift(dst, dst_tok, om, mue, muo, np_=128, p0=0, post=None):
            ps_ = slice(p0, p0 + np_)
            tmp = tA[ps_, :]
            P.add("pool", zero_pads, r=("rawp",), w=("rawp",))
            P.add("act", lambda e: e.activation(out=tmp, in_=rawp[ps_, 1:T + 1], func=AF.Identity, scale=om[ps_, :]), r=("rawp",), w=("tA",))
            P.add("dve", lambda e: e.scalar_tensor_tensor(out=tmp, in0=rawp[ps_, 0:T], scalar=mue[ps_, :], in1=tmp, op0=ALU.mult, op1=ALU.add),
                  r=("rawp", "tA"), w=("tA",))
            if post is None:
                P.add("dve", lambda e: e.scalar_tensor_tensor(out=dst, in0=rawp[ps_, 2:T + 2], scalar=muo[ps_, :], in1=tmp, op0=ALU.mult, op1=ALU.add),
                      r=("rawp", "tA"), w=(dst_tok,))
            else:
                P.add("dve", lambda e: e.scalar_tensor_tensor(out=tmp, in0=rawp[ps_, 2:T + 2], scalar=muo[ps_, :], in1=tmp, op0=ALU.mult, op1=ALU.add),
                      r=("rawp", "tA"), w=("tA",))
                P.add("act", lambda e: e.activation(out=dst, in_=tmp, func=post), r=("tA",), w=(dst_tok,))

        def do_hp(hp):
            hc = slice(hp * 128, (hp + 1) * 128)
            project(hp, to_raw)
            shift(r_bf[:], "r_bf", om_c[:, hp:hp + 1], mue_c[:, hp:hp + 1], muo_c[:, hp:hp + 1])
            project(16 + hp, to_raw)
            shift(v_bf[:], "v_bf", om_c[:, 16 + hp:17 + hp], mue_c[:, 16 + hp:17 + hp], muo_c[:, 16 + hp:17 + hp])
            project(8 + hp, to_raw)
            shift(tB[:], "tB", om_c[:, 8 + hp:9 + hp], mue_c[:, 8 + hp:9 + hp], muo_c[:, 8 + hp:9 + hp])
            chk(21)
            for q in range(NQ):
                bank = 4 + q % 2
                P.add("pe", lambda e, q=q, bank=bank: e.matmul(PS[bank][:], a2g[0:64, hc], lo_a[0:64, q * 512:(q + 1) * 512], start=True, stop=True),
                      r=LWW + ("lo_a",), w=(psb(bank),))
                P.add("act", lambda e, q=q, bank=bank: e.activation(out=yacc[:, q * 512:(q + 1) * 512], in_=PS[bank][:], func=AF.Sigmoid,
                                                                      bias=par8["a0"][:, hp:hp + 1]), r=(psb(bank),), w=("yacc",))
            P.add("dve", lambda e: e.tensor_scalar(out=tA[:], in0=tB[:], scalar1=par8["k_k"][:, hp:hp + 1], scalar2=None, op0=ALU.mult),
                  r=("tB",), w=("tA",))
            for q in range(NQ):
                qs = slice(q * 512, (q + 1) * 512)
                bank = 4 + q % 2
                tqq = tq[q % 2]
                tqt = ("tq", q % 2)
                P.add("pool", lambda e, qs=qs, tqq=tqq: e.tensor_tensor(out=tqq[:], in0=tA[:, qs], in1=tA[:, qs], op=ALU.mult), r=("tA",), w=(tqt,))
                P.add("pe", lambda e, tqq=tqq, bank=bank: e.matmul(PS[bank][:], blk_f[:], tqq[:], start=True, stop=True), r=(tqt,), w=(psb(bank),))
                P.add("act", lambda e, tqq=tqq, bank=bank: e.activation(out=tqq[:], in_=PS[bank][:], func=AF.Sqrt), r=(psb(bank),), w=(tqt,))
                P.add("dve", lambda e, tqq=tqq: e.tensor_scalar(out=tqq[:], in0=tqq[:], scalar1=1e-12, scalar2=None, op0=ALU.max), r=(tqt,), w=(tqt,))
                P.add("dve", lambda e, tqq=tqq: e.reciprocal(out=tqq[:], in_=tqq[:]), r=(tqt,), w=(tqt,))
                P.add("dve", lambda e, qs=qs, tqq=tqq: e.tensor_tensor(out=tA[:, qs], in0=tA[:, qs], in1=tqq[:], op=ALU.mult), r=("tA", tqt), w=("tA",))
            P.add("act", lambda e: e.activation(out=ka_bf[:], in_=tA[:], func=AF.Copy), r=("tA",), w=("ka_bf",))
            P.add("dve", lambda e: e.scalar_tensor_tensor(out=nb_bf[:], in0=tA[:], scalar=-1.0, in1=yacc[:], op0=ALU.mult, op1=ALU.mult),
                  r=("tA", "yacc"), w=("nb_bf",))
            P.add("dve", lambda e: e.tensor_scalar(out=yacc[:], in0=yacc[:], scalar1=par8["k_a"][:, hp:hp + 1], scalar2=par8["omka"][:, hp:hp + 1],
                                                   op0=ALU.mult, op1=ALU.add), r=("yacc",), w=("yacc",))
            P.add("pool", lambda e: e.tensor_tensor(out=k_bf[:], in0=tB[:], in1=yacc[:], op=ALU.mult), r=("tB", "yacc"), w=("k_bf",))
            chk(22)
            P.add("dve", lambda e: e.scalar_tensor_tensor(out=tA[:], in0=r_bf[:], scalar=par8["r_k"][:, hp:hp + 1], in1=k_bf[:], op0=ALU.mult, op1=ALU.mult),
                  r=("r_bf", "k_bf", "tA"), w=("tA",))
            for q in range(NQ):
                qs = slice(q * 512, (q + 1) * 512)
                bank = 4 + q % 2
                P.add("pe", lambda e, qs=qs, bank=bank: e.matmul(PS[bank][:], blk_f[:], tA[:, qs], start=True, stop=True), r=("tA",), w=(psb(bank),))
                P.add("dve", lambda e, qs=qs, bank=bank: e.tensor_tensor(out=tB[:, qs], in0=PS[bank][:], in1=v_bf[:, qs], op=ALU.mult),
                      r=(psb(bank), "v_bf", "tB"), w=("tB",))
            chk(23)
            for ti in range(NT):
                bank = 4 + ti % 2
                pst = PS[bank][:].bitcast(BF16)
                P.add("pe", lambda e, ti=ti, pst=pst: e.transpose(pst[:, 0:128], v_bf[:, ti * 128:(ti + 1) * 128], id_b[:]), r=("v_bf",), w=(psb(bank),))
                eng = evac_eng()
                if eng == "dve":
                    P.add("dve", lambda e, ti=ti, pst=pst: e.tensor_copy(out=vpad[:, ti, 0:64], in_=pst[:, 0:64]), r=(psb(bank),), w=("vpad",))
                    P.add("dve", lambda e, ti=ti, pst=pst: e.tensor_copy(out=vpad[:, ti, 192:256], in_=pst[:, 64:128]), r=(psb(bank),), w=("vpad",))
                else:
                    P.add("act", lambda e, ti=ti, pst=pst: e.activation(out=vpad[:, ti, 0:64], in_=pst[:, 0:64], func=AF.Copy), r=(psb(bank),), w=("vpad",))
                    P.add("act", lambda e, ti=ti, pst=pst: e.activation(out=vpad[:, ti, 192:256], in_=pst[:, 64:128], func=AF.Copy), r=(psb(bank),), w=("vpad",))
            chk(24)
            for d in range(2):
                P.add("sp", lambda e, d=d: e.dma_start(out=w0b[:, d * 128:(d + 1) * 128], in_=w0_decay[d, hc].partition_broadcast(128)),
                      w=(("w0b", d),), chan="w0b")
            for d in range(2):
                dr = slice(d * 64, (d + 1) * 64)
                for t4 in range(NT // 4):
                    bank = 4 + t4 % 2
                    for j in range(4):
                        ti = t4 * 4 + j
                        P.add("pe", lambda e, ti=ti, j=j, bank=bank, dr=dr: e.matmul(PS[bank][:, j * 128:(j + 1) * 128], lo_w[dr, ti * 128:(ti + 1) * 128], w2b[dr, hc],
                                                                                   start=True, stop=True), r=("lo_w",) + LWW, w=(psb(bank),))
                    for j in range(4):
                        ti = t4 * 4 + j
                        P.add("dve", lambda e, ti=ti, j=j, bank=bank, d=d: e.tensor_tensor(out=LW[d][:, ti, :], in0=PS[bank][:, j * 128:(j + 1) * 128],
                                                                                        in1=w0b[:, d * 128:(d + 1) * 128], op=ALU.add),
                              r=(psb(bank), ("w0b", d)), w=(LWT[d],))
                P.add("act", lambda e, d=d: e.activation(out=LW[d][:].rearrange("p a b -> p (a b)"), in_=LW[d][:].rearrange("p a b -> p (a b)"), func=AF.Sigmoid),
                      r=(LWT[d],), w=(LWT[d],))

            chk(25)
            def zero_state(e):
                for d in range(2):
                    e.memset(HN[d][:], 0.0)
                    e.memset(GC[d][:], 1.0)
                    r_ = e.memset(HB[d][:], 0.0)
                return r_
            P.add("pool", zero_state, w=(("HN", 0), ("HN", 1), ("HB", 0), ("HB", 1), ("GC", 0), ("GC", 1)))

            def unit_stages(d, step):
                ti = step if d == 0 else NT - 1 - step
                U = UB[d]
                E1 = U["E1"][step % 2]
                E1p = U["E1"][(step - 1) % 2]
                e1t = ("ub", d, "E1", step % 2)
                e1pt = ("ub", d, "E1", (step - 1) % 2)
                kb = ("ub", d)
                b0, b1, b2, b3 = 4 * d, 4 * d + 1, 4 * d + 2, 4 * d + 3
                ts_ = slice(ti * 128, (ti + 1) * 128)
                endcol = 127 if d == 0 else 0
                startcol = 128 + (0 if d == 0 else 127)
                st = []
                fm = U["fm"]
                tkk = U["tk"]

                def s_prep():
                    P.add("pe", lambda e: e.matmul(PS[b3][:, 0:256], LW[d][:, ti, :], TRI[d][:], start=True, stop=True), r=(LWT[d],), w=(psb(b3),))
                    P.add("act", lambda e: e.activation(out=E1[:], in_=PS[b3][:, 0:256], func=AF.Exp), r=(psb(b3),), w=(e1t,))
                    P.add("act", lambda e: e.activation(out=U["Em"][:], in_=PS[b3][:, 0:128], func=AF.Exp, scale=-1.0), r=(psb(b3),), w=(kb + ("Em",),))
                    if step > 0:
                        P.add("dve", lambda e: e.reciprocal(out=GC[d][:, 1:2], in_=E1[:, startcol:startcol + 1]), r=(e1t,), w=(("GCt", d),))
                        P.add("dve", lambda e: e.tensor_tensor(out=GC[d][:, 0:1], in0=E1p[:, endcol:endcol + 1], in1=GC[d][:, 1:2], op=ALU.mult),
                              r=(e1pt, ("GCt", d)), w=(("GC", d),))
                    P.add("dve", lambda e: e.tensor_tensor(out=fm[:, 0, :], in0=r_bf[:, ts_], in1=E1[:, 0:128], op=ALU.mult), r=("r_bf", e1t), w=(kb + ("fm",),))
                    P.add("pool", lambda e: e.tensor_tensor(out=fm[:, 1, :], in0=k_bf[:, ts_], in1=U["Em"][:], op=ALU.mult), r=("k_bf", kb + ("Em",)), w=(kb + ("fm",),))
                    P.add("pool", lambda e: e.tensor_tensor(out=fm[:, 2, :], in0=nb_bf[:, ts_], in1=U["Em"][:], op=ALU.mult), r=("nb_bf", kb + ("Em",)), w=(kb + ("fm",),))
                    P.add("dve", lambda e: e.tensor_tensor(out=fm[:, 3, :], in0=ka_bf[:, ts_], in1=E1[:, 128:256], op=ALU.mult), r=("ka_bf", e1t), w=(kb + ("fm",),))
                    for h in range(2):
                        hs = slice(h * 64, (h + 1) * 64)
                        P.add("pool", lambda e, h=h, hs=hs: e.tensor_copy(out=U["rh"][hs, h, :], in_=fm[hs, 0, :]), r=(kb + ("fm",),), w=(kb + ("fmh",),))
                        P.add("pool", lambda e, h=h, hs=hs: e.tensor_copy(out=U["kh"][hs, h, :], in_=fm[hs, 3, :]), r=(kb + ("fm",),), w=(kb + ("fmh",),))
                    pst = PS[b3][:].bitcast(BF16)
                    for m in range(3):
                        P.add("pe", lambda e, m=m: e.transpose(pst[:, 512 + m * 128:512 + (m + 1) * 128], fm[:, 1 + m, :], id_b[:]), r=(kb + ("fm",),), w=(psb(b3),))
                    for m in range(3):
                        eng = evac_eng()
                        for h in range(2):
                            src = pst[:, 512 + m * 128 + h * 64:512 + m * 128 + (h + 1) * 64]
                            dst = tkk[:, m, h * 192:h * 192 + 64]
                            if eng == "dve":
                                P.add("dve", lambda e, src=src, dst=dst: e.tensor_copy(out=dst, in_=src), r=(psb(b3),), w=(kb + ("tk",),))
                            else:
                                P.add("act", lambda e, src=src, dst=dst: e.activation(out=dst, in_=src, func=AF.Copy), r=(psb(b3),), w=(kb + ("tk",),))
                st.append(s_prep)

                def s_A():
                    for h in range(2):
                        hs = slice(h * 64, (h + 1) * 64)
                        bank = b0 + h
                        for j, (li, rsrc) in enumerate(((2, "kh"), (1, "kh"), (1, "rh"), (2, "rh"))):
                            P.add("pe", lambda e, j=j, li=li, rsrc=rsrc, h=h, bank=bank: e.matmul(PS[bank][:, j * 128:(j + 1) * 128], fm[:, li, :], U[rsrc][:, h, :], start=True, stop=True),
                                  r=(kb + ("fm",), kb + ("fmh",)), w=(psb(bank),))
                        P.add("pe", lambda e, h=h: e.matmul(PS[b2][:, h * 128:(h + 1) * 128], U["kh"][:, h, :], fm[:, 2, :], start=True, stop=True),
                              r=(kb + ("fm",), kb + ("fmh",)), w=(psb(b2),))
                    if step == 0 and d == 0:
                        chk(201)
                    for h in range(2):
                        P.add("dve", lambda e, h=h: e.tensor_tensor(out=U["A"][:, h, :], in0=PS[b0 + h][:], in1=MA[d][:], op=ALU.mult), r=(psb(b0 + h),), w=(kb + ("A",),))
                    P.add("dve", lambda e: e.tensor_tensor(out=U["X0"][:].rearrange("p a b -> p (a b)"), in0=PS[b2][:, 0:256], in1=MX[d][:], op=ALU.mult),
                          r=(psb(b2),), w=(kb + ("X0",),))
                    if step == 0 and d == 0:
                        chk(202)
                    for h in range(2):
                        P.add("pool", lambda e, h=h: e.tensor_tensor(out=U["Pm"][0][:, h, :], in0=U["A"][:, h, 0:128], in1=id_b[:], op=ALU.add),
                              r=(kb + ("A",),), w=(kb + ("P", 0),))
                    if step == 0 and d == 0:
                        chk(203)
                    for h in range(2):
                        P.add("pe", lambda e, h=h: e.matmul(PS[b3][:, 256:384], U["A"][:, h, 128:256], vpad[:, ti, h * 128:(h + 1) * 128], start=(h == 0), stop=(h == 1)),
                              r=(kb + ("A",), "vpad"), w=(psb(b3),))
                    P.add("act", lambda e: e.activation(out=U["akv"][:, 0:64], in_=PS[b3][:, 256:320], func=AF.Copy), r=(psb(b3),), w=(kb + ("akv",),))
                    P.add("act", lambda e: e.activation(out=U["akv"][:, 192:256], in_=PS[b3][:, 320:384], func=AF.Copy), r=(psb(b3),), w=(kb + ("akv",),))
                st.append(s_A)

                def mk_level(lev):
                    def s_lev():
                        import os
                        if os.environ.get("LEVBAR") and lev == 1 and d == 0 and step == 0:
                            P.barrier()
                        cur = lev % 2
                        nxt = (lev + 1) % 2
                        XZn = U["XZ"][nxt]
                        for h in range(2):
                            if lev == 0:
                                Xh = U["X0"][:, h, :]
                                Zh = U["A"][:, h, 0:128]
                                rt = (kb + ("X0",), kb + ("A",))
                            else:
                                Xh = U["XZ"][cur][:, 2 * h, :]
                                Zh = U["XZ"][cur][:, 2 * h + 1, :]
                                rt = (kb + ("XZ", cur),)
                            P.add("pe", lambda e, h=h, Xh=Xh, Zh=Zh: e.matmul(PS[b2][:, (2 * h) * 128:(2 * h + 1) * 128], Zh, Xh, start=True, stop=True), r=rt, w=(psb(b2),))
                            if lev < 5:
                                P.add("pe", lambda e, h=h, Xh=Xh, Zh=Zh: e.matmul(PS[b2][:, (2 * h + 1) * 128:(2 * h + 2) * 128], Xh, Zh, start=True, stop=True), r=rt, w=(psb(b2),))
                        if lev == 1 and d == 0 and step == 0:
                            chk(60)
                        if lev < 5:
                            P.add("act", lambda e: e.activation(out=XZn[:].rearrange("p a b -> p (a b)"), in_=PS[b2][:], func=AF.Copy), r=(psb(b2),), w=(kb + ("XZ", nxt),))
                        if lev == 1 and d == 0 and step == 0:
                            chk(61)
                        else:
                            for h in range(2):
                                P.add("act", lambda e, h=h: e.activation(out=XZn[:, 2 * h, :], in_=PS[b2][:, (2 * h) * 128:(2 * h + 1) * 128], func=AF.Copy),
                                      r=(psb(b2),), w=(kb + ("XZ", nxt),))
                        Pc = U["Pm"][cur]
                        Pn = U["Pm"][nxt]
                        for h in range(2):
                            P.add("pe", lambda e, h=h: e.matmul(PS[b3][:, h * 128:(h + 1) * 128], XZn[:, 2 * h, :], Pc[:, h, :], start=True, stop=True),
                                  r=(kb + ("XZ", nxt), kb + ("P", cur)), w=(psb(b3),))
                        if lev == 1 and d == 0 and step == 0:
                            chk(62)
                        P.add("dve", lambda e: e.tensor_tensor(out=Pn[:].rearrange("p a b -> p (a b)"), in0=PS[b3][:, 0:256], in1=Pc[:].rearrange("p a b -> p (a b)"), op=ALU.add),
                              r=(psb(b3), kb + ("P", cur)), w=(kb + ("P", nxt),))
                    return s_lev
                for lev in range(6):
                    st.append(mk_level(lev))

                def s_tail():
                    TT = U["Pm"][0]
                    for h in range(2):
                        P.add("pe", lambda e, h=h: e.matmul(PS[b3][:, 256:384], TT[:, h, :], U["akv"][:, h * 128:(h + 1) * 128], start=(h == 0), stop=False),
                              r=(kb + ("P", 0), kb + ("akv",)), w=(psb(b3),))
                    for h in range(2):
                        P.add("pe", lambda e, h=h: e.matmul(PS[b1][:, 128:256], tkk[:, 2, h * 128:(h + 1) * 128], TT[:, h, :], start=(h == 0), stop=(h == 1)),
                              r=(kb + ("P", 0), kb + ("tk",)), w=(psb(b1),))
                    P.add("act", lambda e: e.activation(out=U["wtT"][:], in_=PS[b1][:, 128:256], func=AF.Copy), r=(psb(b1),), w=(kb + ("wtT",),))
                    P.add("act", lambda e: e.activation(out=HB[d][:], in_=HN[d][:], func=AF.Identity, scale=GC[d][:, 0:1]), r=(("HN", d), ("GC", d)), w=(("HB", d),))
                    P.add("pe", lambda e: e.matmul(PS[b3][:, 256:384], U["wtT"][:], HB[d][:], start=False, stop=True), r=(kb + ("wtT",), ("HB", d)), w=(psb(b3),))
                    P.add("dve", lambda e: e.tensor_copy(out=U["upad"][:, 0:64], in_=PS[b3][:, 256:320]), r=(psb(b3),), w=(kb + ("upad",),))
                    P.add("dve", lambda e: e.tensor_copy(out=U["upad"][:, 192:256], in_=PS[b3][:, 320:384]), r=(psb(b3),), w=(kb + ("upad",),))
                    yb = PS[b0][:, 0:128]
                    P.add("pe", lambda e: e.matmul(yb, HB[d][:], fm[:, 0, :], start=True, stop=False), r=(("HB", d), kb + ("fm",)), w=(psb(b0),))
                    for h in range(2):
                        P.add("pe", lambda e, h=h: e.matmul(yb, vpad[:, ti, h * 128:(h + 1) * 128], U["A"][:, h, 256:384], start=False, stop=False),
                              r=("vpad", kb + ("A",)), w=(psb(b0),))
                    for h in range(2):
                        P.add("pe", lambda e, h=h: e.matmul(yb, U["upad"][:, h * 128:(h + 1) * 128], U["A"][:, h, 384:512], start=False, stop=(h == 1)),
                              r=(kb + ("upad",), kb + ("A",)), w=(psb(b0),))
                    if step < NT // 2:
                        P.add("act", lambda e: e.activation(out=yacc[:, ts_], in_=yb, func=AF.Copy), r=(psb(b0),), w=("yacc",))
                    else:
                        P.add("dve", lambda e: e.tensor_tensor(out=yacc[:, ts_], in0=yb, in1=yacc[:, ts_], op=ALU.add), r=(psb(b0), "yacc"), w=("yacc",))
                    hb_ = PS[b1][:, 0:128]
                    for h in range(2):
                        P.add("pe", lambda e, h=h: e.matmul(hb_, tkk[:, 0, h * 128:(h + 1) * 128], vpad[:, ti, h * 128:(h + 1) * 128], start=(h == 0), stop=False),
                              r=(kb + ("tk",), "vpad"), w=(psb(b1),))
                    for h in range(2):
                        P.add("pe", lambda e, h=h: e.matmul(hb_, tkk[:, 1, h * 128:(h + 1) * 128], U["upad"][:, h * 128:(h + 1) * 128], start=False, stop=(h == 1)),
                              r=(kb + ("tk",), kb + ("upad",)), w=(psb(b1),))
                    P.add("dve", lambda e: e.scalar_tensor_tensor(out=HN[d][:], in0=HN[d][:], scalar=GC[d][:, 0:1], in1=hb_, op0=ALU.mult, op1=ALU.add),
                          r=(psb(b1), ("HN", d), ("GC", d)), w=(("HN", d),))
                st.append(s_tail)
                return st

            for step in range(NT):
                chains = [unit_stages(0, step), unit_stages(1, step)]
                for si in range(len(chains[0])):
                    for ch in chains:
                        ch[si]()
                    if step == 0:
                        chk(130 + si)
                chk(140 + step)

            chk(50)
            for q in range(NQ):
                qs = slice(q * 512, (q + 1) * 512)
                bank = q % 4
                tqq = tq[q % 2]
                tqt = ("tq", q % 2)
                P.add("pe", lambda e, qs=qs, bank=bank: e.matmul(PS[bank][:], blk_f[:], yacc[:, qs], start=True, stop=True), r=("yacc",), w=(psb(bank),))
                P.add("dve", lambda e, qs=qs, bank=bank: e.scalar_tensor_tensor(out=tA[:, qs], in0=PS[bank][:], scalar=-1.0 / 64, in1=yacc[:, qs], op0=ALU.mult, op1=ALU.add),
                      r=(psb(bank), "yacc"), w=("tA",))
                P.add("pool", lambda e, qs=qs, tqq=tqq: e.tensor_tensor(out=tqq[:], in0=tA[:, qs], in1=tA[:, qs], op=ALU.mult), r=("tA",), w=(tqt,))
                P.add("pe", lambda e, tqq=tqq, bank=bank: e.matmul(PS[bank][:], blk_f[:], tqq[:], start=True, stop=True), r=(tqt,), w=(psb(bank),))
                P.add("act", lambda e, tqq=tqq, bank=bank: e.activation(out=tqq[:], in_=PS[bank][:], func=AF.Sqrt, bias=epsc[:, 1:2], scale=1.0 / 64),
                      r=(psb(bank),), w=(tqt,))
                P.add("dve", lambda e, tqq=tqq: e.reciprocal(out=tqq[:], in_=tqq[:]), r=(tqt,), w=(tqt,))
                P.add("dve", lambda e, qs=qs, tqq=tqq: e.tensor_tensor(out=tA[:, qs], in0=tA[:, qs], in1=tqq[:], op=ALU.mult), r=("tA", tqt), w=("tA",))
                P.add("act", lambda e, qs=qs: e.activation(out=tA[:, qs], in_=tA[:, qs], func=AF.Identity, bias=par8["lnx_b"][:, hp:hp + 1], scale=par8["lnx_g"][:, hp:hp + 1]),
                      r=("tA",), w=("tA",))
                P.add("pool", lambda e, qs=qs: e.tensor_tensor(out=tA[:, qs], in0=tA[:, qs], in1=tB[:, qs], op=ALU.add), r=("tA", "tB"), w=("tA",))
                gbank = 4 + q % 4
                P.add("pe", lambda e, qs=qs, gbank=gbank: e.matmul(PS[gbank][:], g2a[:, hc], lo_g[:, qs], start=True, stop=False), r=LWW + ("lo_g",), w=(psb(gbank),))
                P.add("pe", lambda e, qs=qs, gbank=gbank: e.matmul(PS[gbank][:], a2g[64:96, hc], lo_a[64:96, qs], start=False, stop=True), r=LWW + ("lo_a",), w=(psb(gbank),))
                P.add("dve", lambda e, qs=qs, gbank=gbank: e.tensor_tensor(out=mixo[:, qs], in0=PS[gbank][:], in1=tA[:, qs], op=ALU.mult), r=(psb(gbank), "tA", "mixo"), w=("mixo",))
            P.add("sp", lambda e: e.dma_start(out=mixT_d[s, hp], in_=mixo[:]), r=("mixo",), w=(("mixd", hp),), chan="mixst")

        if do_rwkv:
            project(G_WFWB, to_raw)
            shift(lo_w[:], "lo_w", om_c[:, 24:25], mue_c[:, 24:25], muo_c[:, 24:25], post=AF.Tanh)
            project(G_AG1, to_raw)
            shift(lo_a[0:64, :], "lo_a", muag[:, 1:2], muag[:, 2:3], muag[:, 3:4], np_=64, p0=0, post=AF.Copy)
            shift(lo_a[64:96, :], "lo_a", muag[:, 1:2], muag[:, 2:3], muag[:, 3:4], np_=32, p0=64, post=AF.Sigmoid)
            project(G_G0, to_raw)
            shift(lo_g[:], "lo_g", mug0[:, 1:2], mug0[:, 2:3], mug0[:, 3:4], post=AF.Sigmoid)
            chk(20)
            for hp in range(8):
                do_hp(hp)
                chk(51)
        else:
            for hp in range(8):
                P.add("pool", lambda e: e.memset(mixo[:], 0.0), r=("mixo",), w=("mixo",))
                P.add("sp", lambda e, hp=hp: e.dma_start(out=mixT_d[s, hp], in_=mixo[:]), r=("mixo",), w=(("mixd", hp),), chan="mixst")

        def do_conv(cc):
            def cons_C(q, ps, ptok):
                P.add("act", lambda e: e.activation(out=tA[:, q * 512:(q + 1) * 512], in_=ps, func=AF.Copy), r=(ptok,), w=("tA",))
            project(G_CONV + 8 + cc, cons_C)
            P.add("pool", zero_pads, r=("rawp",), w=("rawp",))

            def cons_X(q, ps, ptok):
                P.add("dve", lambda e: e.tensor_tensor(out=rawp[:, 1 + q * 512:1 + (q + 1) * 512], in0=ps, in1=tA[:, q * 512:(q + 1) * 512], op=ALU.mult),
                      r=(ptok, "tA"), w=("rawp",))
            project(G_CONV + 16 + cc, cons_X)

            def cons_B(q, ps, ptok):
                P.add("act", lambda e: e.activation(out=tB[:, q * 512:(q + 1) * 512], in_=ps, func=AF.Copy), r=(ptok,), w=("tB",))
            project(G_CONV + cc, cons_B)
            cw = [par8["cw%d" % j][:, cc:cc + 1] for j in range(3)]
            P.add("act", lambda e: e.activation(out=tA[:], in_=rawp[:, 1:T + 1], func=AF.Identity, scale=cw[1]), r=("rawp", "tA"), w=("tA",))
            P.add("dve", lambda e: e.scalar_tensor_tensor(out=tA[:], in0=rawp[:, 0:T], scalar=cw[0], in1=tA[:], op0=ALU.mult, op1=ALU.add), r=("rawp", "tA"), w=("tA",))
            P.add("dve", lambda e: e.scalar_tensor_tensor(out=tA[:], in0=rawp[:, 2:T + 2], scalar=cw[2], in1=tA[:], op0=ALU.mult, op1=ALU.add), r=("rawp", "tA"), w=("tA",))
            P.add("dve", lambda e: e.tensor_tensor(out=mixo[:], in0=tA[:], in1=tB[:], op=ALU.mult), r=("tA", "tB", "mixo"), w=("mixo",))
            P.add("sp", lambda e: e.dma_start(out=mixT_d[s, 8 + cc], in_=mixo[:]), r=("mixo",), w=(("mixd", 8 + cc),), chan="mixst")
        for cc in range(8):
            do_conv(cc)

        P.barrier()
        if stage == 6:
            raise StopIteration

        apos[0] = 0
        mx = carve(KC * 512 * 2, BF16, [128, KC, 512])
        xr = carve(4 * D * 4, F32, [128, 4, D])
        wo = [carve(8 * 512 * 2, BF16, [128, 8, 512]) for _ in range(2)]
        xn2 = carve(D * 4, F32, [128, D])
        junk2 = carve(D * 2, BF16, [128, D])
        wgu = [[carve(KC * 128 * 2, BF16, [128, KC, 128]) for _ in range(2)] for _ in range(2)]
        actT = carve(FC * 512 * 2, BF16, [128, FC, 512])
        wdt = [carve(11 * 512 * 2, BF16, [128, 11, 512]) for _ in range(2)]
        sgt = [carve(512 * 4, F32, [128, 512]) for _ in range(2)]
        dgm = carve(128 * 4, F32, [128, 128])
        nfb = carve(D * 4, F32, [128, D])
        gtb = carve(D * 4, F32, [128, D])
        P.add("sp", lambda e: e.dma_start(out=nfb[:], in_=norm_f_g.partition_broadcast(128)), w=("nfb",), chan="nfb")

        def build_gtb(chunk0):
            for kq in range(4):
                bank = 4 + kq % 2
                for k4 in range(4):
                    kc = kq * 4 + k4
                    P.add("dve", lambda e, kc=kc: e.tensor_scalar(out=dgm[:], in0=id_f[:], scalar1=modT[:, s, chunk0 + kc:chunk0 + kc + 1], scalar2=None, op0=ALU.mult),
                          r=("dgm",), w=("dgm",))
                    P.add("pe", lambda e, k4=k4, bank=bank: e.matmul(PS[bank][:, k4 * 128:(k4 + 1) * 128], ones_f[:], dgm[:], start=True, stop=True), r=("dgm",), w=(psb(bank),))
                P.add("act", lambda e, kq=kq, bank=bank: e.activation(out=gtb[:, kq * 512:(kq + 1) * 512], in_=PS[bank][:], func=AF.Copy), r=(psb(bank),), w=("gtb",))

        wo_i = {"i": 0}
        wgu_i = {"i": 0}
        wd_i = {"i": 0}
        MXT = tuple(("mx", kc) for kc in range(KC))

        def do_tile(tt):
            tsl = slice(tt * 512, (tt + 1) * 512)
            for kc in range(KC):
                P.add("sp", lambda e, kc=kc: e.dma_start(out=mx[:, kc, :], in_=mixT_d[s, kc, :, tsl]), r=(("mixd", kc),), w=(("mx", kc),), chan="mxl")
            for j in range(4):
                P.add("sp", lambda e, j=j: e.dma_start(out=xr[:, j, :], in_=xs[s, tt * 512 + j * 128:tt * 512 + (j + 1) * 128, :]), w=(("xr", j),), chan="xrl")
            if stage == 70:
                raise StopIteration
            build_gtb(32)
            if stage == 71:
                raise StopIteration
            for dg in range(4):
                dsl = slice(dg * 512, (dg + 1) * 512)
                for half in range(2):
                    slot = wo_i["i"] % 2
                    wo_i["i"] += 1
                    P.add("sp", lambda e, slot=slot, dg=dg, half=half: e.dma_start(
                        out=wo[slot][:].rearrange("p a b -> p (a b)"), in_=wout_b[dg][:, half * 8 * 512:(half + 1) * 8 * 512]),
                        r=(("scr", "wout", dg, half),), w=(("wo", slot),), chan="wo%d" % slot)
                    for k8 in range(8):
                        kc = half * 8 + k8
                        for j in range(4):
                            P.add("pe", lambda e, j=j, k8=k8, kc=kc, slot=slot: e.matmul(PS[j][:], mx[:, kc, j * 128:(j + 1) * 128], wo[slot][:, k8, :],
                                                                                       start=(kc == 0), stop=(kc == KC - 1)),
                                  r=(("mx", kc), ("wo", slot)), w=(psb(j),))
                for j in range(4):
                    P.add("dve", lambda e, j=j, dsl=dsl: e.tensor_tensor(out=sgt[j % 2][:], in0=PS[j][:], in1=gtb[:, dsl], op=ALU.mult), r=(psb(j), "gtb"), w=(("sgt", j % 2),))
                    P.add("pool", lambda e, j=j, dsl=dsl: e.tensor_tensor(out=xr[:, j, dsl], in0=xr[:, j, dsl], in1=sgt[j % 2][:], op=ALU.add), r=(("sgt", j % 2), ("xr", j)), w=(("xr", j),))
            if stage == 7:
                raise StopIteration
            P.barrier()
            for j in range(4):
                norm_transpose(xr[:, j, :], ("xr", j), (s1c if stage == 84 else s2c)[:, s, :], modT[:, s, 0:16] if stage == 84 else modT[:, s, 48:64], mx, slice(j * 128, (j + 1) * 128), (xn2[:], "xn2", junk2[:]),
                               lambda kc: ("mx", kc), bank0=6)
            if stage in (8, 81, 82, 83, 84):
                raise StopIteration
            for fb in range(FC):
                slot = wgu_i["i"] % 2
                wgu_i["i"] += 1
                P.add("sp", lambda e, fb=fb, slot=slot: e.dma_start(out=wgu[0][slot][:].rearrange("p a b -> p (a b)"), in_=wg_b[fb]),
                      r=(("scr", "wg", fb),), w=(("wgu", 0, slot),), chan="wgu%d" % slot)
                P.add("sp", lambda e, fb=fb, slot=slot: e.dma_start(out=wgu[1][slot][:].rearrange("p a b -> p (a b)"), in_=wu_b[fb]),
                      r=(("scr", "wu", fb),), w=(("wgu", 1, slot),), chan="wgu%d" % slot)
                gb = 4 + 2 * (fb % 2)
                for w_ in range(2):
                    for kc in range(KC):
                        P.add("pe", lambda e, w_=w_, kc=kc, slot=slot, gb=gb: e.matmul(PS[gb + w_][:], wgu[w_][slot][:, kc, :], mx[:, kc, :], start=(kc == 0), stop=(kc == KC - 1)),
                              r=(("wgu", w_, slot), ("mx", kc)), w=(psb(gb + w_),))
                P.add("act", lambda e, fb=fb, gb=gb: e.activation(out=sgt[fb % 2][:], in_=PS[gb][:], func=AF.Silu), r=(psb(gb),), w=(("sgt", fb % 2),))
                P.add("dve", lambda e, fb=fb, gb=gb: e.tensor_tensor(out=actT[:, fb, :], in0=PS[gb + 1][:], in1=sgt[fb % 2][:], op=ALU.mult),
                      r=(psb(gb + 1), ("sgt", fb % 2)), w=(("actT", fb),))
            if stage == 9:
                raise StopIteration
            build_gtb(80)
            for dg in range(4):
                dsl = slice(dg * 512, (dg + 1) * 512)
                for fq in range(4):
                    slot = wd_i["i"] % 2
                    wd_i["i"] += 1
                    P.add("sp", lambda e, slot=slot, dg=dg, fq=fq: e.dma_start(
                        out=wdt[slot][:].rearrange("p a b -> p (a b)"), in_=wd_b[dg, fq]),
                        r=(("scr", "wd", dg, fq),), w=(("wdt", slot),), chan="wd%d" % slot)
                    for f11 in range(11):
                        fc = fq * 11 + f11
                        for j in range(4):
                            P.add("pe", lambda e, j=j, f11=f11, fc=fc, slot=slot: e.matmul(PS[j][:], actT[:, fc, j * 128:(j + 1) * 128], wdt[slot][:, f11, :],
                                                                                         start=(fc == 0), stop=(fc == FC - 1)),
                                  r=(("actT", fc), ("wdt", slot)), w=(psb(j),))
                for j in range(4):
                    P.add("dve", lambda e, j=j, dsl=dsl: e.tensor_tensor(out=sgt[j % 2][:], in0=PS[j][:], in1=gtb[:, dsl], op=ALU.mult), r=(psb(j), "gtb"), w=(("sgt", j % 2),))
                    P.add("pool", lambda e, j=j, dsl=dsl: e.tensor_tensor(out=xr[:, j, dsl], in0=xr[:, j, dsl], in1=sgt[j % 2][:], op=ALU.add), r=(("sgt", j % 2), ("xr", j)), w=(("xr", j),))
            if stage == 10:
                raise StopIteration
            for j in range(4):
                P.add("act", lambda e, j=j: e.activation(out=junk2[:], in_=xr[:, j, :], func=AF.Square, accum_out=stat[:, 2:3]), r=(("xr", j),), w=("stat", "junk"))
                rstd_from_ss(stat[:, 2:3], stat[:, 3:4], 1.0 / D, 0)
                P.add("dve", lambda e, j=j: e.scalar_tensor_tensor(out=xr[:, j, :], in0=xr[:, j, :], scalar=stat[:, 3:4], in1=nfb[:], op0=ALU.mult, op1=ALU.mult),
                      r=(("xr", j), "stat", "nfb"), w=(("xr", j),))
                P.add("sp", lambda e, j=j: e.dma_start(out=ys[s, tt * 512 + j * 128:tt * 512 + (j + 1) * 128, :], in_=xr[:, j, :]), r=(("xr", j),), w=(("ysd", j),), chan="yst")
        for tt in range(NQ):
            do_tile(tt)
        P.barrier()

    try:
        for s in range(NSEQ):
            do_seq(s)
    except StopIteration:
        pass

    P.finish()
    P.emit(nc, es)
    es.close()
    return nc


_W_NAMES = ["w_ada", "b_ada", "norm1_g", "w_in", "mu_shift", "w0_decay", "w2_decay", "a0", "a2", "g2_gate", "k_k", "k_a", "r_k",
            "lnx_g", "lnx_b", "conv_w", "w_out", "norm2_g", "w_ffn_gate", "w_ffn_up", "w_ffn_down"]


def _weights_map(inputs):
    m = {}
    for nm in _W_NAMES:
        a = np.asarray(inputs[nm], dtype=np.float32)
        a = a[0]
        if nm == "r_k":
            a = a.reshape(-1)
        m[nm] = np.ascontiguousarray(a)
    m["norm_f_g"] = np.ascontiguousarray(np.asarray(inputs["norm_f_g"], dtype=np.float32))
    return m


def kernel(**inputs):
    x_prompt = np.asarray(inputs["x_prompt"], dtype=np.float32)
    x_sample = np.asarray(inputs["x_sample"], dtype=np.float32)
    c_prompt = np.asarray(inputs["c_prompt"], dtype=np.float32)
    c_sample = np.asarray(inputs["c_sample"], dtype=np.float32)
    NB, T = x_prompt.shape[0], x_prompt.shape[1]
    NS = x_sample.shape[0]
    ntot = NB + NS
    NSEQ = 3
    ncore = 8

    def getx(i):
        return x_prompt[i] if i < NB else x_sample[i - NB]

    def getc(i):
        return c_prompt[i] if i < NB else c_sample[i - NB]

    wm = _weights_map(inputs)
    in_maps = []
    assign = []
    for c in range(ncore):
        ids = [c, c + 8, c + 16 if c + 16 < ntot else c]
        assign.append(ids)
        m = dict(wm)
        m["xs"] = np.stack([getx(i) for i in ids])
        m["cs"] = np.stack([getc(i) for i in ids])
        in_maps.append(m)
    nc = build_program(NSEQ, T)
    res = run_bass_kernel_spmd(nc, in_maps, core_ids=list(range(ncore)))
    y_p = np.empty_like(x_prompt)
    y_s = np.empty_like(x_sample)
    for c in range(ncore):
        ysc = res.results[c]["ys"]
        for slot, i in enumerate(assign[c]):
            if slot == 2 and c + 16 >= ntot:
                continue
            if i < NB:
                y_p[i] = ysc[slot]
            else:
                y_s[i - NB] = ysc[slot]
    return (y_p, y_s)
```

```python
import math
from contextlib import ExitStack
import numpy as np
import concourse.bass as bass
import concourse.mybir as mybir
from concourse.bass_utils import run_bass_kernel_spmd

F32 = mybir.dt.float32
BF16 = mybir.dt.bfloat16
AF = mybir.ActivationFunctionType
ALU = mybir.AluOpType
AX = mybir.AxisListType

D = 2048
KC = 16
RW = 1024
RWC = 3424
IN_COLS = 6496
DFF = 5632
FC = 44
NMOD = 6
RMS_EPS = 1e-6
LNX_EPS = 64e-5
C0 = -math.exp(-0.5)

GROUPS = []
for i in range(8):
    GROUPS.append([(i * 128, 128, 0)])
for i in range(8):
    GROUPS.append([(1024 + i * 128, 128, 0)])
for i in range(8):
    GROUPS.append([(2048 + i * 128, 128, 0)])
G_WFWB = len(GROUPS); GROUPS.append([(3072, 128, 0)])
G_AG1 = len(GROUPS); GROUPS.append([(3200, 64, 0), (3392, 32, 64), (3392, 32, 96)])
G_G0 = len(GROUPS); GROUPS.append([(3264, 128, 0)])
G_CONV = len(GROUPS)
for j in range(3):
    for i in range(8):
        GROUPS.append([(RWC + j * 1024 + i * 128, 128, 0)])
NG = len(GROUPS)


class Prog:
    ENGS = ("pe", "act", "dve", "pool", "sp")

    def __init__(self):
        self.ops = []
        self.last_w = {}
        self.readers = {}
        self.chan_cnt = {}

    def add(self, eng, fn, r=(), w=(), chan=None):
        i = len(self.ops)
        deps = set()
        for t in r:
            lw = self.last_w.get(t)
            if lw is not None:
                deps.add(lw)
        for t in w:
            lw = self.last_w.get(t)
            if lw is not None:
                deps.add(lw)
            for rd in self.readers.get(t, ()):
                deps.add(rd)
        for t in r:
            self.readers.setdefault(t, []).append(i)
        for t in w:
            self.last_w[t] = i
            self.readers[t] = []
        cval = None
        if chan is not None:
            self.chan_cnt[chan] = self.chan_cnt.get(chan, 0) + 1
            cval = 16 * self.chan_cnt[chan]
        import sys as _s
        fr = _s._getframe(1)
        self.ops.append(dict(eng=eng, fn=fn, deps=deps, chan=chan, cval=cval, sig=False, seq=0, tag=fr.f_lineno))
        return i

    def barrier(self):
        last = {}
        for i, op in enumerate(self.ops):
            if op["fn"] is None:
                continue
            if op["chan"] is not None:
                last[("c", op["chan"])] = i
            else:
                last[("e", op["eng"])] = i
        deps = set(last.values())
        for e in self.ENGS:
            self.ops.append(dict(eng=e, fn=None, deps=set(deps), chan=None, cval=None, sig=False, seq=0))
        self.last_w = {}
        self.readers = {}

    def finish(self):
        self.barrier()

    def simulate(self):
        import bisect
        ops = self.ops
        for op in ops:
            for d in op["deps"]:
                if ops[d]["chan"] is None:
                    ops[d]["sig"] = True
        cnt = {e: 0 for e in self.ENGS}
        for op in ops:
            if op["chan"] is None and op["sig"] and op["fn"] is not None:
                cnt[op["eng"]] += 1
                op["seq"] = cnt[op["eng"]]
        chan_ops = {}
        for i, op in enumerate(ops):
            op["idx"] = i
            if op["chan"] is not None:
                chan_ops.setdefault(op["chan"], []).append(i)
        per = {e: [op for op in ops if op["eng"] == e] for e in self.ENGS}
        pos = {e: 0 for e in self.ENGS}
        sem = {}
        progress = True
        while progress:
            progress = False
            for e in self.ENGS:
                while pos[e] < len(per[e]):
                    op = per[e][pos[e]]
                    ok = True
                    for d in op["deps"]:
                        dop = ops[d]
                        if dop["chan"] is not None:
                            key = ("c", dop["chan"])
                            if str(dop["chan"]).startswith("pcs"):
                                val = dop["cval"]
                            else:
                                val = 16 * bisect.bisect_left(chan_ops[dop["chan"]], op["idx"])
                        else:
                            if dop["eng"] == e and e == "pe":
                                continue
                            if dop["fn"] is None:
                                print("DEP ON NONE OP", op["idx"], d)
                            key = ("e", dop["eng"]); val = dop["seq"]
                        if sem.get(key, 0) < val:
                            ok = False
                            blk = (key, val, sem.get(key, 0), d)
                            break
                    if not ok:
                        op["blk"] = blk
                        break
                    if op["fn"] is not None:
                        if op["chan"] is not None:
                            sem[("c", op["chan"])] = sem.get(("c", op["chan"]), 0) + 16
                        elif op["sig"]:
                            sem[("e", e)] = sem.get(("e", e), 0) + 1
                    pos[e] += 1
                    progress = True
        stuck = {e: (pos[e], len(per[e])) for e in self.ENGS if pos[e] < len(per[e])}
        if stuck:
            print("DEADLOCK", stuck)
            for e in stuck:
                op = per[e][pos[e]]
                print(e, "op idx", op["idx"], "blocked on", op.get("blk"), "tag", op.get("tag"))
        else:
            print("simulate: no deadlock;", {e: len(per[e]) for e in self.ENGS})
        return not stuck

    def emit(self, nc, es):
        import os
        if os.environ.get("KSIM"):
            self.simulate()
        ops = self.ops
        for op in ops:
            for d in op["deps"]:
                if ops[d]["chan"] is None and not (ops[d]["eng"] == "pe" and op["eng"] == "pe"):
                    ops[d]["sig"] = True
        cnt = {e: 0 for e in self.ENGS}
        for op in ops:
            if op["chan"] is None and op["sig"]:
                cnt[op["eng"]] += 1
                op["seq"] = cnt[op["eng"]]
        sems = {}
        for e in self.ENGS:
            sems[("e", e)] = es.enter_context(nc.semaphore("sem_" + e))
        for c in self.chan_cnt:
            sems[("c", c)] = es.enter_context(nc.semaphore("ch_" + str(c)))
        block = es.enter_context(nc.Block())
        for i, op in enumerate(ops):
            op["idx"] = i
        per = {e: [op for op in ops if op["eng"] == e] for e in self.ENGS}
        import bisect
        chan_ops = {}
        for i, op in enumerate(ops):
            if op["chan"] is not None:
                chan_ops.setdefault(op["chan"], []).append(i)

        def run(eng_name, e):
            waited = {}
            for op in per[eng_name]:
                need = {}
                for d in op["deps"]:
                    dop = ops[d]
                    if dop["chan"] is not None:
                        key = ("c", dop["chan"])
                        if str(dop["chan"]).startswith("pcs"):
                            val = dop["cval"]
                        else:
                            lst = chan_ops[dop["chan"]]
                            k = bisect.bisect_left(lst, op["idx"])
                            val = 16 * k
                    else:
                        if dop["eng"] == eng_name and eng_name == "pe":
                            continue
                        key = ("e", dop["eng"]); val = dop["seq"]
                    if need.get(key, 0) < val:
                        need[key] = val
                for key, val in need.items():
                    if waited.get(key, 0) >= val:
                        continue
                    e.wait_ge(sems[key], val)
                    waited[key] = val
                if op["fn"] is None:
                    continue
                ins = op["fn"](e)
                if op["chan"] is not None:
                    ins.then_inc(sems[("c", op["chan"])], 16)
                elif op["sig"]:
                    ins.then_inc(sems[("e", eng_name)], 1)

        @block.tensor
        def _(e):
            run("pe", e)

        @block.scalar
        def _(e):
            run("act", e)

        @block.vector
        def _(e):
            run("dve", e)

        @block.gpsimd
        def _(e):
            run("pool", e)

        @block.sync
        def _(e):
            run("sp", e)


def build_program(NSEQ, T, do_rwkv=True, stage=99):
    NT = T // 128
    NQ = T // 512
    assert T % 512 == 0
    nc = bass.Bass("TRN2", target_bir_lowering=False)
    P = Prog()
    es = ExitStack()

    def din(name, shape):
        return nc.dram_tensor(name, list(shape), F32, kind="ExternalInput").ap()

    xs = din("xs", [NSEQ, T, D])
    cs = din("cs", [NSEQ, D])
    w_ada = din("w_ada", [D, NMOD * D])
    b_ada = din("b_ada", [NMOD * D])
    norm1_g = din("norm1_g", [D])
    w_in = din("w_in", [D, IN_COLS])
    mu_shift = din("mu_shift", [RWC])
    w0_decay = din("w0_decay", [2, RW])
    w2_decay = din("w2_decay", [2, 64, RW])
    a0 = din("a0", [RW])
    a2 = din("a2", [64, RW])
    g2_gate = din("g2_gate", [160, RW])
    k_k = din("k_k", [RW])
    k_a = din("k_a", [RW])
    r_k = din("r_k", [RW])
    lnx_g = din("lnx_g", [RW])
    lnx_b = din("lnx_b", [RW])
    conv_w = din("conv_w", [3, RW])
    w_out = din("w_out", [D, D])
    norm2_g = din("norm2_g", [D])
    w_ffn_gate = din("w_ffn_gate", [D, DFF])
    w_ffn_up = din("w_ffn_up", [D, DFF])
    w_ffn_down = din("w_ffn_down", [DFF, D])
    norm_f_g = din("norm_f_g", [D])
    ys = nc.dram_tensor("ys", [NSEQ, T, D], F32, kind="ExternalOutput").ap()

    wd_b = nc.dram_tensor("wd_b", [4, 4, 128, 11 * 512], BF16).ap()
    win_b = nc.dram_tensor("win_b", [NG, 128, KC * 128], BF16).ap()
    wg_b = nc.dram_tensor("wg_b", [FC, 128, KC * 128], BF16).ap()
    wu_b = nc.dram_tensor("wu_b", [FC, 128, KC * 128], BF16).ap()
    wout_b = nc.dram_tensor("wout_b", [4, 128, KC * 512], BF16).ap()
    mixT_d = nc.dram_tensor("mixT_d", [NSEQ, 16, 128, T], BF16).ap()

    def sb(name, shape, dt):
        return es.enter_context(nc.sbuf_tensor(name, list(shape), dt))

    id_f = sb("id_f", [128, 128], F32)
    id_b = sb("id_b", [128, 128], BF16)
    ones_f = sb("ones_f", [128, 128], F32)
    blk_f = sb("blk_f", [128, 128], F32)
    MA = [sb("MA%d" % d, [128, 512], F32) for d in range(2)]
    MX = [sb("MX%d" % d, [128, 256], F32) for d in range(2)]
    TRI = [sb("TRI%d" % d, [128, 256], F32) for d in range(2)]
    n1g = sb("n1g", [128, 16], F32)
    n2g = sb("n2g", [128, 16], F32)
    bada = sb("bada", [128, 96], F32)
    modT = sb("modT", [128, NSEQ, 96], F32)
    s1c = sb("s1c", [128, NSEQ, 16], F32)
    s2c = sb("s2c", [128, NSEQ, 16], F32)
    mu_c = sb("mu_c", [128, 27], F32)
    om_c = sb("om_c", [128, 27], F32)
    mue_c = sb("mue_c", [128, 27], F32)
    muo_c = sb("muo_c", [128, 27], F32)
    muag = sb("muag", [128, 4], F32)
    par8 = {nm: sb("p_" + nm, [128, 8], F32) for nm in ("a0", "k_k", "k_a", "r_k", "lnx_g", "lnx_b", "omka", "cw0", "cw1", "cw2")}
    evn = sb("evn", [128, 1], F32)
    mug0 = sb("mug0", [128, 4], F32)
    odd = sb("odd", [128, 1], F32)
    stat = sb("stat", [128, 8], F32)
    epsc = sb("epsc", [128, 2], F32)

    ARENA = 49800
    arena = sb("arena", [128, ARENA], F32)
    apos = [0]

    def carve(nbytes, dt, shape):
        n32 = (nbytes + 3) // 4
        n32 = (n32 + 7) // 8 * 8
        a = arena[:, apos[0]:apos[0] + n32]
        apos[0] += n32
        assert apos[0] <= ARENA, (apos[0], ARENA)
        if dt == BF16:
            a = a.bitcast(BF16)
        v = a
        if len(shape) == 3:
            v = a[:, 0:shape[1] * shape[2]].rearrange("p (a b) -> p a b", b=shape[2])
        elif len(shape) == 2:
            v = a[:, 0:shape[1]]
        return v

    PS = [es.enter_context(nc.psum_tensor("ps%d" % i, [128, 512], F32)) for i in range(8)]

    def psb(i):
        return ("ps", i)

    def pool_op(fn, r=(), w=()):
        return P.add("pool", fn, r, w)

    def mk_mask(ap, kind, tok="const"):
        mt = ("mask", id(ap), kind, len(P.ops))
        pool_op(lambda e: e.memset(ap, 1.0), w=(mt,))

        def f(e):
            if kind == "SL":
                return e.affine_select(out=ap, in_=ap, pattern=[[-1, 128]], compare_op=ALU.is_gt, fill=0.0, base=0, channel_multiplier=1)
            if kind == "SU":
                return e.affine_select(out=ap, in_=ap, pattern=[[1, 128]], compare_op=ALU.is_gt, fill=0.0, base=0, channel_multiplier=-1)
            if kind == "IU":
                return e.affine_select(out=ap, in_=ap, pattern=[[1, 128]], compare_op=ALU.is_ge, fill=0.0, base=0, channel_multiplier=-1)
            if kind == "IL":
                return e.affine_select(out=ap, in_=ap, pattern=[[-1, 128]], compare_op=ALU.is_ge, fill=0.0, base=0, channel_multiplier=1)
        pool_op(f, r=(mt,), w=(mt, tok))

    pool_op(lambda e: e.memset(id_f[:], 0.0), w=("id_f0",))
    pool_op(lambda e: e.affine_select(out=id_f[:], in_=id_f[:], pattern=[[-1, 128]], compare_op=ALU.not_equal, fill=1.0, base=0, channel_multiplier=1),
            r=("id_f0",), w=("id_f0", "const"))

    def f_ident(e):
        e.memset(ones_f[:], 1.0)
        e.memset(epsc[:, 0:1], RMS_EPS)
        return e.memset(epsc[:, 1:2], LNX_EPS)
    pool_op(f_ident, w=("const_b",))
    pool_op(lambda e: e.memset(blk_f[:], 0.0), w=("blk0",))

    def f_blk(e):
        e.memset(blk_f[0:64, 0:64], 1.0)
        return e.memset(blk_f[64:128, 64:128], 1.0)
    pool_op(f_blk, r=("blk0",), w=("blk0", "const_c"))
    pool_op(lambda e: e.tensor_copy(out=id_b[:], in_=id_f[:]), r=("const",), w=("const2",))
    for d, (ks, ki, kx) in enumerate((("SU", "IU", "SL"), ("SL", "IL", "SU"))):
        mk_mask(MA[d][:, 0:128], ks); mk_mask(MA[d][:, 128:256], ks)
        mk_mask(MA[d][:, 256:384], ki); mk_mask(MA[d][:, 384:512], ki)
        mk_mask(MX[d][:, 0:128], kx); mk_mask(MX[d][:, 128:256], kx)
    mk_mask(TRI[0][:, 0:128], "IU", tok=("tri", 0)); mk_mask(TRI[0][:, 128:256], "SU", tok=("tri", 0))
    mk_mask(TRI[1][:, 0:128], "IL", tok=("tri", 1)); mk_mask(TRI[1][:, 128:256], "SL", tok=("tri", 1))
    pool_op(lambda e: e.tensor_scalar(out=TRI[0][0:64, :], in0=TRI[0][0:64, :], scalar1=-1.0, scalar2=None, op0=ALU.add), r=(("tri", 0),), w=(("tri", 0),))
    pool_op(lambda e: e.tensor_scalar(out=TRI[1][64:128, :], in0=TRI[1][64:128, :], scalar1=-1.0, scalar2=None, op0=ALU.add), r=(("tri", 1),), w=(("tri", 1),))
    pool_op(lambda e: e.tensor_scalar(out=TRI[0][:], in0=TRI[0][:], scalar1=C0, scalar2=None, op0=ALU.mult), r=(("tri", 0),), w=(("tri", 0),))
    pool_op(lambda e: e.tensor_scalar(out=TRI[1][:], in0=TRI[1][:], scalar1=C0, scalar2=None, op0=ALU.mult), r=(("tri", 1),), w=(("tri", 1), "const3"))

    def f_par(e):
        idv = id_f[:].rearrange("p (a b) -> p a b", b=2)
        e.tensor_reduce(out=evn[:], in_=idv[:, :, 0], axis=AX.X, op=ALU.add)
        return e.tensor_reduce(out=odd[:], in_=idv[:, :, 1], axis=AX.X, op=ALU.add)
    P.add("dve", f_par, r=("const",), w=("const4",))

    def early(k):
        if stage == k:
            P.finish()
            P.emit(nc, es)
            es.close()
            return True
        return False
    if early(0):
        return nc
    pcn = {"i": 0}
    PCD = 4

    def precast(dst, src, tok, chan):
        k = pcn["i"] % PCD
        pcn["i"] += 1
        P.add("pool", lambda e: e.dma_start(out=dst, in_=src), r=(), w=(tok, ("pcring", k)), chan="pcs%d" % k)

    for g, parts in enumerate(GROUPS):
        dstg = win_b[g].rearrange("p (kc m) -> p kc m", m=128)
        for (c0, wd, d0) in parts:
            src = w_in[:, c0:c0 + wd].rearrange("(kc p) m -> p kc m", p=128)
            precast(dstg[:, :, d0:d0 + wd], src, ("scr", "win", g, d0), "pc_win")

    if early(1):
        return nc
    stg = sb("stg", [128, 128], F32)

    def load_cols(vec, n, dst_cols, tag):
        rows = n // 128
        rem = n - rows * 128
        nr = rows + (1 if rem else 0)
        P.add("pool", lambda e: e.memset(stg[:], 0.0), w=("stg",))
        if rows:
            P.add("sp", lambda e: e.dma_start(out=stg[0:rows, :], in_=vec[0:rows * 128].rearrange("(r c) -> r c", c=128)),
                  w=("stg",), chan="misc")
        if rem:
            P.add("sp", lambda e: e.dma_start(out=stg[rows:rows + 1, 0:rem], in_=vec[rows * 128:n].rearrange("(r c) -> r c", r=1)),
                  w=("stg",), chan="misc")
        P.add("pe", lambda e: e.transpose(PS[0][:, 0:128], stg[:], id_f[:]), r=("stg", "const"), w=(psb(0),))
        P.add("dve", lambda e: e.tensor_copy(out=dst_cols, in_=PS[0][:, 0:nr]), r=(psb(0),), w=("par", tag))

    load_cols(norm1_g, D, n1g[:], "n1g")
    load_cols(norm2_g, D, n2g[:], "n2g")
    load_cols(b_ada, NMOD * D, bada[:], "bada")
    load_cols(mu_shift, RWC, mu_c[:], "mu")
    for nm, v in (("a0", a0), ("k_k", k_k), ("k_a", k_a), ("r_k", r_k), ("lnx_g", lnx_g), ("lnx_b", lnx_b)):
        load_cols(v, RW, par8[nm][:], nm)
    for j in range(3):
        load_cols(conv_w[j], RW, par8["cw%d" % j][:], "cw%d" % j)
    P.add("pool", lambda e: e.memset(muag[:], 0.0), w=("muag",))
    P.add("sp", lambda e: e.dma_start(out=muag[0:64, 0:1], in_=mu_shift[3200:3264].rearrange("(p o) -> p o", o=1), allow_slow_non_contiguous=True),
          w=("muag",), chan="c_muag")
    P.add("sp", lambda e: e.dma_start(out=muag[64:96, 0:1], in_=mu_shift[3392:3424].rearrange("(p o) -> p o", o=1), allow_slow_non_contiguous=True),
          w=("muag",), chan="c_muag")

    P.add("sp", lambda e: e.dma_start(out=mug0[:, 0:1], in_=mu_shift[3264:3392].rearrange("(p o) -> p o", o=1), allow_slow_non_contiguous=True),
          w=("mug0",), chan="c_mug0")

    def f_mu(e):
        e.tensor_scalar(out=mug0[:, 1:2], in0=mug0[:, 0:1], scalar1=-1.0, scalar2=1.0, op0=ALU.mult, op1=ALU.add)
        e.tensor_scalar(out=mug0[:, 2:3], in0=mug0[:, 0:1], scalar1=evn[:, 0:1], scalar2=None, op0=ALU.mult)
        e.tensor_scalar(out=mug0[:, 3:4], in0=mug0[:, 0:1], scalar1=odd[:, 0:1], scalar2=None, op0=ALU.mult)
        e.tensor_scalar(out=om_c[:], in0=mu_c[:], scalar1=-1.0, scalar2=1.0, op0=ALU.mult, op1=ALU.add)
        e.tensor_scalar(out=mue_c[:], in0=mu_c[:], scalar1=evn[:, 0:1], scalar2=None, op0=ALU.mult)
        e.tensor_scalar(out=muo_c[:], in0=mu_c[:], scalar1=odd[:, 0:1], scalar2=None, op0=ALU.mult)
        e.tensor_scalar(out=muag[:, 1:2], in0=muag[:, 0:1], scalar1=-1.0, scalar2=1.0, op0=ALU.mult, op1=ALU.add)
        e.tensor_scalar(out=muag[:, 2:3], in0=muag[:, 0:1], scalar1=evn[:, 0:1], scalar2=None, op0=ALU.mult)
        e.tensor_scalar(out=muag[:, 3:4], in0=muag[:, 0:1], scalar1=odd[:, 0:1], scalar2=None, op0=ALU.mult)
        return e.tensor_scalar(out=par8["omka"][:], in0=par8["k_a"][:], scalar1=-1.0, scalar2=1.0, op0=ALU.mult, op1=ALU.add)
    P.add("dve", f_mu, r=(("par", "mu"), ("par", "k_a"), "muag", "mug0", "const3", "const4"), w=("mud",))

    if early(2):
        return nc
    apos[0] = 0
    cT = carve(KC * NSEQ * 2, BF16, [128, KC, NSEQ])
    crow = carve(D * 4, F32, [128, D])
    WA_G = 512
    wa = [carve(KC * WA_G * 2, BF16, [128, KC, WA_G]) for _ in range(2)]
    P.add("sp", lambda e: e.dma_start(out=crow[0:NSEQ, :], in_=cs), w=("crow",), chan="c_crow")
    for kc in range(KC):
        P.add("pe", lambda e, kc=kc: e.transpose(PS[1][:, kc * NSEQ:(kc + 1) * NSEQ], crow[0:NSEQ, kc * 128:(kc + 1) * 128], id_f[0:NSEQ, 0:NSEQ]),
              r=("crow", "const"), w=(psb(1),))
    P.add("act", lambda e: e.activation(out=cT[:].rearrange("p a b -> p (a b)"), in_=PS[1][:, 0:KC * NSEQ], func=AF.Silu), r=(psb(1),), w=("cT",))
    NWA = NMOD * D // WA_G
    for gi in range(NWA):
        slot = gi % 2
        P.add("pool", lambda e, gi=gi, slot=slot: e.dma_start(
            out=wa[slot][:], in_=w_ada[:, gi * WA_G:(gi + 1) * WA_G].rearrange("(kc p) m -> p kc m", p=128)),
            w=(("wa", slot),), chan="wa%d" % slot)
        bank = 2 + (gi % 2)
        for jj in range(WA_G // 128):
            for kc in range(KC):
                P.add("pe", lambda e, jj=jj, kc=kc, slot=slot, bank=bank: e.matmul(
                    PS[bank][:, jj * NSEQ:(jj + 1) * NSEQ], wa[slot][:, kc, jj * 128:(jj + 1) * 128], cT[:, kc, :],
                    start=(kc == 0), stop=(kc == KC - 1)), r=(("wa", slot), "cT"), w=(psb(bank),))
        nj = WA_G // 128
        for s in range(NSEQ):
            P.add("dve", lambda e, gi=gi, s=s, bank=bank, nj=nj: e.tensor_tensor(
                out=modT[:, s, gi * nj:(gi + 1) * nj],
                in0=PS[bank][:, 0:nj * NSEQ].rearrange("p (j s) -> p j s", s=NSEQ)[:, :, s],
                in1=bada[:, gi * nj:(gi + 1) * nj], op=ALU.add), r=(psb(bank), ("par", "bada")), w=("modT",))

    def f_s12(e):
        for s in range(NSEQ):
            e.scalar_tensor_tensor(out=s1c[:, s, :], in0=modT[:, s, 16:32], scalar=1.0, in1=n1g[:], op0=ALU.add, op1=ALU.mult)
            r_ = e.scalar_tensor_tensor(out=s2c[:, s, :], in0=modT[:, s, 64:80], scalar=1.0, in1=n2g[:], op0=ALU.add, op1=ALU.mult)
        return r_
    P.add("dve", f_s12, r=("modT", ("par", "n1g"), ("par", "n2g")), w=("s12",))

    if early(3):
        return nc
    for dg in range(4):
        dst = wout_b[dg].rearrange("p (kc n) -> p kc n", n=512)
        for half in range(2):
            src = w_out[half * 1024:(half + 1) * 1024, dg * 512:(dg + 1) * 512].rearrange("(kc p) n -> p kc n", p=128)
            precast(dst[:, half * 8:(half + 1) * 8, :], src, ("scr", "wout", dg, half), "pc_wout")
    if early(31):
        return nc
    import os as _os
    for fb in range(0 if _os.environ.get("SKIPGU") is None else FC, FC):
        for (wsrc, wdst, nm) in ((w_ffn_gate, wg_b, "wg"), (w_ffn_up, wu_b, "wu")):
            src = wsrc[:, fb * 128:(fb + 1) * 128].rearrange("(kc p) m -> p kc m", p=128)
            precast(wdst[fb].rearrange("p (kc m) -> p kc m", m=128), src, ("scr", nm, fb), "pc_" + nm)
    if early(32):
        return nc
    for dg in range(4):
        for fq in range(4):
            src = w_ffn_down[fq * 1408:(fq + 1) * 1408, dg * 512:(dg + 1) * 512].rearrange("(fc p) n -> p fc n", p=128)
            precast(wd_b[dg, fq].rearrange("p (fc n) -> p fc n", n=512), src, ("scr", "wd", dg, fq), "pc_wd")

    P.barrier()
    if early(4):
        return nc

    rr = {"i": 0}

    def evac_eng():
        import os
        if os.environ.get("EVAC"):
            return os.environ["EVAC"]
        return "dve"

    def rstd_from_ss(ss_col, out_col, scale, eps_idx):
        P.add("act", lambda e: e.activation(out=out_col, in_=ss_col, func=AF.Sqrt, bias=epsc[:, eps_idx:eps_idx + 1], scale=scale),
              r=("stat",), w=("stat",))
        P.add("dve", lambda e: e.reciprocal(out=out_col, in_=out_col), r=("stat",), w=("stat",))

    def norm_transpose(src_row, rtok, s_cols, sh_cols, dstT, dtoks, ttok, w_tok_fn, bank0=4):
        xn, xn_tok, junk = ttok
        P.add("act", lambda e: e.activation(out=junk, in_=src_row, func=AF.Square, accum_out=stat[:, 0:1]), r=(rtok,), w=("stat", "junk"))
        rstd_from_ss(stat[:, 0:1], stat[:, 1:2], 1.0 / D, 0)
        if stage == 81 and bank0 == 6:
            return
        P.add("dve", lambda e: e.tensor_scalar(out=xn, in0=src_row, scalar1=stat[:, 1:2], scalar2=None, op0=ALU.mult), r=(rtok, "stat"), w=(xn_tok,))
        if stage == 82 and bank0 == 6:
            return
        for kq in range(4):
            bank = bank0 + (kq % 2)
            for k4 in range(4):
                kc = kq * 4 + k4
                P.add("pe", lambda e, kc=kc, k4=k4, bank=bank: e.transpose(PS[bank][:, k4 * 128:(k4 + 1) * 128], xn[:, kc * 128:(kc + 1) * 128], id_f[:]),
                      r=(xn_tok,), w=(psb(bank),))
            if stage == 83 and bank0 == 6:
                continue
            for k4 in range(4):
                kc = kq * 4 + k4
                eng = evac_eng()
                if eng == "dve":
                    P.add("dve", lambda e, kc=kc, k4=k4, bank=bank: e.tensor_scalar(
                        out=dstT[:, kc, dtoks], in0=PS[bank][:, k4 * 128:(k4 + 1) * 128], scalar1=s_cols[:, kc:kc + 1], scalar2=sh_cols[:, kc:kc + 1],
                        op0=ALU.mult, op1=ALU.add), r=(psb(bank),), w=(w_tok_fn(kc),))
                else:
                    P.add("act", lambda e, kc=kc, k4=k4, bank=bank: e.activation(
                        out=dstT[:, kc, dtoks], in_=PS[bank][:, k4 * 128:(k4 + 1) * 128], func=AF.Identity,
                        bias=sh_cols[:, kc:kc + 1], scale=s_cols[:, kc:kc + 1]), r=(psb(bank),), w=(w_tok_fn(kc),))

    def chk(n):
        if stage == n:
            raise StopIteration

    import os as _os2
    for _i in range(int(_os2.environ.get("PEPAD", "0"))):
        P.add("pe", lambda e: e.matmul(PS[7][:, 0:128], id_b[:], id_b[:], start=True, stop=True), w=(psb(7),))

    def do_seq(s):
        apos[0] = 0
        hn1T = carve(KC * T * 2, BF16, [128, KC, T])
        a_mark = apos[0]
        xrow = [carve(D * 4, F32, [128, D]) for _ in range(2)]
        xn = carve(D * 4, F32, [128, D])
        junk = carve(D * 2, BF16, [128, D])
        for ti in range(NT):
            slot = ti % 2
            P.add("sp", lambda e, ti=ti, slot=slot: e.dma_start(out=xrow[slot][:], in_=xs[s, ti * 128:(ti + 1) * 128, :]),
                  w=(("xrow", slot),), chan="xr%d" % slot)
            norm_transpose(xrow[slot][:], ("xrow", slot), s1c[:, s, :], modT[:, s, 0:16], hn1T, slice(ti * 128, (ti + 1) * 128),
                           (xn[:], "xn", junk[:]), lambda kc, ti=ti: ("hn1T", ti // 4))
        P.barrier()
        if stage == 5:
            raise StopIteration

        apos[0] = a_mark
        rawp = carve((T + 2) * 4, F32, [128, T + 2])
        tA = carve(T * 4, F32, [128, T])
        tB = carve(T * 4, F32, [128, T])
        yacc = carve(T * 4, F32, [128, T])
        r_bf = carve(T * 2, BF16, [128, T])
        k_bf = carve(T * 2, BF16, [128, T])
        v_bf = carve(T * 2, BF16, [128, T])
        ka_bf = carve(T * 2, BF16, [128, T])
        nb_bf = carve(T * 2, BF16, [128, T])
        mixo = carve(T * 2, BF16, [128, T])
        lo_w = carve(T * 2, BF16, [128, T])
        lo_a = carve(T * 2, BF16, [128, T])
        lo_g = carve(T * 2, BF16, [128, T])
        w2b = carve(RW * 2, BF16, [128, RW])
        a2g = carve(RW * 2, BF16, [128, RW])
        g2a = carve(RW * 2, BF16, [128, RW])
        w0b = carve(256 * 4, F32, [128, 256])
        LW = [tA[:, 0:NT * 128].rearrange("p (a b) -> p a b", b=128), rawp[:, 0:NT * 128].rearrange("p (a b) -> p a b", b=128)]
        LWT = ["tA", "rawp"]
        vpad = carve(NT * 256 * 2, BF16, [128, NT, 256])
        win_t = [carve(KC * 128 * 2, BF16, [128, KC, 128]) for _ in range(2)]
        UB = {}
        for d in range(2):
            UB[d] = dict(
                E1=[carve(256 * 4, F32, [128, 256]) for _ in range(2)], Em=[carve(128 * 4, F32, [128, 128]) for _ in range(2)],
                fm=[carve(512 * 2, BF16, [128, 4, 128]) for _ in range(2)],
                rh=[carve(256 * 2, BF16, [128, 2, 128]) for _ in range(2)],
                kh=[carve(256 * 2, BF16, [128, 2, 128]) for _ in range(2)],
                tk=[carve(768 * 2, BF16, [128, 3, 256]) for _ in range(2)],
                A=carve(1024 * 2, BF16, [128, 2, 512]),
                X0=carve(256 * 2, BF16, [128, 2, 128]),
                XZ=[carve(512 * 2, BF16, [128, 4, 128]) for _ in range(2)],
                Pm=[carve(256 * 2, BF16, [128, 2, 128]) for _ in range(2)],
                akv=carve(256 * 2, BF16, [128, 256]), wtT=carve(128 * 2, BF16, [128, 128]),
                upad=carve(256 * 2, BF16, [128, 256]),
            )
        HN = [carve(128 * 4, F32, [128, 128]) for _ in range(2)]
        HB = [carve(128 * 2, BF16, [128, 128]) for _ in range(2)]
        GC = [carve(4 * 4, F32, [128, 4]) for _ in range(2)]
        tq = [carve(512 * 4, F32, [128, 512]) for _ in range(2)]

        def zero_pads(e):
            e.memset(rawp[:, 0:1], 0.0)
            return e.memset(rawp[:, T + 1:T + 2], 0.0)

        def zero_padded(e):
            e.memset(vpad[:].rearrange("p a b -> p (a b)"), 0.0)
            for d in range(2):
                for sl in range(2):
                    e.memset(UB[d]["tk"][sl][:].rearrange("p a b -> p (a b)"), 0.0)
                    e.memset(UB[d]["rh"][sl][:].rearrange("p a b -> p (a b)"), 0.0)
                    e.memset(UB[d]["kh"][sl][:].rearrange("p a b -> p (a b)"), 0.0)
                e.memset(UB[d]["akv"][:], 0.0)
                r_ = e.memset(UB[d]["upad"][:], 0.0)
            return r_
        P.add("pool", zero_padded, w=("vpad",) + tuple(("ub", d, nm) for d in range(2) for nm in ("akv", "upad"))
              + tuple(("ub", d, nm, sl) for d in range(2) for nm in ("tk", "fmh") for sl in range(2)))

        P.add("pool", lambda e: e.dma_start(out=w2b[0:64, :], in_=w2_decay[0]), w=(("lw_w", 0),), chan="lora")
        P.add("pool", lambda e: e.dma_start(out=w2b[64:128, :], in_=w2_decay[1]), w=(("lw_w", 1),), chan="lora")
        P.add("pool", lambda e: e.dma_start(out=a2g[0:64, :], in_=a2), w=(("lw_w", 2),), chan="lora")
        P.add("pool", lambda e: e.dma_start(out=a2g[64:96, :], in_=g2_gate[128:160, :]), w=(("lw_w", 3),), chan="lora")
        P.add("pool", lambda e: e.dma_start(out=g2a[:], in_=g2_gate[0:128, :]), w=(("lw_w", 4),), chan="lora")
        LWW = tuple(("lw_w", i) for i in range(5))

        wslot = {"i": 0}

        def project(g, consume):
            slot = wslot["i"] % 2
            wslot["i"] += 1
            rt = tuple(("scr", "win", g, d0) for (_, _, d0) in GROUPS[g])
            P.add("sp", lambda e: e.dma_start(out=win_t[slot][:].rearrange("p a b -> p (a b)"), in_=win_b[g]),
                  r=rt, w=(("win_t", slot),), chan="win%d" % slot)
            for q in range(NQ):
                bank = q % 4
                for kc in range(KC):
                    P.add("pe", lambda e, kc=kc, q=q, bank=bank: e.matmul(PS[bank][:], win_t[slot][:, kc, :], hn1T[:, kc, q * 512:(q + 1) * 512],
                                                                         start=(kc == 0), stop=(kc == KC - 1)),
                          r=(("win_t", slot),), w=(psb(bank),))
                consume(q, PS[bank][:], psb(bank))

        def to_raw(q, ps, ptok):
            eng = evac_eng()
            dst = rawp[:, 1 + q * 512:1 + (q + 1) * 512]
            if eng == "dve":
                P.add("dve", lambda e: e.tensor_copy(out=dst, in_=ps), r=(ptok,), w=("rawp",))
            else:
                P.add("act", lambda e: e.activation(out=dst, in_=ps, func=AF.Copy), r=(ptok,), w=("rawp",))

        def shift(dst, dst_tok, om, mue, muo, np_=128, p0=0, post=None):
            ps_ = slice(p0, p0 + np_)
            tmp = tA[ps_, :]
            P.add("pool", zero_pads, r=("rawp",), w=("rawp",))
            P.add("act", lambda e: e.activation(out=tmp, in_=rawp[ps_, 1:T + 1], func=AF.Identity, scale=om[ps_, :]), r=("rawp",), w=("tA",))
            P.add("dve", lambda e: e.scalar_tensor_tensor(out=tmp, in0=rawp[ps_, 0:T], scalar=mue[ps_, :], in1=tmp, op0=ALU.mult, op1=ALU.add),
                  r=("rawp", "tA"), w=("tA",))
            if post is None:
                P.add("dve", lambda e: e.scalar_tensor_tensor(out=dst, in0=rawp[ps_, 2:T + 2], scalar=muo[ps_, :], in1=tmp, op0=ALU.mult, op1=ALU.add),
                      r=("rawp", "tA"), w=(dst_tok,))
            else:
                P.add("dve", lambda e: e.scalar_tensor_tensor(out=tmp, in0=rawp[ps_, 2:T + 2], scalar=muo[ps_, :], in1=tmp, op0=ALU.mult, op1=ALU.add),
                      r=("rawp", "tA"), w=("tA",))
                P.add("act", lambda e: e.activation(out=dst, in_=tmp, func=post), r=("tA",), w=(dst_tok,))

        def do_hp(hp):
            hc = slice(hp * 128, (hp + 1) * 128)
            project(hp, to_raw)
            shift(r_bf[:], "r_bf", om_c[:, hp:hp + 1], mue_c[:, hp:hp + 1], muo_c[:, hp:hp + 1])
            project(16 + hp, to_raw)
            shift(v_bf[:], "v_bf", om_c[:, 16 + hp:17 + hp], mue_c[:, 16 + hp:17 + hp], muo_c[:, 16 + hp:17 + hp])
            project(8 + hp, to_raw)
            shift(tB[:], "tB", om_c[:, 8 + hp:9 + hp], mue_c[:, 8 + hp:9 + hp], muo_c[:, 8 + hp:9 + hp])
            chk(21)
            for q in range(NQ):
                bank = 4 + q % 2
                P.add("pe", lambda e, q=q, bank=bank: e.matmul(PS[bank][:], a2g[0:64, hc], lo_a[0:64, q * 512:(q + 1) * 512], start=True, stop=True),
                      r=LWW + ("lo_a",), w=(psb(bank),))
                P.add("act", lambda e, q=q, bank=bank: e.activation(out=yacc[:, q * 512:(q + 1) * 512], in_=PS[bank][:], func=AF.Sigmoid,
                                                                      bias=par8["a0"][:, hp:hp + 1]), r=(psb(bank),), w=("yacc",))
            P.add("dve", lambda e: e.tensor_scalar(out=tA[:], in0=tB[:], scalar1=par8["k_k"][:, hp:hp + 1], scalar2=None, op0=ALU.mult),
                  r=("tB",), w=("tA",))
            for q in range(NQ):
                qs = slice(q * 512, (q + 1) * 512)
                bank = 4 + q % 2
                tqq = tq[q % 2]
                tqt = ("tq", q % 2)
                P.add("pool", lambda e, qs=qs, tqq=tqq: e.tensor_tensor(out=tqq[:], in0=tA[:, qs], in1=tA[:, qs], op=ALU.mult), r=("tA",), w=(tqt,))
                P.add("pe", lambda e, tqq=tqq, bank=bank: e.matmul(PS[bank][:], blk_f[:], tqq[:], start=True, stop=True), r=(tqt,), w=(psb(bank),))
                P.add("act", lambda e, tqq=tqq, bank=bank: e.activation(out=tqq[:], in_=PS[bank][:], func=AF.Sqrt), r=(psb(bank),), w=(tqt,))
                P.add("dve", lambda e, tqq=tqq: e.tensor_scalar(out=tqq[:], in0=tqq[:], scalar1=1e-12, scalar2=None, op0=ALU.max), r=(tqt,), w=(tqt,))
                P.add("dve", lambda e, tqq=tqq: e.reciprocal(out=tqq[:], in_=tqq[:]), r=(tqt,), w=(tqt,))
                P.add("dve", lambda e, qs=qs, tqq=tqq: e.tensor_tensor(out=tA[:, qs], in0=tA[:, qs], in1=tqq[:], op=ALU.mult), r=("tA", tqt), w=("tA",))
            P.add("act", lambda e: e.activation(out=ka_bf[:], in_=tA[:], func=AF.Copy), r=("tA",), w=("ka_bf",))
            P.add("dve", lambda e: e.scalar_tensor_tensor(out=nb_bf[:], in0=tA[:], scalar=-1.0, in1=yacc[:], op0=ALU.mult, op1=ALU.mult),
                  r=("tA", "yacc"), w=("nb_bf",))
            P.add("dve", lambda e: e.tensor_scalar(out=yacc[:], in0=yacc[:], scalar1=par8["k_a"][:, hp:hp + 1], scalar2=par8["omka"][:, hp:hp + 1],
                                                   op0=ALU.mult, op1=ALU.add), r=("yacc",), w=("yacc",))
            P.add("pool", lambda e: e.tensor_tensor(out=k_bf[:], in0=tB[:], in1=yacc[:], op=ALU.mult), r=("tB", "yacc"), w=("k_bf",))
            chk(22)
            P.add("dve", lambda e: e.scalar_tensor_tensor(out=tA[:], in0=r_bf[:], scalar=par8["r_k"][:, hp:hp + 1], in1=k_bf[:], op0=ALU.mult, op1=ALU.mult),
                  r=("r_bf", "k_bf", "tA"), w=("tA",))
            for q in range(NQ):
                qs = slice(q * 512, (q + 1) * 512)
                bank = 4 + q % 2
                P.add("pe", lambda e, qs=qs, bank=bank: e.matmul(PS[bank][:], blk_f[:], tA[:, qs], start=True, stop=True), r=("tA",), w=(psb(bank),))
                P.add("dve", lambda e, qs=qs, bank=bank: e.tensor_tensor(out=tB[:, qs], in0=PS[bank][:], in1=v_bf[:, qs], op=ALU.mult),
                      r=(psb(bank), "v_bf", "tB"), w=("tB",))
            chk(23)
            for ti in range(NT):
                bank = 4 + ti % 2
                pst = PS[bank][:].bitcast(BF16)
                P.add("pe", lambda e, ti=ti, pst=pst: e.transpose(pst[:, 0:128], v_bf[:, ti * 128:(ti + 1) * 128], id_b[:]), r=("v_bf",), w=(psb(bank),))
                eng = evac_eng()
                if eng == "dve":
                    P.add("dve", lambda e, ti=ti, pst=pst: e.tensor_copy(out=vpad[:, ti, 0:64], in_=pst[:, 0:64]), r=(psb(bank),), w=("vpad",))
                    P.add("dve", lambda e, ti=ti, pst=pst: e.tensor_copy(out=vpad[:, ti, 192:256], in_=pst[:, 64:128]), r=(psb(bank),), w=("vpad",))
                else:
                    P.add("act", lambda e, ti=ti, pst=pst: e.activation(out=vpad[:, ti, 0:64], in_=pst[:, 0:64], func=AF.Copy), r=(psb(bank),), w=("vpad",))
                    P.add("act", lambda e, ti=ti, pst=pst: e.activation(out=vpad[:, ti, 192:256], in_=pst[:, 64:128], func=AF.Copy), r=(psb(bank),), w=("vpad",))
            chk(24)
            for d in range(2):
                P.add("sp", lambda e, d=d: e.dma_start(out=w0b[:, d * 128:(d + 1) * 128], in_=w0_decay[d, hc].partition_broadcast(128)),
                      w=(("w0b", d),), chan="w0b")
            for d in range(2):
                dr = slice(d * 64, (d + 1) * 64)
                for t4 in range(NT // 4):
                    bank = 4 + t4 % 2
                    for j in range(4):
                        ti = t4 * 4 + j
                        P.add("pe", lambda e, ti=ti, j=j, bank=bank, dr=dr: e.matmul(PS[bank][:, j * 128:(j + 1) * 128], lo_w[dr, ti * 128:(ti + 1) * 128], w2b[dr, hc],
                                                                                   start=True, stop=True), r=("lo_w",) + LWW, w=(psb(bank),))
                    for j in range(4):
                        ti = t4 * 4 + j
                        P.add("dve", lambda e, ti=ti, j=j, bank=bank, d=d: e.tensor_tensor(out=LW[d][:, ti, :], in0=PS[bank][:, j * 128:(j + 1) * 128],
                                                                                        in1=w0b[:, d * 128:(d + 1) * 128], op=ALU.add),
                              r=(psb(bank), ("w0b", d)), w=(LWT[d],))
                P.add("act", lambda e, d=d: e.activation(out=LW[d][:].rearrange("p a b -> p (a b)"), in_=LW[d][:].rearrange("p a b -> p (a b)"), func=AF.Sigmoid),
                      r=(LWT[d],), w=(LWT[d],))

            chk(25)
            def zero_state(e):
                for d in range(2):
                    e.memset(HN[d][:], 0.0)
                    e.memset(GC[d][:], 1.0)
                    r_ = e.memset(HB[d][:], 0.0)
                return r_
            P.add("pool", zero_state, w=(("HN", 0), ("HN", 1), ("HB", 0), ("HB", 1), ("GC", 0, 0), ("GC", 0, 1), ("GC", 1, 0), ("GC", 1, 1)))

            def unit_stages(d, step):
                ti = step if d == 0 else NT - 1 - step
                U = UB[d]
                E1 = U["E1"][step % 2]
                E1p = U["E1"][(step - 1) % 2]
                e1t = ("ub", d, "E1", step % 2)
                e1pt = ("ub", d, "E1", (step - 1) % 2)
                kb = ("ub", d)
                b0, b1, b2, b3 = 4 * d, 4 * d + 1, 4 * d + 2, 4 * d + 3
                ts_ = slice(ti * 128, (ti + 1) * 128)
                endcol = 127 if d == 0 else 0
                startcol = 128 + (0 if d == 0 else 127)
                st = []
                sl = step % 2
                fm = U["fm"][sl]
                tkk = U["tk"][sl]
                Em = U["Em"][sl]
                rhh = U["rh"][sl]
                khh = U["kh"][sl]
                FMT = kb + ("fm", sl)
                TKT = kb + ("tk", sl)
                FHT = kb + ("fmh", sl)
                EMT = kb + ("Em", sl)
                gcol = 2 * sl
                GCT = ("GC", d, sl)

                def s_prep():
                    P.add("pe", lambda e: e.matmul(PS[b1][:, 0:256], LW[d][:, ti, :], TRI[d][:], start=True, stop=True), r=(LWT[d],), w=(psb(b1),))
                    P.add("act", lambda e: e.activation(out=E1[:], in_=PS[b1][:, 0:256], func=AF.Exp), r=(psb(b1),), w=(e1t,))
                    P.add("act", lambda e: e.activation(out=Em[:], in_=PS[b1][:, 0:128], func=AF.Exp, scale=-1.0), r=(psb(b1),), w=(EMT,))
                    if step > 0:
                        P.add("dve", lambda e: e.reciprocal(out=GC[d][:, gcol + 1:gcol + 2], in_=E1[:, startcol:startcol + 1]), r=(e1t,), w=(("GCt", d, sl),))
                        P.add("dve", lambda e: e.tensor_tensor(out=GC[d][:, gcol:gcol + 1], in0=E1p[:, endcol:endcol + 1], in1=GC[d][:, gcol + 1:gcol + 2], op=ALU.mult),
                              r=(e1pt, ("GCt", d, sl)), w=(GCT,))
                    P.add("dve", lambda e: e.tensor_tensor(out=fm[:, 0, :], in0=r_bf[:, ts_], in1=E1[:, 0:128], op=ALU.mult), r=("r_bf", e1t), w=(FMT,))
                    P.add("pool", lambda e: e.tensor_tensor(out=fm[:, 1, :], in0=k_bf[:, ts_], in1=Em[:], op=ALU.mult), r=("k_bf", EMT), w=(FMT,))
                    P.add("pool", lambda e: e.tensor_tensor(out=fm[:, 2, :], in0=nb_bf[:, ts_], in1=Em[:], op=ALU.mult), r=("nb_bf", EMT), w=(FMT,))
                    P.add("dve", lambda e: e.tensor_tensor(out=fm[:, 3, :], in0=ka_bf[:, ts_], in1=E1[:, 128:256], op=ALU.mult), r=("ka_bf", e1t), w=(FMT,))
                    for h in range(2):
                        hs = slice(h * 64, (h + 1) * 64)
                        P.add("pool", lambda e, h=h, hs=hs: e.tensor_copy(out=rhh[hs, h, :], in_=fm[hs, 0, :]), r=(FMT,), w=(FHT,))
                        P.add("pool", lambda e, h=h, hs=hs: e.tensor_copy(out=khh[hs, h, :], in_=fm[hs, 3, :]), r=(FMT,), w=(FHT,))
                    pst = PS[b1][:].bitcast(BF16)
                    for m in range(3):
                        P.add("pe", lambda e, m=m: e.transpose(pst[:, 512 + m * 128:512 + (m + 1) * 128], fm[:, 1 + m, :], id_b[:]), r=(FMT,), w=(psb(b1),))
                    for m in range(3):
                        eng = evac_eng()
                        for h in range(2):
                            src = pst[:, 512 + m * 128 + h * 64:512 + m * 128 + (h + 1) * 64]
                            dst = tkk[:, m, h * 192:h * 192 + 64]
                            if eng == "dve":
                                P.add("dve", lambda e, src=src, dst=dst: e.tensor_copy(out=dst, in_=src), r=(psb(b1),), w=(TKT,))
                            else:
                                P.add("act", lambda e, src=src, dst=dst: e.activation(out=dst, in_=src, func=AF.Copy), r=(psb(b1),), w=(TKT,))
                st.append(s_prep)

                def s_A():
                    for h in range(2):
                        hs = slice(h * 64, (h + 1) * 64)
                        bank = b0 + h
                        for j, (li, rsrc) in enumerate(((2, "kh"), (1, "kh"), (1, "rh"), (2, "rh"))):
                            P.add("pe", lambda e, j=j, li=li, rsrc=rsrc, h=h, bank=bank: e.matmul(PS[bank][:, j * 128:(j + 1) * 128], fm[:, li, :], (khh if rsrc == "kh" else rhh)[:, h, :], start=True, stop=True),
                                  r=(FMT, FHT), w=(psb(bank),))
                        P.add("pe", lambda e, h=h: e.matmul(PS[b2][:, h * 128:(h + 1) * 128], khh[:, h, :], fm[:, 2, :], start=True, stop=True),
                              r=(FMT, FHT), w=(psb(b2),))
                    if step == 0 and d == 0:
                        chk(201)
                    for h in range(2):
                        P.add("dve", lambda e, h=h: e.tensor_tensor(out=U["A"][:, h, :], in0=PS[b0 + h][:], in1=MA[d][:], op=ALU.mult), r=(psb(b0 + h),), w=(kb + ("A",),))
                    P.add("dve", lambda e: e.tensor_tensor(out=U["X0"][:].rearrange("p a b -> p (a b)"), in0=PS[b2][:, 0:256], in1=MX[d][:], op=ALU.mult),
                          r=(psb(b2),), w=(kb + ("X0",),))
                    if step == 0 and d == 0:
                        chk(202)
                    for h in range(2):
                        P.add("pool", lambda e, h=h: e.tensor_tensor(out=U["Pm"][0][:, h, :], in0=U["A"][:, h, 0:128], in1=id_b[:], op=ALU.add),
                              r=(kb + ("A",),), w=(kb + ("P", 0),))
                    if step == 0 and d == 0:
                        chk(203)
                    for h in range(2):
                        P.add("pe", lambda e, h=h: e.matmul(PS[b3][:, 256:384], U["A"][:, h, 128:256], vpad[:, ti, h * 128:(h + 1) * 128], start=(h == 0), stop=(h == 1)),
                              r=(kb + ("A",), "vpad"), w=(psb(b3),))
                    P.add("act", lambda e: e.activation(out=U["akv"][:, 0:64], in_=PS[b3][:, 256:320], func=AF.Copy), r=(psb(b3),), w=(kb + ("akv",),))
                    P.add("act", lambda e: e.activation(out=U["akv"][:, 192:256], in_=PS[b3][:, 320:384], func=AF.Copy), r=(psb(b3),), w=(kb + ("akv",),))
                st.append(s_A)

                def mk_level(lev):
                    def s_lev():
                        import os
                        if os.environ.get("LEVBAR") and lev == 1 and d == 0 and step == 0:
                            P.barrier()
                        cur = lev % 2
                        nxt = (lev + 1) % 2
                        XZn = U["XZ"][nxt]
                        for h in range(2):
                            if lev == 0:
                                Xh = U["X0"][:, h, :]
                                Zh = U["A"][:, h, 0:128]
                                rt = (kb + ("X0",), kb + ("A",))
                            else:
                                Xh = U["XZ"][cur][:, 2 * h, :]
                                Zh = U["XZ"][cur][:, 2 * h + 1, :]
                                rt = (kb + ("XZ", cur),)
                            P.add("pe", lambda e, h=h, Xh=Xh, Zh=Zh: e.matmul(PS[b2][:, (2 * h) * 128:(2 * h + 1) * 128], Zh, Xh, start=True, stop=True), r=rt, w=(psb(b2),))
                            if lev < 5:
                                P.add("pe", lambda e, h=h, Xh=Xh, Zh=Zh: e.matmul(PS[b2][:, (2 * h + 1) * 128:(2 * h + 2) * 128], Xh, Zh, start=True, stop=True), r=rt, w=(psb(b2),))
                        if lev == 1 and d == 0 and step == 0:
                            chk(60)
                        if lev < 5:
                            P.add("act", lambda e: e.activation(out=XZn[:].rearrange("p a b -> p (a b)"), in_=PS[b2][:], func=AF.Copy), r=(psb(b2),), w=(kb + ("XZ", nxt),))
                        if lev == 1 and d == 0 and step == 0:
                            chk(61)
                        else:
                            for h in range(2):
                                P.add("act", lambda e, h=h: e.activation(out=XZn[:, 2 * h, :], in_=PS[b2][:, (2 * h) * 128:(2 * h + 1) * 128], func=AF.Copy),
                                      r=(psb(b2),), w=(kb + ("XZ", nxt),))
                        Pc = U["Pm"][cur]
                        Pn = U["Pm"][nxt]
                        for h in range(2):
                            P.add("pe", lambda e, h=h: e.matmul(PS[b3][:, h * 128:(h + 1) * 128], XZn[:, 2 * h, :], Pc[:, h, :], start=True, stop=True),
                                  r=(kb + ("XZ", nxt), kb + ("P", cur)), w=(psb(b3),))
                        if lev == 1 and d == 0 and step == 0:
                            chk(62)
                        P.add("dve", lambda e: e.tensor_tensor(out=Pn[:].rearrange("p a b -> p (a b)"), in0=PS[b3][:, 0:256], in1=Pc[:].rearrange("p a b -> p (a b)"), op=ALU.add),
                              r=(psb(b3), kb + ("P", cur)), w=(kb + ("P", nxt),))
                    return s_lev
                for lev in range(6):
                    st.append(mk_level(lev))

                def s_tail():
                    TT = U["Pm"][0]
                    for h in range(2):
                        P.add("pe", lambda e, h=h: e.matmul(PS[b3][:, 256:384], TT[:, h, :], U["akv"][:, h * 128:(h + 1) * 128], start=(h == 0), stop=False),
                              r=(kb + ("P", 0), kb + ("akv",)), w=(psb(b3),))
                    for h in range(2):
                        P.add("pe", lambda e, h=h: e.matmul(PS[b1][:, 128:256], tkk[:, 2, h * 128:(h + 1) * 128], TT[:, h, :], start=(h == 0), stop=(h == 1)),
                              r=(kb + ("P", 0), TKT), w=(psb(b1),))
                    P.add("act", lambda e: e.activation(out=U["wtT"][:], in_=PS[b1][:, 128:256], func=AF.Copy), r=(psb(b1),), w=(kb + ("wtT",),))
                    P.add("act", lambda e: e.activation(out=HB[d][:], in_=HN[d][:], func=AF.Identity, scale=GC[d][:, gcol:gcol + 1]), r=(("HN", d), GCT), w=(("HB", d),))
                    P.add("pe", lambda e: e.matmul(PS[b3][:, 256:384], U["wtT"][:], HB[d][:], start=False, stop=True), r=(kb + ("wtT",), ("HB", d)), w=(psb(b3),))
                    P.add("dve", lambda e: e.tensor_copy(out=U["upad"][:, 0:64], in_=PS[b3][:, 256:320]), r=(psb(b3),), w=(kb + ("upad",),))
                    P.add("dve", lambda e: e.tensor_copy(out=U["upad"][:, 192:256], in_=PS[b3][:, 320:384]), r=(psb(b3),), w=(kb + ("upad",),))
                    yb = PS[b0][:, 0:128]
                    P.add("pe", lambda e: e.matmul(yb, HB[d][:], fm[:, 0, :], start=True, stop=False), r=(("HB", d), FMT), w=(psb(b0),))
                    for h in range(2):
                        P.add("pe", lambda e, h=h: e.matmul(yb, vpad[:, ti, h * 128:(h + 1) * 128], U["A"][:, h, 256:384], start=False, stop=False),
                              r=("vpad", kb + ("A",)), w=(psb(b0),))
                    for h in range(2):
                        P.add("pe", lambda e, h=h: e.matmul(yb, U["upad"][:, h * 128:(h + 1) * 128], U["A"][:, h, 384:512], start=False, stop=(h == 1)),
                              r=(kb + ("upad",), kb + ("A",)), w=(psb(b0),))
                    if step < NT // 2:
                        P.add("act", lambda e: e.activation(out=yacc[:, ts_], in_=yb, func=AF.Copy), r=(psb(b0),), w=("yacc",))
                    else:
                        P.add("dve", lambda e: e.tensor_tensor(out=yacc[:, ts_], in0=yb, in1=yacc[:, ts_], op=ALU.add), r=(psb(b0), "yacc"), w=("yacc",))
                    hb_ = PS[b1][:, 0:128]
                    for h in range(2):
                        P.add("pe", lambda e, h=h: e.matmul(hb_, tkk[:, 0, h * 128:(h + 1) * 128], vpad[:, ti, h * 128:(h + 1) * 128], start=(h == 0), stop=False),
                              r=(TKT, "vpad"), w=(psb(b1),))
                    for h in range(2):
                        P.add("pe", lambda e, h=h: e.matmul(hb_, tkk[:, 1, h * 128:(h + 1) * 128], U["upad"][:, h * 128:(h + 1) * 128], start=False, stop=(h == 1)),
                              r=(TKT, kb + ("upad",)), w=(psb(b1),))
                    P.add("dve", lambda e: e.scalar_tensor_tensor(out=HN[d][:], in0=HN[d][:], scalar=GC[d][:, gcol:gcol + 1], in1=hb_, op0=ALU.mult, op1=ALU.add),
                          r=(psb(b1), ("HN", d), GCT), w=(("HN", d),))
                st.append(s_tail)
                return st

            units = [[unit_stages(0, step), unit_stages(1, step)] for step in range(NT)]
            for ch in units[0]:
                ch[0]()
            for step in range(NT):
                for ch in units[step]:
                    ch[1]()
                if step + 1 < NT:
                    for ch in units[step + 1]:
                        ch[0]()
                for si in range(2, len(units[step][0])):
                    for ch in units[step]:
                        ch[si]()
                chk(140 + step)
            chk(50)
            for q in range(NQ):
                qs = slice(q * 512, (q + 1) * 512)
                bank = q % 4
                tqq = tq[q % 2]
                tqt = ("tq", q % 2)
                P.add("pe", lambda e, qs=qs, bank=bank: e.matmul(PS[bank][:], blk_f[:], yacc[:, qs], start=True, stop=True), r=("yacc",), w=(psb(bank),))
                P.add("dve", lambda e, qs=qs, bank=bank: e.scalar_tensor_tensor(out=tA[:, qs], in0=PS[bank][:], scalar=-1.0 / 64, in1=yacc[:, qs], op0=ALU.mult, op1=ALU.add),
                      r=(psb(bank), "yacc"), w=("tA",))
                P.add("pool", lambda e, qs=qs, tqq=tqq: e.tensor_tensor(out=tqq[:], in0=tA[:, qs], in1=tA[:, qs], op=ALU.mult), r=("tA",), w=(tqt,))
                P.add("pe", lambda e, tqq=tqq, bank=bank: e.matmul(PS[bank][:], blk_f[:], tqq[:], start=True, stop=True), r=(tqt,), w=(psb(bank),))
                P.add("act", lambda e, tqq=tqq, bank=bank: e.activation(out=tqq[:], in_=PS[bank][:], func=AF.Sqrt, bias=epsc[:, 1:2], scale=1.0 / 64),
                      r=(psb(bank),), w=(tqt,))
                P.add("dve", lambda e, tqq=tqq: e.reciprocal(out=tqq[:], in_=tqq[:]), r=(tqt,), w=(tqt,))
                P.add("dve", lambda e, qs=qs, tqq=tqq: e.tensor_tensor(out=tA[:, qs], in0=tA[:, qs], in1=tqq[:], op=ALU.mult), r=("tA", tqt), w=("tA",))
                P.add("act", lambda e, qs=qs: e.activation(out=tA[:, qs], in_=tA[:, qs], func=AF.Identity, bias=par8["lnx_b"][:, hp:hp + 1], scale=par8["lnx_g"][:, hp:hp + 1]),
                      r=("tA",), w=("tA",))
                P.add("pool", lambda e, qs=qs: e.tensor_tensor(out=tA[:, qs], in0=tA[:, qs], in1=tB[:, qs], op=ALU.add), r=("tA", "tB"), w=("tA",))
                gbank = 4 + q % 4
                P.add("pe", lambda e, qs=qs, gbank=gbank: e.matmul(PS[gbank][:], g2a[:, hc], lo_g[:, qs], start=True, stop=False), r=LWW + ("lo_g",), w=(psb(gbank),))
                P.add("pe", lambda e, qs=qs, gbank=gbank: e.matmul(PS[gbank][:], a2g[64:96, hc], lo_a[64:96, qs], start=False, stop=True), r=LWW + ("lo_a",), w=(psb(gbank),))
                P.add("dve", lambda e, qs=qs, gbank=gbank: e.tensor_tensor(out=mixo[:, qs], in0=PS[gbank][:], in1=tA[:, qs], op=ALU.mult), r=(psb(gbank), "tA", "mixo"), w=("mixo",))
            P.add("sp", lambda e: e.dma_start(out=mixT_d[s, hp], in_=mixo[:]), r=("mixo",), w=(("mixd", hp),), chan="mixst")

        if do_rwkv:
            project(G_WFWB, to_raw)
            shift(lo_w[:], "lo_w", om_c[:, 24:25], mue_c[:, 24:25], muo_c[:, 24:25], post=AF.Tanh)
            project(G_AG1, to_raw)
            shift(lo_a[0:64, :], "lo_a", muag[:, 1:2], muag[:, 2:3], muag[:, 3:4], np_=64, p0=0, post=AF.Copy)
            shift(lo_a[64:96, :], "lo_a", muag[:, 1:2], muag[:, 2:3], muag[:, 3:4], np_=32, p0=64, post=AF.Sigmoid)
            project(G_G0, to_raw)
            shift(lo_g[:], "lo_g", mug0[:, 1:2], mug0[:, 2:3], mug0[:, 3:4], post=AF.Sigmoid)
            chk(20)
            for hp in range(8):
                do_hp(hp)
                chk(51)
        else:
            for hp in range(8):
                P.add("pool", lambda e: e.memset(mixo[:], 0.0), r=("mixo",), w=("mixo",))
                P.add("sp", lambda e, hp=hp: e.dma_start(out=mixT_d[s, hp], in_=mixo[:]), r=("mixo",), w=(("mixd", hp),), chan="mixst")

        def do_conv(cc):
            def cons_C(q, ps, ptok):
                P.add("act", lambda e: e.activation(out=tA[:, q * 512:(q + 1) * 512], in_=ps, func=AF.Copy), r=(ptok,), w=("tA",))
            project(G_CONV + 8 + cc, cons_C)
            P.add("pool", zero_pads, r=("rawp",), w=("rawp",))

            def cons_X(q, ps, ptok):
                P.add("dve", lambda e: e.tensor_tensor(out=rawp[:, 1 + q * 512:1 + (q + 1) * 512], in0=ps, in1=tA[:, q * 512:(q + 1) * 512], op=ALU.mult),
                      r=(ptok, "tA"), w=("rawp",))
            project(G_CONV + 16 + cc, cons_X)

            def cons_B(q, ps, ptok):
                P.add("act", lambda e: e.activation(out=tB[:, q * 512:(q + 1) * 512], in_=ps, func=AF.Copy), r=(ptok,), w=("tB",))
            project(G_CONV + cc, cons_B)
            cw = [par8["cw%d" % j][:, cc:cc + 1] for j in range(3)]
            P.add("act", lambda e: e.activation(out=tA[:], in_=rawp[:, 1:T + 1], func=AF.Identity, scale=cw[1]), r=("rawp", "tA"), w=("tA",))
            P.add("dve", lambda e: e.scalar_tensor_tensor(out=tA[:], in0=rawp[:, 0:T], scalar=cw[0], in1=tA[:], op0=ALU.mult, op1=ALU.add), r=("rawp", "tA"), w=("tA",))
            P.add("dve", lambda e: e.scalar_tensor_tensor(out=tA[:], in0=rawp[:, 2:T + 2], scalar=cw[2], in1=tA[:], op0=ALU.mult, op1=ALU.add), r=("rawp", "tA"), w=("tA",))
            P.add("dve", lambda e: e.tensor_tensor(out=mixo[:], in0=tA[:], in1=tB[:], op=ALU.mult), r=("tA", "tB", "mixo"), w=("mixo",))
            P.add("sp", lambda e: e.dma_start(out=mixT_d[s, 8 + cc], in_=mixo[:]), r=("mixo",), w=(("mixd", 8 + cc),), chan="mixst")
        for cc in range(8):
            do_conv(cc)

        P.barrier()
        if stage == 6:
            raise StopIteration

        apos[0] = 0
        mx = carve(KC * 512 * 2, BF16, [128, KC, 512])
        xr = carve(4 * D * 4, F32, [128, 4, D])
        wo = [carve(8 * 512 * 2, BF16, [128, 8, 512]) for _ in range(2)]
        xn2 = carve(D * 4, F32, [128, D])
        junk2 = carve(D * 2, BF16, [128, D])
        NWGU = 3
        wgu = [[carve(KC * 128 * 2, BF16, [128, KC, 128]) for _ in range(NWGU)] for _ in range(2)]
        actT = carve(FC * 512 * 2, BF16, [128, FC, 512])
        wdt = [carve(11 * 512 * 2, BF16, [128, 11, 512]) for _ in range(2)]
        sgt = [carve(512 * 4, F32, [128, 512]) for _ in range(2)]
        dgm = carve(128 * 4, F32, [128, 128])
        nfb = carve(D * 4, F32, [128, D])
        gtb = carve(D * 4, F32, [128, D])
        P.add("sp", lambda e: e.dma_start(out=nfb[:], in_=norm_f_g.partition_broadcast(128)), w=("nfb",), chan="nfb")

        def build_gtb(chunk0):
            for kq in range(4):
                bank = 4 + kq % 2
                for k4 in range(4):
                    kc = kq * 4 + k4
                    P.add("dve", lambda e, kc=kc: e.tensor_scalar(out=dgm[:], in0=id_f[:], scalar1=modT[:, s, chunk0 + kc:chunk0 + kc + 1], scalar2=None, op0=ALU.mult),
                          r=("dgm",), w=("dgm",))
                    P.add("pe", lambda e, k4=k4, bank=bank: e.matmul(PS[bank][:, k4 * 128:(k4 + 1) * 128], ones_f[:], dgm[:], start=True, stop=True), r=("dgm",), w=(psb(bank),))
                P.add("act", lambda e, kq=kq, bank=bank: e.activation(out=gtb[:, kq * 512:(kq + 1) * 512], in_=PS[bank][:], func=AF.Copy), r=(psb(bank),), w=("gtb",))

        wo_i = {"i": 0}
        wgu_i = {"i": 0}
        wd_i = {"i": 0}
        MXT = tuple(("mx", kc) for kc in range(KC))

        def do_tile(tt):
            tsl = slice(tt * 512, (tt + 1) * 512)
            for kc in range(KC):
                P.add("sp", lambda e, kc=kc: e.dma_start(out=mx[:, kc, :], in_=mixT_d[s, kc, :, tsl]), r=(("mixd", kc),), w=(("mx", kc),), chan="mxl")
            for j in range(4):
                P.add("sp", lambda e, j=j: e.dma_start(out=xr[:, j, :], in_=xs[s, tt * 512 + j * 128:tt * 512 + (j + 1) * 128, :]), w=(("xr", j),), chan="xrl")
            if stage == 70:
                raise StopIteration
            build_gtb(32)
            if stage == 71:
                raise StopIteration
            for dg in range(4):
                dsl = slice(dg * 512, (dg + 1) * 512)
                for half in range(2):
                    slot = wo_i["i"] % 2
                    wo_i["i"] += 1
                    P.add("sp", lambda e, slot=slot, dg=dg, half=half: e.dma_start(
                        out=wo[slot][:].rearrange("p a b -> p (a b)"), in_=wout_b[dg][:, half * 8 * 512:(half + 1) * 8 * 512]),
                        r=(("scr", "wout", dg, half),), w=(("wo", slot),), chan="wo%d" % slot)
                    for k8 in range(8):
                        kc = half * 8 + k8
                        for j in range(4):
                            P.add("pe", lambda e, j=j, k8=k8, kc=kc, slot=slot: e.matmul(PS[j][:], mx[:, kc, j * 128:(j + 1) * 128], wo[slot][:, k8, :],
                                                                                       start=(kc == 0), stop=(kc == KC - 1)),
                                  r=(("mx", kc), ("wo", slot)), w=(psb(j),))
                for j in range(4):
                    P.add("dve", lambda e, j=j, dsl=dsl: e.tensor_tensor(out=sgt[j % 2][:], in0=PS[j][:], in1=gtb[:, dsl], op=ALU.mult), r=(psb(j), "gtb"), w=(("sgt", j % 2),))
                    P.add("pool", lambda e, j=j, dsl=dsl: e.tensor_tensor(out=xr[:, j, dsl], in0=xr[:, j, dsl], in1=sgt[j % 2][:], op=ALU.add), r=(("sgt", j % 2), ("xr", j)), w=(("xr", j),))
            if stage == 7:
                raise StopIteration
            for j in range(4):
                norm_transpose(xr[:, j, :], ("xr", j), (s1c if stage == 84 else s2c)[:, s, :], modT[:, s, 0:16] if stage == 84 else modT[:, s, 48:64], mx, slice(j * 128, (j + 1) * 128), (xn2[:], "xn2", junk2[:]),
                               lambda kc: ("mx", kc), bank0=6)
            if stage in (8, 81, 82, 83, 84):
                raise StopIteration
            for fb in range(FC):
                slot = wgu_i["i"] % NWGU
                wgu_i["i"] += 1
                P.add("sp", lambda e, fb=fb, slot=slot: e.dma_start(out=wgu[0][slot][:].rearrange("p a b -> p (a b)"), in_=wg_b[fb]),
                      r=(("scr", "wg", fb),), w=(("wgu", 0, slot),), chan="wgu%d" % slot)
                P.add("sp", lambda e, fb=fb, slot=slot: e.dma_start(out=wgu[1][slot][:].rearrange("p a b -> p (a b)"), in_=wu_b[fb]),
                      r=(("scr", "wu", fb),), w=(("wgu", 1, slot),), chan="wgu%d" % slot)
                gb = 4 + 2 * (fb % 2)
                for w_ in range(2):
                    for kc in range(KC):
                        P.add("pe", lambda e, w_=w_, kc=kc, slot=slot, gb=gb: e.matmul(PS[gb + w_][:], wgu[w_][slot][:, kc, :], mx[:, kc, :], start=(kc == 0), stop=(kc == KC - 1)),
                              r=(("wgu", w_, slot), ("mx", kc)), w=(psb(gb + w_),))
                P.add("act", lambda e, fb=fb, gb=gb: e.activation(out=sgt[fb % 2][:], in_=PS[gb][:], func=AF.Silu), r=(psb(gb),), w=(("sgt", fb % 2),))
                P.add("dve", lambda e, fb=fb, gb=gb: e.tensor_tensor(out=actT[:, fb, :], in0=PS[gb + 1][:], in1=sgt[fb % 2][:], op=ALU.mult),
                      r=(psb(gb + 1), ("sgt", fb % 2)), w=(("actT", fb),))
            if stage == 9:
                raise StopIteration
            build_gtb(80)
            for dg in range(4):
                dsl = slice(dg * 512, (dg + 1) * 512)
                for fq in range(4):
                    slot = wd_i["i"] % 2
                    wd_i["i"] += 1
                    P.add("sp", lambda e, slot=slot, dg=dg, fq=fq: e.dma_start(
                        out=wdt[slot][:].rearrange("p a b -> p (a b)"), in_=wd_b[dg, fq]),
                        r=(("scr", "wd", dg, fq),), w=(("wdt", slot),), chan="wd%d" % slot)
                    for f11 in range(11):
                        fc = fq * 11 + f11
                        for j in range(4):
                            P.add("pe", lambda e, j=j, f11=f11, fc=fc, slot=slot: e.matmul(PS[j][:], actT[:, fc, j * 128:(j + 1) * 128], wdt[slot][:, f11, :],
                                                                                         start=(fc == 0), stop=(fc == FC - 1)),
                                  r=(("actT", fc), ("wdt", slot)), w=(psb(j),))
                for j in range(4):
                    P.add("dve", lambda e, j=j, dsl=dsl: e.tensor_tensor(out=sgt[j % 2][:], in0=PS[j][:], in1=gtb[:, dsl], op=ALU.mult), r=(psb(j), "gtb"), w=(("sgt", j % 2),))
                    P.add("pool", lambda e, j=j, dsl=dsl: e.tensor_tensor(out=xr[:, j, dsl], in0=xr[:, j, dsl], in1=sgt[j % 2][:], op=ALU.add), r=(("sgt", j % 2), ("xr", j)), w=(("xr", j),))
            if stage == 10:
                raise StopIteration
            for j in range(4):
                P.add("act", lambda e, j=j: e.activation(out=junk2[:], in_=xr[:, j, :], func=AF.Square, accum_out=stat[:, 2:3]), r=(("xr", j),), w=("stat", "junk"))
                rstd_from_ss(stat[:, 2:3], stat[:, 3:4], 1.0 / D, 0)
                P.add("dve", lambda e, j=j: e.scalar_tensor_tensor(out=xr[:, j, :], in0=xr[:, j, :], scalar=stat[:, 3:4], in1=nfb[:], op0=ALU.mult, op1=ALU.mult),
                      r=(("xr", j), "stat", "nfb"), w=(("xr", j),))
                P.add("sp", lambda e, j=j: e.dma_start(out=ys[s, tt * 512 + j * 128:tt * 512 + (j + 1) * 128, :], in_=xr[:, j, :]), r=(("xr", j),), w=(("ysd", j),), chan="yst")
        for tt in range(NQ):
            do_tile(tt)
        P.barrier()

    try:
        for s in range(NSEQ):
            do_seq(s)
    except StopIteration:
        pass

    P.finish()
    P.emit(nc, es)
    es.close()
    return nc


_W_NAMES = ["w_ada", "b_ada", "norm1_g", "w_in", "mu_shift", "w0_decay", "w2_decay", "a0", "a2", "g2_gate", "k_k", "k_a", "r_k",
            "lnx_g", "lnx_b", "conv_w", "w_out", "norm2_g", "w_ffn_gate", "w_ffn_up", "w_ffn_down"]


def _weights_map(inputs):
    m = {}
    for nm in _W_NAMES:
        a = np.asarray(inputs[nm], dtype=np.float32)
        a = a[0]
        if nm == "r_k":
            a = a.reshape(-1)
        m[nm] = np.ascontiguousarray(a)
    m["norm_f_g"] = np.ascontiguousarray(np.asarray(inputs["norm_f_g"], dtype=np.float32))
    return m


def kernel(**inputs):
    x_prompt = np.asarray(inputs["x_prompt"], dtype=np.float32)
    x_sample = np.asarray(inputs["x_sample"], dtype=np.float32)
    c_prompt = np.asarray(inputs["c_prompt"], dtype=np.float32)
    c_sample = np.asarray(inputs["c_sample"], dtype=np.float32)
    NB, T = x_prompt.shape[0], x_prompt.shape[1]
    NS = x_sample.shape[0]
    ntot = NB + NS
    NSEQ = 3
    ncore = 8

    def getx(i):
        return x_prompt[i] if i < NB else x_sample[i - NB]

    def getc(i):
        return c_prompt[i] if i < NB else c_sample[i - NB]

    wm = _weights_map(inputs)
    in_maps = []
    assign = []
    for c in range(ncore):
        ids = [c, c + 8, c + 16 if c + 16 < ntot else c]
        assign.append(ids)
        m = dict(wm)
        m["xs"] = np.stack([getx(i) for i in ids])
        m["cs"] = np.stack([getc(i) for i in ids])
        in_maps.append(m)
    nc = build_program(NSEQ, T)
    res = run_bass_kernel_spmd(nc, in_maps, core_ids=list(range(ncore)))
    y_p = np.empty_like(x_prompt)
    y_s = np.empty_like(x_sample)
    for c in range(ncore):
        ysc = res.results[c]["ys"]
        for slot, i in enumerate(assign[c]):
            if slot == 2 and c + 16 >= ntot:
                continue
            if i < NB:
                y_p[i] = ysc[slot]
            else:
                y_s[i - NB] = ysc[slot]
    return (y_p, y_s)
```

```python
import math
from contextlib import ExitStack
import numpy as np
import concourse.bass as bass
import concourse.mybir as mybir
from concourse.bass_utils import run_bass_kernel_spmd

F32 = mybir.dt.float32
BF16 = mybir.dt.bfloat16
AF = mybir.ActivationFunctionType
ALU = mybir.AluOpType
AX = mybir.AxisListType

D = 2048
KC = 16
RW = 1024
RWC = 3424
IN_COLS = 6496
DFF = 5632
FC = 44
NMOD = 6
RMS_EPS = 1e-6
LNX_EPS = 64e-5
C0 = -math.exp(-0.5)

GROUPS = []
for i in range(8):
    GROUPS.append([(i * 128, 128, 0)])
for i in range(8):
    GROUPS.append([(1024 + i * 128, 128, 0)])
for i in range(8):
    GROUPS.append([(2048 + i * 128, 128, 0)])
G_WFWB = len(GROUPS); GROUPS.append([(3072, 128, 0)])
G_AG1 = len(GROUPS); GROUPS.append([(3200, 64, 0), (3392, 32, 64), (3392, 32, 96)])
G_G0 = len(GROUPS); GROUPS.append([(3264, 128, 0)])
G_CONV = len(GROUPS)
for j in range(3):
    for i in range(8):
        GROUPS.append([(RWC + j * 1024 + i * 128, 128, 0)])
NG = len(GROUPS)


class Prog:
    ENGS = ("pe", "act", "dve", "pool", "sp")

    def __init__(self):
        self.ops = []
        self.last_w = {}
        self.readers = {}
        self.chan_cnt = {}

    def add(self, eng, fn, r=(), w=(), chan=None):
        i = len(self.ops)
        deps = set()
        for t in r:
            lw = self.last_w.get(t)
            if lw is not None:
                deps.add(lw)
        for t in w:
            lw = self.last_w.get(t)
            if lw is not None:
                deps.add(lw)
            for rd in self.readers.get(t, ()):
                deps.add(rd)
        for t in r:
            self.readers.setdefault(t, []).append(i)
        for t in w:
            self.last_w[t] = i
            self.readers[t] = []
        cval = None
        if chan is not None:
            self.chan_cnt[chan] = self.chan_cnt.get(chan, 0) + 1
            cval = 16 * self.chan_cnt[chan]
        import sys as _s
        fr = _s._getframe(1)
        self.ops.append(dict(eng=eng, fn=fn, deps=deps, chan=chan, cval=cval, sig=False, seq=0, tag=fr.f_lineno))
        return i

    def barrier(self):
        last = {}
        for i, op in enumerate(self.ops):
            if op["fn"] is None:
                continue
            if op["chan"] is not None:
                last[("c", op["chan"])] = i
            else:
                last[("e", op["eng"])] = i
        deps = set(last.values())
        for e in self.ENGS:
            self.ops.append(dict(eng=e, fn=None, deps=set(deps), chan=None, cval=None, sig=False, seq=0))
        self.last_w = {}
        self.readers = {}

    def finish(self):
        self.barrier()

    def simulate(self):
        import bisect
        ops = self.ops
        for op in ops:
            for d in op["deps"]:
                if ops[d]["chan"] is None:
                    ops[d]["sig"] = True
        cnt = {e: 0 for e in self.ENGS}
        for op in ops:
            if op["chan"] is None and op["sig"] and op["fn"] is not None:
                cnt[op["eng"]] += 1
                op["seq"] = cnt[op["eng"]]
        chan_ops = {}
        for i, op in enumerate(ops):
            op["idx"] = i
            if op["chan"] is not None:
                chan_ops.setdefault(op["chan"], []).append(i)
        per = {e: [op for op in ops if op["eng"] == e] for e in self.ENGS}
        pos = {e: 0 for e in self.ENGS}
        sem = {}
        progress = True
        while progress:
            progress = False
            for e in self.ENGS:
                while pos[e] < len(per[e]):
                    op = per[e][pos[e]]
                    ok = True
                    for d in op["deps"]:
                        dop = ops[d]
                        if dop["chan"] is not None:
                            key = ("c", dop["chan"])
                            if str(dop["chan"]).startswith("pcs"):
                                val = dop["cval"]
                            else:
                                val = 16 * bisect.bisect_left(chan_ops[dop["chan"]], op["idx"])
                        else:
                            if dop["eng"] == e and e == "pe":
                                continue
                            if dop["fn"] is None:
                                print("DEP ON NONE OP", op["idx"], d)
                            key = ("e", dop["eng"]); val = dop["seq"]
                        if sem.get(key, 0) < val:
                            ok = False
                            blk = (key, val, sem.get(key, 0), d)
                            break
                    if not ok:
                        op["blk"] = blk
                        break
                    if op["fn"] is not None:
                        if op["chan"] is not None:
                            sem[("c", op["chan"])] = sem.get(("c", op["chan"]), 0) + 16
                        elif op["sig"]:
                            sem[("e", e)] = sem.get(("e", e), 0) + 1
                    pos[e] += 1
                    progress = True
        stuck = {e: (pos[e], len(per[e])) for e in self.ENGS if pos[e] < len(per[e])}
        if stuck:
            print("DEADLOCK", stuck)
            for e in stuck:
                op = per[e][pos[e]]
                print(e, "op idx", op["idx"], "blocked on", op.get("blk"), "tag", op.get("tag"))
        else:
            print("simulate: no deadlock;", {e: len(per[e]) for e in self.ENGS})
        return not stuck

    def emit(self, nc, es):
        import os
        if os.environ.get("KSIM"):
            self.simulate()
        ops = self.ops
        for op in ops:
            for d in op["deps"]:
                if ops[d]["chan"] is None and not (ops[d]["eng"] == "pe" and op["eng"] == "pe"):
                    ops[d]["sig"] = True
        cnt = {e: 0 for e in self.ENGS}
        for op in ops:
            if op["chan"] is None and op["sig"]:
                cnt[op["eng"]] += 1
                op["seq"] = cnt[op["eng"]]
        sems = {}
        for e in self.ENGS:
            sems[("e", e)] = es.enter_context(nc.semaphore("sem_" + e))
        for c in self.chan_cnt:
            sems[("c", c)] = es.enter_context(nc.semaphore("ch_" + str(c)))
        block = es.enter_context(nc.Block())
        for i, op in enumerate(ops):
            op["idx"] = i
        per = {e: [op for op in ops if op["eng"] == e] for e in self.ENGS}
        import bisect
        chan_ops = {}
        for i, op in enumerate(ops):
            if op["chan"] is not None:
                chan_ops.setdefault(op["chan"], []).append(i)

        def run(eng_name, e):
            waited = {}
            for op in per[eng_name]:
                need = {}
                for d in op["deps"]:
                    dop = ops[d]
                    if dop["chan"] is not None:
                        key = ("c", dop["chan"])
                        if str(dop["chan"]).startswith("pcs"):
                            val = dop["cval"]
                        else:
                            lst = chan_ops[dop["chan"]]
                            k = bisect.bisect_left(lst, op["idx"])
                            val = 16 * k
                    else:
                        if dop["eng"] == eng_name and eng_name == "pe":
                            continue
                        key = ("e", dop["eng"]); val = dop["seq"]
                    if need.get(key, 0) < val:
                        need[key] = val
                for key, val in need.items():
                    if waited.get(key, 0) >= val:
                        continue
                    e.wait_ge(sems[key], val)
                    waited[key] = val
                if op["fn"] is None:
                    continue
                ins = op["fn"](e)
                if op["chan"] is not None:
                    ins.then_inc(sems[("c", op["chan"])], 16)
                elif op["sig"]:
                    ins.then_inc(sems[("e", eng_name)], 1)

        @block.tensor
        def _(e):
            run("pe", e)

        @block.scalar
        def _(e):
            run("act", e)

        @block.vector
        def _(e):
            run("dve", e)

        @block.gpsimd
        def _(e):
            run("pool", e)

        @block.sync
        def _(e):
            run("sp", e)


def build_program(NSEQ, T, do_rwkv=True, stage=99):
    NT = T // 128
    NQ = T // 512
    assert T % 512 == 0
    nc = bass.Bass("TRN2", target_bir_lowering=False)
    P = Prog()
    es = ExitStack()

    def din(name, shape):
        return nc.dram_tensor(name, list(shape), F32, kind="ExternalInput").ap()

    xs = din("xs", [NSEQ, T, D])
    cs = din("cs", [NSEQ, D])
    w_ada = din("w_ada", [D, NMOD * D])
    b_ada = din("b_ada", [NMOD * D])
    norm1_g = din("norm1_g", [D])
    w_in = din("w_in", [D, IN_COLS])
    mu_shift = din("mu_shift", [RWC])
    w0_decay = din("w0_decay", [2, RW])
    w2_decay = din("w2_decay", [2, 64, RW])
    a0 = din("a0", [RW])
    a2 = din("a2", [64, RW])
    g2_gate = din("g2_gate", [160, RW])
    k_k = din("k_k", [RW])
    k_a = din("k_a", [RW])
    r_k = din("r_k", [RW])
    lnx_g = din("lnx_g", [RW])
    lnx_b = din("lnx_b", [RW])
    conv_w = din("conv_w", [3, RW])
    w_out = din("w_out", [D, D])
    norm2_g = din("norm2_g", [D])
    w_ffn_gate = din("w_ffn_gate", [D, DFF])
    w_ffn_up = din("w_ffn_up", [D, DFF])
    w_ffn_down = din("w_ffn_down", [DFF, D])
    norm_f_g = din("norm_f_g", [D])
    ys = nc.dram_tensor("ys", [NSEQ, T, D], F32, kind="ExternalOutput").ap()

    wd_b = nc.dram_tensor("wd_b", [4, 4, 128, 11 * 512], BF16).ap()
    win_b = nc.dram_tensor("win_b", [NG, 128, KC * 128], BF16).ap()
    wg_b = nc.dram_tensor("wg_b", [FC, 128, KC * 128], BF16).ap()
    wu_b = nc.dram_tensor("wu_b", [FC, 128, KC * 128], BF16).ap()
    wout_b = nc.dram_tensor("wout_b", [4, 128, KC * 512], BF16).ap()
    mixT_d = nc.dram_tensor("mixT_d", [NSEQ, 16, 128, T], BF16).ap()

    def sb(name, shape, dt):
        return es.enter_context(nc.sbuf_tensor(name, list(shape), dt))

    id_f = sb("id_f", [128, 128], F32)
    id_b = sb("id_b", [128, 128], BF16)
    ones_f = sb("ones_f", [128, 128], F32)
    blk_f = sb("blk_f", [128, 128], F32)
    MA = [sb("MA%d" % d, [128, 512], F32) for d in range(2)]
    MX = [sb("MX%d" % d, [128, 256], F32) for d in range(2)]
    TRI = [sb("TRI%d" % d, [128, 256], F32) for d in range(2)]
    n1g = sb("n1g", [128, 16], F32)
    n2g = sb("n2g", [128, 16], F32)
    bada = sb("bada", [128, 96], F32)
    modT = sb("modT", [128, NSEQ, 96], F32)
    s1c = sb("s1c", [128, NSEQ, 16], F32)
    s2c = sb("s2c", [128, NSEQ, 16], F32)
    mu_c = sb("mu_c", [128, 27], F32)
    om_c = sb("om_c", [128, 27], F32)
    mue_c = sb("mue_c", [128, 27], F32)
    muo_c = sb("muo_c", [128, 27], F32)
    muag = sb("muag", [128, 4], F32)
    par8 = {nm: sb("p_" + nm, [128, 8], F32) for nm in ("a0", "k_k", "k_a", "r_k", "lnx_g", "lnx_b", "omka", "cw0", "cw1", "cw2")}
    evn = sb("evn", [128, 1], F32)
    mug0 = sb("mug0", [128, 4], F32)
    odd = sb("odd", [128, 1], F32)
    stat = sb("stat", [128, 8], F32)
    epsc = sb("epsc", [128, 2], F32)

    ARENA = 49800
    arena = sb("arena", [128, ARENA], F32)
    apos = [0]

    def carve(nbytes, dt, shape):
        n32 = (nbytes + 3) // 4
        n32 = (n32 + 7) // 8 * 8
        a = arena[:, apos[0]:apos[0] + n32]
        apos[0] += n32
        assert apos[0] <= ARENA, (apos[0], ARENA)
        if dt == BF16:
            a = a.bitcast(BF16)
        v = a
        if len(shape) == 3:
            v = a[:, 0:shape[1] * shape[2]].rearrange("p (a b) -> p a b", b=shape[2])
        elif len(shape) == 2:
            v = a[:, 0:shape[1]]
        return v

    PS = [es.enter_context(nc.psum_tensor("ps%d" % i, [128, 512], F32)) for i in range(8)]

    def psb(i):
        return ("ps", i)

    def pool_op(fn, r=(), w=()):
        return P.add("pool", fn, r, w)

    def mk_mask(ap, kind, tok="const"):
        mt = ("mask", id(ap), kind, len(P.ops))
        pool_op(lambda e: e.memset(ap, 1.0), w=(mt,))

        def f(e):
            if kind == "SL":
                return e.affine_select(out=ap, in_=ap, pattern=[[-1, 128]], compare_op=ALU.is_gt, fill=0.0, base=0, channel_multiplier=1)
            if kind == "SU":
                return e.affine_select(out=ap, in_=ap, pattern=[[1, 128]], compare_op=ALU.is_gt, fill=0.0, base=0, channel_multiplier=-1)
            if kind == "IU":
                return e.affine_select(out=ap, in_=ap, pattern=[[1, 128]], compare_op=ALU.is_ge, fill=0.0, base=0, channel_multiplier=-1)
            if kind == "IL":
                return e.affine_select(out=ap, in_=ap, pattern=[[-1, 128]], compare_op=ALU.is_ge, fill=0.0, base=0, channel_multiplier=1)
        pool_op(f, r=(mt,), w=(mt, tok))

    pool_op(lambda e: e.memset(id_f[:], 0.0), w=("id_f0",))
    pool_op(lambda e: e.affine_select(out=id_f[:], in_=id_f[:], pattern=[[-1, 128]], compare_op=ALU.not_equal, fill=1.0, base=0, channel_multiplier=1),
            r=("id_f0",), w=("id_f0", "const"))

    def f_ident(e):
        e.memset(ones_f[:], 1.0)
        e.memset(epsc[:, 0:1], RMS_EPS)
        return e.memset(epsc[:, 1:2], LNX_EPS)
    pool_op(f_ident, w=("const_b",))
    pool_op(lambda e: e.memset(blk_f[:], 0.0), w=("blk0",))

    def f_blk(e):
        e.memset(blk_f[0:64, 0:64], 1.0)
        return e.memset(blk_f[64:128, 64:128], 1.0)
    pool_op(f_blk, r=("blk0",), w=("blk0", "const_c"))
    pool_op(lambda e: e.tensor_copy(out=id_b[:], in_=id_f[:]), r=("const",), w=("const2",))
    for d, (ks, ki, kx) in enumerate((("SU", "IU", "SL"), ("SL", "IL", "SU"))):
        mk_mask(MA[d][:, 0:128], ks); mk_mask(MA[d][:, 128:256], ks)
        mk_mask(MA[d][:, 256:384], ki); mk_mask(MA[d][:, 384:512], ki)
        mk_mask(MX[d][:, 0:128], kx); mk_mask(MX[d][:, 128:256], kx)
    mk_mask(TRI[0][:, 0:128], "IU", tok=("tri", 0)); mk_mask(TRI[0][:, 128:256], "SU", tok=("tri", 0))
    mk_mask(TRI[1][:, 0:128], "IL", tok=("tri", 1)); mk_mask(TRI[1][:, 128:256], "SL", tok=("tri", 1))
    pool_op(lambda e: e.tensor_scalar(out=TRI[0][0:64, :], in0=TRI[0][0:64, :], scalar1=-1.0, scalar2=None, op0=ALU.add), r=(("tri", 0),), w=(("tri", 0),))
    pool_op(lambda e: e.tensor_scalar(out=TRI[1][64:128, :], in0=TRI[1][64:128, :], scalar1=-1.0, scalar2=None, op0=ALU.add), r=(("tri", 1),), w=(("tri", 1),))
    pool_op(lambda e: e.tensor_scalar(out=TRI[0][:], in0=TRI[0][:], scalar1=C0, scalar2=None, op0=ALU.mult), r=(("tri", 0),), w=(("tri", 0),))
    pool_op(lambda e: e.tensor_scalar(out=TRI[1][:], in0=TRI[1][:], scalar1=C0, scalar2=None, op0=ALU.mult), r=(("tri", 1),), w=(("tri", 1), "const3"))

    def f_par(e):
        idv = id_f[:].rearrange("p (a b) -> p a b", b=2)
        e.tensor_reduce(out=evn[:], in_=idv[:, :, 0], axis=AX.X, op=ALU.add)
        return e.tensor_reduce(out=odd[:], in_=idv[:, :, 1], axis=AX.X, op=ALU.add)
    P.add("dve", f_par, r=("const",), w=("const4",))

    def early(k):
        if stage == k:
            P.finish()
            P.emit(nc, es)
            es.close()
            return True
        return False
    if early(0):
        return nc
    pcn = {"i": 0}
    PCD = 4

    def precast(dst, src, tok, chan):
        k = pcn["i"] % PCD
        pcn["i"] += 1
        P.add("pool", lambda e: e.dma_start(out=dst, in_=src), r=(), w=(tok, ("pcring", k)), chan="pcs%d" % k)

    for g, parts in enumerate(GROUPS):
        dstg = win_b[g].rearrange("p (kc m) -> p kc m", m=128)
        for (c0, wd, d0) in parts:
            src = w_in[:, c0:c0 + wd].rearrange("(kc p) m -> p kc m", p=128)
            precast(dstg[:, :, d0:d0 + wd], src, ("scr", "win", g, d0), "pc_win")

    if early(1):
        return nc
    stg = sb("stg", [128, 128], F32)

    def load_cols(vec, n, dst_cols, tag):
        rows = n // 128
        rem = n - rows * 128
        nr = rows + (1 if rem else 0)
        P.add("pool", lambda e: e.memset(stg[:], 0.0), w=("stg",))
        if rows:
            P.add("sp", lambda e: e.dma_start(out=stg[0:rows, :], in_=vec[0:rows * 128].rearrange("(r c) -> r c", c=128)),
                  w=("stg",), chan="misc")
        if rem:
            P.add("sp", lambda e: e.dma_start(out=stg[rows:rows + 1, 0:rem], in_=vec[rows * 128:n].rearrange("(r c) -> r c", r=1)),
                  w=("stg",), chan="misc")
        P.add("pe", lambda e: e.transpose(PS[0][:, 0:128], stg[:], id_f[:]), r=("stg", "const"), w=(psb(0),))
        P.add("dve", lambda e: e.tensor_copy(out=dst_cols, in_=PS[0][:, 0:nr]), r=(psb(0),), w=("par", tag))

    load_cols(norm1_g, D, n1g[:], "n1g")
    load_cols(norm2_g, D, n2g[:], "n2g")
    load_cols(b_ada, NMOD * D, bada[:], "bada")
    load_cols(mu_shift, RWC, mu_c[:], "mu")
    for nm, v in (("a0", a0), ("k_k", k_k), ("k_a", k_a), ("r_k", r_k), ("lnx_g", lnx_g), ("lnx_b", lnx_b)):
        load_cols(v, RW, par8[nm][:], nm)
    for j in range(3):
        load_cols(conv_w[j], RW, par8["cw%d" % j][:], "cw%d" % j)
    P.add("pool", lambda e: e.memset(muag[:], 0.0), w=("muag",))
    P.add("sp", lambda e: e.dma_start(out=muag[0:64, 0:1], in_=mu_shift[3200:3264].rearrange("(p o) -> p o", o=1), allow_slow_non_contiguous=True),
          w=("muag",), chan="c_muag")
    P.add("sp", lambda e: e.dma_start(out=muag[64:96, 0:1], in_=mu_shift[3392:3424].rearrange("(p o) -> p o", o=1), allow_slow_non_contiguous=True),
          w=("muag",), chan="c_muag")

    P.add("sp", lambda e: e.dma_start(out=mug0[:, 0:1], in_=mu_shift[3264:3392].rearrange("(p o) -> p o", o=1), allow_slow_non_contiguous=True),
          w=("mug0",), chan="c_mug0")

    def f_mu(e):
        e.tensor_scalar(out=mug0[:, 1:2], in0=mug0[:, 0:1], scalar1=-1.0, scalar2=1.0, op0=ALU.mult, op1=ALU.add)
        e.tensor_scalar(out=mug0[:, 2:3], in0=mug0[:, 0:1], scalar1=evn[:, 0:1], scalar2=None, op0=ALU.mult)
        e.tensor_scalar(out=mug0[:, 3:4], in0=mug0[:, 0:1], scalar1=odd[:, 0:1], scalar2=None, op0=ALU.mult)
        e.tensor_scalar(out=om_c[:], in0=mu_c[:], scalar1=-1.0, scalar2=1.0, op0=ALU.mult, op1=ALU.add)
        e.tensor_scalar(out=mue_c[:], in0=mu_c[:], scalar1=evn[:, 0:1], scalar2=None, op0=ALU.mult)
        e.tensor_scalar(out=muo_c[:], in0=mu_c[:], scalar1=odd[:, 0:1], scalar2=None, op0=ALU.mult)
        e.tensor_scalar(out=muag[:, 1:2], in0=muag[:, 0:1], scalar1=-1.0, scalar2=1.0, op0=ALU.mult, op1=ALU.add)
        e.tensor_scalar(out=muag[:, 2:3], in0=muag[:, 0:1], scalar1=evn[:, 0:1], scalar2=None, op0=ALU.mult)
        e.tensor_scalar(out=muag[:, 3:4], in0=muag[:, 0:1], scalar1=odd[:, 0:1], scalar2=None, op0=ALU.mult)
        return e.tensor_scalar(out=par8["omka"][:], in0=par8["k_a"][:], scalar1=-1.0, scalar2=1.0, op0=ALU.mult, op1=ALU.add)
    P.add("dve", f_mu, r=(("par", "mu"), ("par", "k_a"), "muag", "mug0", "const3", "const4"), w=("mud",))

    if early(2):
        return nc
    apos[0] = 0
    cT = carve(KC * NSEQ * 2, BF16, [128, KC, NSEQ])
    crow = carve(D * 4, F32, [128, D])
    WA_G = 512
    wa = [carve(KC * WA_G * 2, BF16, [128, KC, WA_G]) for _ in range(2)]
    P.add("sp", lambda e: e.dma_start(out=crow[0:NSEQ, :], in_=cs), w=("crow",), chan="c_crow")
    for kc in range(KC):
        P.add("pe", lambda e, kc=kc: e.transpose(PS[1][:, kc * NSEQ:(kc + 1) * NSEQ], crow[0:NSEQ, kc * 128:(kc + 1) * 128], id_f[0:NSEQ, 0:NSEQ]),
              r=("crow", "const"), w=(psb(1),))
    P.add("act", lambda e: e.activation(out=cT[:].rearrange("p a b -> p (a b)"), in_=PS[1][:, 0:KC * NSEQ], func=AF.Silu), r=(psb(1),), w=("cT",))
    NWA = NMOD * D // WA_G
    for gi in range(NWA):
        slot = gi % 2
        P.add("pool", lambda e, gi=gi, slot=slot: e.dma_start(
            out=wa[slot][:], in_=w_ada[:, gi * WA_G:(gi + 1) * WA_G].rearrange("(kc p) m -> p kc m", p=128)),
            w=(("wa", slot),), chan="wa%d" % slot)
        bank = 2 + (gi % 2)
        for jj in range(WA_G // 128):
            for kc in range(KC):
                P.add("pe", lambda e, jj=jj, kc=kc, slot=slot, bank=bank: e.matmul(
                    PS[bank][:, jj * NSEQ:(jj + 1) * NSEQ], wa[slot][:, kc, jj * 128:(jj + 1) * 128], cT[:, kc, :],
                    start=(kc == 0), stop=(kc == KC - 1)), r=(("wa", slot), "cT"), w=(psb(bank),))
        nj = WA_G // 128
        for s in range(NSEQ):
            P.add("dve", lambda e, gi=gi, s=s, bank=bank, nj=nj: e.tensor_tensor(
                out=modT[:, s, gi * nj:(gi + 1) * nj],
                in0=PS[bank][:, 0:nj * NSEQ].rearrange("p (j s) -> p j s", s=NSEQ)[:, :, s],
                in1=bada[:, gi * nj:(gi + 1) * nj], op=ALU.add), r=(psb(bank), ("par", "bada")), w=("modT",))

    def f_s12(e):
        for s in range(NSEQ):
            e.scalar_tensor_tensor(out=s1c[:, s, :], in0=modT[:, s, 16:32], scalar=1.0, in1=n1g[:], op0=ALU.add, op1=ALU.mult)
            r_ = e.scalar_tensor_tensor(out=s2c[:, s, :], in0=modT[:, s, 64:80], scalar=1.0, in1=n2g[:], op0=ALU.add, op1=ALU.mult)
        return r_
    P.add("dve", f_s12, r=("modT", ("par", "n1g"), ("par", "n2g")), w=("s12",))

    if early(3):
        return nc
    for dg in range(4):
        dst = wout_b[dg].rearrange("p (kc n) -> p kc n", n=512)
        for half in range(2):
            src = w_out[half * 1024:(half + 1) * 1024, dg * 512:(dg + 1) * 512].rearrange("(kc p) n -> p kc n", p=128)
            precast(dst[:, half * 8:(half + 1) * 8, :], src, ("scr", "wout", dg, half), "pc_wout")
    if early(31):
        return nc
    import os as _os
    for fb in range(0 if _os.environ.get("SKIPGU") is None else FC, FC):
        for (wsrc, wdst, nm) in ((w_ffn_gate, wg_b, "wg"), (w_ffn_up, wu_b, "wu")):
            src = wsrc[:, fb * 128:(fb + 1) * 128].rearrange("(kc p) m -> p kc m", p=128)
            precast(wdst[fb].rearrange("p (kc m) -> p kc m", m=128), src, ("scr", nm, fb), "pc_" + nm)
    if early(32):
        return nc
    for dg in range(4):
        for fq in range(4):
            src = w_ffn_down[fq * 1408:(fq + 1) * 1408, dg * 512:(dg + 1) * 512].rearrange("(fc p) n -> p fc n", p=128)
            precast(wd_b[dg, fq].rearrange("p (fc n) -> p fc n", n=512), src, ("scr", "wd", dg, fq), "pc_wd")

    P.barrier()
    if early(4):
        return nc

    rr = {"i": 0}

    def evac_eng():
        import os
        if os.environ.get("EVAC"):
            return os.environ["EVAC"]
        return "dve"

    def rstd_from_ss(ss_col, out_col, scale, eps_idx):
        P.add("act", lambda e: e.activation(out=out_col, in_=ss_col, func=AF.Sqrt, bias=epsc[:, eps_idx:eps_idx + 1], scale=scale),
              r=("stat",), w=("stat",))
        P.add("dve", lambda e: e.reciprocal(out=out_col, in_=out_col), r=("stat",), w=("stat",))

    def norm_transpose(src_row, rtok, s_cols, sh_cols, dstT, dtoks, ttok, w_tok_fn, bank0=4):
        xn, xn_tok, junk = ttok
        P.add("act", lambda e: e.activation(out=junk, in_=src_row, func=AF.Square, accum_out=stat[:, 0:1]), r=(rtok,), w=("stat", "junk"))
        rstd_from_ss(stat[:, 0:1], stat[:, 1:2], 1.0 / D, 0)
        if stage == 81 and bank0 == 6:
            return
        P.add("dve", lambda e: e.tensor_scalar(out=xn, in0=src_row, scalar1=stat[:, 1:2], scalar2=None, op0=ALU.mult), r=(rtok, "stat"), w=(xn_tok,))
        if stage == 82 and bank0 == 6:
            return
        for kq in range(4):
            bank = bank0 + (kq % 2)
            for k4 in range(4):
                kc = kq * 4 + k4
                P.add("pe", lambda e, kc=kc, k4=k4, bank=bank: e.transpose(PS[bank][:, k4 * 128:(k4 + 1) * 128], xn[:, kc * 128:(kc + 1) * 128], id_f[:]),
                      r=(xn_tok,), w=(psb(bank),))
            if stage == 83 and bank0 == 6:
                continue
            for k4 in range(4):
                kc = kq * 4 + k4
                eng = evac_eng()
                if eng == "dve":
                    P.add("dve", lambda e, kc=kc, k4=k4, bank=bank: e.tensor_scalar(
                        out=dstT[:, kc, dtoks], in0=PS[bank][:, k4 * 128:(k4 + 1) * 128], scalar1=s_cols[:, kc:kc + 1], scalar2=sh_cols[:, kc:kc + 1],
                        op0=ALU.mult, op1=ALU.add), r=(psb(bank),), w=(w_tok_fn(kc),))
                else:
                    P.add("act", lambda e, kc=kc, k4=k4, bank=bank: e.activation(
                        out=dstT[:, kc, dtoks], in_=PS[bank][:, k4 * 128:(k4 + 1) * 128], func=AF.Identity,
                        bias=sh_cols[:, kc:kc + 1], scale=s_cols[:, kc:kc + 1]), r=(psb(bank),), w=(w_tok_fn(kc),))

    def chk(n):
        if stage == n:
            raise StopIteration

    import os as _os2
    for _i in range(int(_os2.environ.get("PEPAD", "0"))):
        P.add("pe", lambda e: e.matmul(PS[7][:, 0:128], id_b[:], id_b[:], start=True, stop=True), w=(psb(7),))

    def do_seq(s):
        apos[0] = 0
        hn1T = carve(KC * T * 2, BF16, [128, KC, T])
        a_mark = apos[0]
        xrow = [carve(D * 4, F32, [128, D]) for _ in range(2)]
        xn = carve(D * 4, F32, [128, D])
        junk = carve(D * 2, BF16, [128, D])
        for ti in range(NT):
            slot = ti % 2
            P.add("sp", lambda e, ti=ti, slot=slot: e.dma_start(out=xrow[slot][:], in_=xs[s, ti * 128:(ti + 1) * 128, :]),
                  w=(("xrow", slot),), chan="xr%d" % slot)
            norm_transpose(xrow[slot][:], ("xrow", slot), s1c[:, s, :], modT[:, s, 0:16], hn1T, slice(ti * 128, (ti + 1) * 128),
                           (xn[:], "xn", junk[:]), lambda kc, ti=ti: ("hn1T", ti // 4))
        P.barrier()
        if stage == 5:
            raise StopIteration

        apos[0] = a_mark
        rawp = carve((T + 2) * 4, F32, [128, T + 2])
        tA = carve(T * 4, F32, [128, T])
        tB = carve(T * 4, F32, [128, T])
        yacc = carve(T * 4, F32, [128, T])
        r_bf = carve(T * 2, BF16, [128, T])
        k_bf = carve(T * 2, BF16, [128, T])
        v_bf = carve(T * 2, BF16, [128, T])
        ka_bf = carve(T * 2, BF16, [128, T])
        nb_bf = carve(T * 2, BF16, [128, T])
        mixo = carve(T * 2, BF16, [128, T])
        lo_w = carve(T * 2, BF16, [128, T])
        lo_a = carve(T * 2, BF16, [128, T])
        lo_g = carve(T * 2, BF16, [128, T])
        w2b = carve(RW * 2, BF16, [128, RW])
        a2g = carve(RW * 2, BF16, [128, RW])
        g2a = carve(RW * 2, BF16, [128, RW])
        w0b = carve(256 * 4, F32, [128, 256])
        LW = [tA[:, 0:NT * 128].rearrange("p (a b) -> p a b", b=128), rawp[:, 0:NT * 128].rearrange("p (a b) -> p a b", b=128)]
        LWT = ["tA", "rawp"]
        vpad = carve(NT * 256 * 2, BF16, [128, NT, 256])
        win_t = [carve(KC * 128 * 2, BF16, [128, KC, 128]) for _ in range(2)]
        UB = {}
        for d in range(2):
            UB[d] = dict(
                E1=[carve(256 * 4, F32, [128, 256]) for _ in range(2)], Em=[carve(128 * 4, F32, [128, 128]) for _ in range(2)],
                fm=[carve(512 * 2, BF16, [128, 4, 128]) for _ in range(2)],
                rh=[carve(256 * 2, BF16, [128, 2, 128]) for _ in range(2)],
                kh=[carve(256 * 2, BF16, [128, 2, 128]) for _ in range(2)],
                tk=[carve(768 * 2, BF16, [128, 3, 256]) for _ in range(2)],
                A=carve(1024 * 2, BF16, [128, 2, 512]),
                X0=carve(256 * 2, BF16, [128, 2, 128]),
                XZ=[carve(512 * 2, BF16, [128, 4, 128]) for _ in range(2)],
                Pm=[carve(256 * 2, BF16, [128, 2, 128]) for _ in range(2)],
                akv=carve(256 * 2, BF16, [128, 256]), wtT=carve(128 * 2, BF16, [128, 128]),
                upad=carve(256 * 2, BF16, [128, 256]),
            )
        HN = [carve(128 * 4, F32, [128, 128]) for _ in range(2)]
        HB = [carve(128 * 2, BF16, [128, 128]) for _ in range(2)]
        GC = [carve(4 * 4, F32, [128, 4]) for _ in range(2)]
        tq = [carve(512 * 4, F32, [128, 512]) for _ in range(2)]

        def zero_pads(e):
            e.memset(rawp[:, 0:1], 0.0)
            return e.memset(rawp[:, T + 1:T + 2], 0.0)

        def zero_padded(e):
            e.memset(vpad[:].rearrange("p a b -> p (a b)"), 0.0)
            for d in range(2):
                for sl in range(2):
                    e.memset(UB[d]["tk"][sl][:].rearrange("p a b -> p (a b)"), 0.0)
                    e.memset(UB[d]["rh"][sl][:].rearrange("p a b -> p (a b)"), 0.0)
                    e.memset(UB[d]["kh"][sl][:].rearrange("p a b -> p (a b)"), 0.0)
                e.memset(UB[d]["akv"][:], 0.0)
                r_ = e.memset(UB[d]["upad"][:], 0.0)
            return r_
        P.add("pool", zero_padded, w=("vpad",) + tuple(("ub", d, nm) for d in range(2) for nm in ("akv", "upad"))
              + tuple(("ub", d, nm, sl) for d in range(2) for nm in ("tk", "fmh") for sl in range(2)))

        P.add("pool", lambda e: e.dma_start(out=w2b[0:64, :], in_=w2_decay[0]), w=(("lw_w", 0),), chan="lora")
        P.add("pool", lambda e: e.dma_start(out=w2b[64:128, :], in_=w2_decay[1]), w=(("lw_w", 1),), chan="lora")
        P.add("pool", lambda e: e.dma_start(out=a2g[0:64, :], in_=a2), w=(("lw_w", 2),), chan="lora")
        P.add("pool", lambda e: e.dma_start(out=a2g[64:96, :], in_=g2_gate[128:160, :]), w=(("lw_w", 3),), chan="lora")
        P.add("pool", lambda e: e.dma_start(out=g2a[:], in_=g2_gate[0:128, :]), w=(("lw_w", 4),), chan="lora")
        LWW = tuple(("lw_w", i) for i in range(5))

        wslot = {"i": 0}

        def project(g, consume):
            slot = wslot["i"] % 2
            wslot["i"] += 1
            rt = tuple(("scr", "win", g, d0) for (_, _, d0) in GROUPS[g])
            P.add("sp", lambda e: e.dma_start(out=win_t[slot][:].rearrange("p a b -> p (a b)"), in_=win_b[g]),
                  r=rt, w=(("win_t", slot),), chan="win%d" % slot)
            for q in range(NQ):
                bank = q % 4
                for kc in range(KC):
                    P.add("pe", lambda e, kc=kc, q=q, bank=bank: e.matmul(PS[bank][:], win_t[slot][:, kc, :], hn1T[:, kc, q * 512:(q + 1) * 512],
                                                                         start=(kc == 0), stop=(kc == KC - 1)),
                          r=(("win_t", slot),), w=(psb(bank),))
                consume(q, PS[bank][:], psb(bank))

        def to_raw(q, ps, ptok):
            eng = evac_eng()
            dst = rawp[:, 1 + q * 512:1 + (q + 1) * 512]
            if eng == "dve":
                P.add("dve", lambda e: e.tensor_copy(out=dst, in_=ps), r=(ptok,), w=("rawp",))
            else:
                P.add("act", lambda e: e.activation(out=dst, in_=ps, func=AF.Copy), r=(ptok,), w=("rawp",))

        def shift(dst, dst_tok, om, mue, muo, np_=128, p0=0, post=None):
            ps_ = slice(p0, p0 + np_)
            tmp = tA[ps_, :]
            P.add("pool", zero_pads, r=("rawp",), w=("rawp",))
            P.add("act", lambda e: e.activation(out=tmp, in_=rawp[ps_, 1:T + 1], func=AF.Identity, scale=om[ps_, :]), r=("rawp",), w=("tA",))
            P.add("dve", lambda e: e.scalar_tensor_tensor(out=tmp, in0=rawp[ps_, 0:T], scalar=mue[ps_, :], in1=tmp, op0=ALU.mult, op1=ALU.add),
                  r=("rawp", "tA"), w=("tA",))
            if post is None:
                P.add("dve", lambda e: e.scalar_tensor_tensor(out=dst, in0=rawp[ps_, 2:T + 2], scalar=muo[ps_, :], in1=tmp, op0=ALU.mult, op1=ALU.add),
                      r=("rawp", "tA"), w=(dst_tok,))
            else:
                P.add("dve", lambda e: e.scalar_tensor_tensor(out=tmp, in0=rawp[ps_, 2:T + 2], scalar=muo[ps_, :], in1=tmp, op0=ALU.mult, op1=ALU.add),
                      r=("rawp", "tA"), w=("tA",))
                P.add("act", lambda e: e.activation(out=dst, in_=tmp, func=post), r=("tA",), w=(dst_tok,))

        def do_hp(hp):
            hc = slice(hp * 128, (hp + 1) * 128)
            project(hp, to_raw)
            shift(r_bf[:], "r_bf", om_c[:, hp:hp + 1], mue_c[:, hp:hp + 1], muo_c[:, hp:hp + 1])
            project(16 + hp, to_raw)
            shift(v_bf[:], "v_bf", om_c[:, 16 + hp:17 + hp], mue_c[:, 16 + hp:17 + hp], muo_c[:, 16 + hp:17 + hp])
            project(8 + hp, to_raw)
            shift(tB[:], "tB", om_c[:, 8 + hp:9 + hp], mue_c[:, 8 + hp:9 + hp], muo_c[:, 8 + hp:9 + hp])
            chk(21)
            for q in range(NQ):
                bank = 4 + q % 2
                P.add("pe", lambda e, q=q, bank=bank: e.matmul(PS[bank][:], a2g[0:64, hc], lo_a[0:64, q * 512:(q + 1) * 512], start=True, stop=True),
                      r=LWW + ("lo_a",), w=(psb(bank),))
                P.add("act", lambda e, q=q, bank=bank: e.activation(out=yacc[:, q * 512:(q + 1) * 512], in_=PS[bank][:], func=AF.Sigmoid,
                                                                      bias=par8["a0"][:, hp:hp + 1]), r=(psb(bank),), w=("yacc",))
            P.add("dve", lambda e: e.tensor_scalar(out=tA[:], in0=tB[:], scalar1=par8["k_k"][:, hp:hp + 1], scalar2=None, op0=ALU.mult),
                  r=("tB",), w=("tA",))
            for q in range(NQ):
                qs = slice(q * 512, (q + 1) * 512)
                bank = 4 + q % 2
                tqq = tq[q % 2]
                tqt = ("tq", q % 2)
                P.add("pool", lambda e, qs=qs, tqq=tqq: e.tensor_tensor(out=tqq[:], in0=tA[:, qs], in1=tA[:, qs], op=ALU.mult), r=("tA",), w=(tqt,))
                P.add("pe", lambda e, tqq=tqq, bank=bank: e.matmul(PS[bank][:], blk_f[:], tqq[:], start=True, stop=True), r=(tqt,), w=(psb(bank),))
                P.add("act", lambda e, tqq=tqq, bank=bank: e.activation(out=tqq[:], in_=PS[bank][:], func=AF.Sqrt), r=(psb(bank),), w=(tqt,))
                P.add("dve", lambda e, tqq=tqq: e.tensor_scalar(out=tqq[:], in0=tqq[:], scalar1=1e-12, scalar2=None, op0=ALU.max), r=(tqt,), w=(tqt,))
                P.add("dve", lambda e, tqq=tqq: e.reciprocal(out=tqq[:], in_=tqq[:]), r=(tqt,), w=(tqt,))
                P.add("dve", lambda e, qs=qs, tqq=tqq: e.tensor_tensor(out=tA[:, qs], in0=tA[:, qs], in1=tqq[:], op=ALU.mult), r=("tA", tqt), w=("tA",))
            P.add("act", lambda e: e.activation(out=ka_bf[:], in_=tA[:], func=AF.Copy), r=("tA",), w=("ka_bf",))
            P.add("dve", lambda e: e.scalar_tensor_tensor(out=nb_bf[:], in0=tA[:], scalar=-1.0, in1=yacc[:], op0=ALU.mult, op1=ALU.mult),
                  r=("tA", "yacc"), w=("nb_bf",))
            P.add("dve", lambda e: e.tensor_scalar(out=yacc[:], in0=yacc[:], scalar1=par8["k_a"][:, hp:hp + 1], scalar2=par8["omka"][:, hp:hp + 1],
                                                   op0=ALU.mult, op1=ALU.add), r=("yacc",), w=("yacc",))
            P.add("pool", lambda e: e.tensor_tensor(out=k_bf[:], in0=tB[:], in1=yacc[:], op=ALU.mult), r=("tB", "yacc"), w=("k_bf",))
            chk(22)
            P.add("dve", lambda e: e.scalar_tensor_tensor(out=tA[:], in0=r_bf[:], scalar=par8["r_k"][:, hp:hp + 1], in1=k_bf[:], op0=ALU.mult, op1=ALU.mult),
                  r=("r_bf", "k_bf", "tA"), w=("tA",))
            for q in range(NQ):
                qs = slice(q * 512, (q + 1) * 512)
                bank = 4 + q % 2
                P.add("pe", lambda e, qs=qs, bank=bank: e.matmul(PS[bank][:], blk_f[:], tA[:, qs], start=True, stop=True), r=("tA",), w=(psb(bank),))
                P.add("dve", lambda e, qs=qs, bank=bank: e.tensor_tensor(out=tB[:, qs], in0=PS[bank][:], in1=v_bf[:, qs], op=ALU.mult),
                      r=(psb(bank), "v_bf", "tB"), w=("tB",))
            chk(23)
            for ti in range(NT):
                bank = 4 + ti % 2
                pst = PS[bank][:].bitcast(BF16)
                P.add("pe", lambda e, ti=ti, pst=pst: e.transpose(pst[:, 0:128], v_bf[:, ti * 128:(ti + 1) * 128], id_b[:]), r=("v_bf",), w=(psb(bank),))
                eng = evac_eng()
                if eng == "dve":
                    P.add("dve", lambda e, ti=ti, pst=pst: e.tensor_copy(out=vpad[:, ti, 0:64], in_=pst[:, 0:64]), r=(psb(bank),), w=("vpad",))
                    P.add("dve", lambda e, ti=ti, pst=pst: e.tensor_copy(out=vpad[:, ti, 192:256], in_=pst[:, 64:128]), r=(psb(bank),), w=("vpad",))
                else:
                    P.add("act", lambda e, ti=ti, pst=pst: e.activation(out=vpad[:, ti, 0:64], in_=pst[:, 0:64], func=AF.Copy), r=(psb(bank),), w=("vpad",))
                    P.add("act", lambda e, ti=ti, pst=pst: e.activation(out=vpad[:, ti, 192:256], in_=pst[:, 64:128], func=AF.Copy), r=(psb(bank),), w=("vpad",))
            chk(24)
            for d in range(2):
                P.add("sp", lambda e, d=d: e.dma_start(out=w0b[:, d * 128:(d + 1) * 128], in_=w0_decay[d, hc].partition_broadcast(128)),
                      w=(("w0b", d),), chan="w0b")
            for d in range(2):
                dr = slice(d * 64, (d + 1) * 64)
                for t4 in range(NT // 4):
                    bank = 4 + t4 % 2
                    for j in range(4):
                        ti = t4 * 4 + j
                        P.add("pe", lambda e, ti=ti, j=j, bank=bank, dr=dr: e.matmul(PS[bank][:, j * 128:(j + 1) * 128], lo_w[dr, ti * 128:(ti + 1) * 128], w2b[dr, hc],
                                                                                   start=True, stop=True), r=("lo_w",) + LWW, w=(psb(bank),))
                    for j in range(4):
                        ti = t4 * 4 + j
                        P.add("dve", lambda e, ti=ti, j=j, bank=bank, d=d: e.tensor_tensor(out=LW[d][:, ti, :], in0=PS[bank][:, j * 128:(j + 1) * 128],
                                                                                        in1=w0b[:, d * 128:(d + 1) * 128], op=ALU.add),
                              r=(psb(bank), ("w0b", d)), w=(LWT[d],))
                P.add("act", lambda e, d=d: e.activation(out=LW[d][:].rearrange("p a b -> p (a b)"), in_=LW[d][:].rearrange("p a b -> p (a b)"), func=AF.Sigmoid),
                      r=(LWT[d],), w=(LWT[d],))

            chk(25)
            def zero_state(e):
                for d in range(2):
                    e.memset(HN[d][:], 0.0)
                    e.memset(GC[d][:], 1.0)
                    r_ = e.memset(HB[d][:], 0.0)
                return r_
            P.add("pool", zero_state, w=(("HN", 0), ("HN", 1), ("HB", 0), ("HB", 1), ("GC", 0, 0), ("GC", 0, 1), ("GC", 1, 0), ("GC", 1, 1)))

            def unit_stages(d, step):
                ti = step if d == 0 else NT - 1 - step
                U = UB[d]
                E1 = U["E1"][step % 2]
                E1p = U["E1"][(step - 1) % 2]
                e1t = ("ub", d, "E1", step % 2)
                e1pt = ("ub", d, "E1", (step - 1) % 2)
                kb = ("ub", d)
                b0, b1, b2, b3 = 4 * d, 4 * d + 1, 4 * d + 2, 4 * d + 3
                ts_ = slice(ti * 128, (ti + 1) * 128)
                endcol = 127 if d == 0 else 0
                startcol = 128 + (0 if d == 0 else 127)
                st = []
                sl = step % 2
                fm = U["fm"][sl]
                tkk = U["tk"][sl]
                Em = U["Em"][sl]
                rhh = U["rh"][sl]
                khh = U["kh"][sl]
                FMT = kb + ("fm", sl)
                TKT = kb + ("tk", sl)
                FHT = kb + ("fmh", sl)
                EMT = kb + ("Em", sl)
                gcol = 2 * sl
                GCT = ("GC", d, sl)

                def s_prep():
                    P.add("pe", lambda e: e.matmul(PS[b3][:, 0:256], LW[d][:, ti, :], TRI[d][:], start=True, stop=True), r=(LWT[d],), w=(psb(b3),))
                    P.add("act", lambda e: e.activation(out=E1[:], in_=PS[b3][:, 0:256], func=AF.Exp), r=(psb(b3),), w=(e1t,))
                    P.add("act", lambda e: e.activation(out=Em[:], in_=PS[b3][:, 0:128], func=AF.Exp, scale=-1.0), r=(psb(b3),), w=(EMT,))
                    if step > 0:
                        P.add("dve", lambda e: e.reciprocal(out=GC[d][:, gcol + 1:gcol + 2], in_=E1[:, startcol:startcol + 1]), r=(e1t,), w=(("GCt", d, sl),))
                        P.add("dve", lambda e: e.tensor_tensor(out=GC[d][:, gcol:gcol + 1], in0=E1p[:, endcol:endcol + 1], in1=GC[d][:, gcol + 1:gcol + 2], op=ALU.mult),
                              r=(e1pt, ("GCt", d, sl)), w=(GCT,))
                    P.add("dve", lambda e: e.tensor_tensor(out=fm[:, 0, :], in0=r_bf[:, ts_], in1=E1[:, 0:128], op=ALU.mult), r=("r_bf", e1t), w=(FMT,))
                    P.add("pool", lambda e: e.tensor_tensor(out=fm[:, 1, :], in0=k_bf[:, ts_], in1=Em[:], op=ALU.mult), r=("k_bf", EMT), w=(FMT,))
                    P.add("pool", lambda e: e.tensor_tensor(out=fm[:, 2, :], in0=nb_bf[:, ts_], in1=Em[:], op=ALU.mult), r=("nb_bf", EMT), w=(FMT,))
                    P.add("dve", lambda e: e.tensor_tensor(out=fm[:, 3, :], in0=ka_bf[:, ts_], in1=E1[:, 128:256], op=ALU.mult), r=("ka_bf", e1t), w=(FMT,))
                    for h in range(2):
                        hs = slice(h * 64, (h + 1) * 64)
                        P.add("pool", lambda e, h=h, hs=hs: e.tensor_copy(out=rhh[hs, h, :], in_=fm[hs, 0, :]), r=(FMT,), w=(FHT,))
                        P.add("pool", lambda e, h=h, hs=hs: e.tensor_copy(out=khh[hs, h, :], in_=fm[hs, 3, :]), r=(FMT,), w=(FHT,))
                    pst = PS[b3][:].bitcast(BF16)
                    for m in range(3):
                        P.add("pe", lambda e, m=m: e.transpose(pst[:, 512 + m * 128:512 + (m + 1) * 128], fm[:, 1 + m, :], id_b[:]), r=(FMT,), w=(psb(b3),))
                    for m in range(3):
                        eng = evac_eng()
                        for h in range(2):
                            src = pst[:, 512 + m * 128 + h * 64:512 + m * 128 + (h + 1) * 64]
                            dst = tkk[:, m, h * 192:h * 192 + 64]
                            if eng == "dve":
                                P.add("dve", lambda e, src=src, dst=dst: e.tensor_copy(out=dst, in_=src), r=(psb(b3),), w=(TKT,))
                            else:
                                P.add("act", lambda e, src=src, dst=dst: e.activation(out=dst, in_=src, func=AF.Copy), r=(psb(b3),), w=(TKT,))
                st.append(s_prep)

                def s_A():
                    for h in range(2):
                        hs = slice(h * 64, (h + 1) * 64)
                        bank = b0 + h
                        for j, (li, rsrc) in enumerate(((2, "kh"), (1, "kh"), (1, "rh"), (2, "rh"))):
                            P.add("pe", lambda e, j=j, li=li, rsrc=rsrc, h=h, bank=bank: e.matmul(PS[bank][:, j * 128:(j + 1) * 128], fm[:, li, :], (khh if rsrc == "kh" else rhh)[:, h, :], start=True, stop=True),
                                  r=(FMT, FHT), w=(psb(bank),))
                        P.add("pe", lambda e, h=h: e.matmul(PS[b2][:, h * 128:(h + 1) * 128], khh[:, h, :], fm[:, 2, :], start=True, stop=True),
                              r=(FMT, FHT), w=(("psh", b2, h),))
                    if step == 0 and d == 0:
                        chk(201)
                    for h in range(2):
                        P.add("dve", lambda e, h=h: e.tensor_tensor(out=U["A"][:, h, :], in0=PS[b0 + h][:], in1=MA[d][:], op=ALU.mult), r=(psb(b0 + h),), w=(kb + ("A",),))
                    P.add("dve", lambda e: e.tensor_tensor(out=U["X0"][:].rearrange("p a b -> p (a b)"), in0=PS[b2][:, 0:256], in1=MX[d][:], op=ALU.mult),
                          r=(("psh", b2, 0), ("psh", b2, 1)), w=(kb + ("X0",),))
                    if step == 0 and d == 0:
                        chk(202)
                    for h in range(2):
                        P.add("pool", lambda e, h=h: e.tensor_tensor(out=U["Pm"][0][:, h, :], in0=U["A"][:, h, 0:128], in1=id_b[:], op=ALU.add),
                              r=(kb + ("A",),), w=(kb + ("P", 0, h),))
                    if step == 0 and d == 0:
                        chk(203)
                    for h in range(2):
                        P.add("pe", lambda e, h=h: e.matmul(PS[b3][:, 256:384], U["A"][:, h, 128:256], vpad[:, ti, h * 128:(h + 1) * 128], start=(h == 0), stop=(h == 1)),
                              r=(kb + ("A",), "vpad"), w=(psb(b3),))
                    P.add("act", lambda e: e.activation(out=U["akv"][:, 0:64], in_=PS[b3][:, 256:320], func=AF.Copy), r=(psb(b3),), w=(kb + ("akv",),))
                    P.add("act", lambda e: e.activation(out=U["akv"][:, 192:256], in_=PS[b3][:, 320:384], func=AF.Copy), r=(psb(b3),), w=(kb + ("akv",),))
                st.append(s_A)

                def mk_level(lev):
                    def s_lev():
                        cur = lev % 2
                        nxt = (lev + 1) % 2
                        XZn = U["XZ"][nxt]
                        Pc = U["Pm"][cur]
                        Pn = U["Pm"][nxt]
                        wcols = 256 if lev < 5 else 128
                        for h in range(2):
                            sbk = b0 + h
                            if lev == 0:
                                Xh = U["X0"][:, h, :]
                                Zh = U["A"][:, h, 0:128]
                                rt = (kb + ("X0",), kb + ("A",))
                            else:
                                Xh = U["XZ"][cur][:, 2 * h, :]
                                Zh = U["XZ"][cur][:, 2 * h + 1, :]
                                rt = (kb + ("XZ", cur, h),)
                            P.add("pe", lambda e, Xh=Xh, Zh=Zh, sbk=sbk: e.matmul(PS[sbk][:, 0:128], Zh, Xh, start=True, stop=True), r=rt, w=(psb(sbk),))
                            if lev < 5:
                                P.add("pe", lambda e, Xh=Xh, Zh=Zh, sbk=sbk: e.matmul(PS[sbk][:, 128:256], Xh, Zh, start=True, stop=True), r=rt, w=(psb(sbk),))
                            dst = XZn[:, 2 * h:2 * h + 2, :].rearrange("p a b -> p (a b)")[:, 0:wcols]
                            if h == 0:
                                P.add("act", lambda e, dst=dst, sbk=sbk: e.activation(out=dst, in_=PS[sbk][:, 0:wcols], func=AF.Copy), r=(psb(sbk),), w=(kb + ("XZ", nxt, h),))
                            else:
                                P.add("dve", lambda e, dst=dst, sbk=sbk: e.tensor_copy(out=dst, in_=PS[sbk][:, 0:wcols]), r=(psb(sbk),), w=(kb + ("XZ", nxt, h),))
                            P.add("pe", lambda e, h=h: e.matmul(PS[b2][:, h * 128:(h + 1) * 128], XZn[:, 2 * h, :], Pc[:, h, :], start=True, stop=True),
                                  r=(kb + ("XZ", nxt, h), kb + ("P", cur, h)), w=(("psh", b2, h),))
                            P.add("dve", lambda e, h=h: e.tensor_tensor(out=Pn[:, h, :], in0=PS[b2][:, h * 128:(h + 1) * 128], in1=Pc[:, h, :], op=ALU.add),
                                  r=(("psh", b2, h), kb + ("P", cur, h)), w=(kb + ("P", nxt, h),))
                    return s_lev
                for lev in range(6):
                    st.append(mk_level(lev))

                def s_tail():
                    TT = U["Pm"][0]
                    for h in range(2):
                        P.add("pe", lambda e, h=h: e.matmul(PS[b3][:, 256:384], TT[:, h, :], U["akv"][:, h * 128:(h + 1) * 128], start=(h == 0), stop=False),
                              r=(kb + ("P", 0, h), kb + ("akv",)), w=(psb(b3),))
                    for h in range(2):
                        P.add("pe", lambda e, h=h: e.matmul(PS[b1][:, 128:256], tkk[:, 2, h * 128:(h + 1) * 128], TT[:, h, :], start=(h == 0), stop=(h == 1)),
                              r=(kb + ("P", 0, h), TKT), w=(psb(b1),))
                    P.add("act", lambda e: e.activation(out=U["wtT"][:], in_=PS[b1][:, 128:256], func=AF.Copy), r=(psb(b1),), w=(kb + ("wtT",),))
                    P.add("act", lambda e: e.activation(out=HB[d][:], in_=HN[d][:], func=AF.Identity, scale=GC[d][:, gcol:gcol + 1]), r=(("HN", d), GCT), w=(("HB", d),))
                    P.add("pe", lambda e: e.matmul(PS[b3][:, 256:384], U["wtT"][:], HB[d][:], start=False, stop=True), r=(kb + ("wtT",), ("HB", d)), w=(psb(b3),))
                    P.add("dve", lambda e: e.tensor_copy(out=U["upad"][:, 0:64], in_=PS[b3][:, 256:320]), r=(psb(b3),), w=(kb + ("upad",),))
                    P.add("dve", lambda e: e.tensor_copy(out=U["upad"][:, 192:256], in_=PS[b3][:, 320:384]), r=(psb(b3),), w=(kb + ("upad",),))
                    yb = PS[b0][:, 0:128]
                    P.add("pe", lambda e: e.matmul(yb, HB[d][:], fm[:, 0, :], start=True, stop=False), r=(("HB", d), FMT), w=(psb(b0),))
                    for h in range(2):
                        P.add("pe", lambda e, h=h: e.matmul(yb, vpad[:, ti, h * 128:(h + 1) * 128], U["A"][:, h, 256:384], start=False, stop=False),
                              r=("vpad", kb + ("A",)), w=(psb(b0),))
                    for h in range(2):
                        P.add("pe", lambda e, h=h: e.matmul(yb, U["upad"][:, h * 128:(h + 1) * 128], U["A"][:, h, 384:512], start=False, stop=(h == 1)),
                              r=(kb + ("upad",), kb + ("A",)), w=(psb(b0),))
                    if step < NT // 2:
                        P.add("act", lambda e: e.activation(out=yacc[:, ts_], in_=yb, func=AF.Copy), r=(psb(b0),), w=("yacc",))
                    else:
                        P.add("dve", lambda e: e.tensor_tensor(out=yacc[:, ts_], in0=yb, in1=yacc[:, ts_], op=ALU.add), r=(psb(b0), "yacc"), w=("yacc",))
                    hb_ = PS[b1][:, 0:128]
                    for h in range(2):
                        P.add("pe", lambda e, h=h: e.matmul(hb_, tkk[:, 0, h * 128:(h + 1) * 128], vpad[:, ti, h * 128:(h + 1) * 128], start=(h == 0), stop=False),
                              r=(TKT, "vpad"), w=(psb(b1),))
                    for h in range(2):
                        P.add("pe", lambda e, h=h: e.matmul(hb_, tkk[:, 1, h * 128:(h + 1) * 128], U["upad"][:, h * 128:(h + 1) * 128], start=False, stop=(h == 1)),
                              r=(TKT, kb + ("upad",)), w=(psb(b1),))
                    P.add("dve", lambda e: e.scalar_tensor_tensor(out=HN[d][:], in0=HN[d][:], scalar=GC[d][:, gcol:gcol + 1], in1=hb_, op0=ALU.mult, op1=ALU.add),
                          r=(psb(b1), ("HN", d), GCT), w=(("HN", d),))
                st.append(s_tail)
                return st

            units = [[unit_stages(0, step), unit_stages(1, step)] for step in range(NT)]
            for ch in units[0]:
                ch[0]()
            for step in range(NT):
                for ch in units[step]:
                    ch[1]()
                if step + 1 < NT:
                    for ch in units[step + 1]:
                        ch[0]()
                for si in range(2, len(units[step][0])):
                    for ch in units[step]:
                        ch[si]()
                chk(140 + step)
            chk(50)
            for q in range(NQ):
                qs = slice(q * 512, (q + 1) * 512)
                bank = q % 4
                tqq = tq[q % 2]
                tqt = ("tq", q % 2)
                P.add("pe", lambda e, qs=qs, bank=bank: e.matmul(PS[bank][:], blk_f[:], yacc[:, qs], start=True, stop=True), r=("yacc",), w=(psb(bank),))
                P.add("dve", lambda e, qs=qs, bank=bank: e.scalar_tensor_tensor(out=tA[:, qs], in0=PS[bank][:], scalar=-1.0 / 64, in1=yacc[:, qs], op0=ALU.mult, op1=ALU.add),
                      r=(psb(bank), "yacc"), w=("tA",))
                P.add("pool", lambda e, qs=qs, tqq=tqq: e.tensor_tensor(out=tqq[:], in0=tA[:, qs], in1=tA[:, qs], op=ALU.mult), r=("tA",), w=(tqt,))
                P.add("pe", lambda e, tqq=tqq, bank=bank: e.matmul(PS[bank][:], blk_f[:], tqq[:], start=True, stop=True), r=(tqt,), w=(psb(bank),))
                P.add("act", lambda e, tqq=tqq, bank=bank: e.activation(out=tqq[:], in_=PS[bank][:], func=AF.Sqrt, bias=epsc[:, 1:2], scale=1.0 / 64),
                      r=(psb(bank),), w=(tqt,))
                P.add("dve", lambda e, tqq=tqq: e.reciprocal(out=tqq[:], in_=tqq[:]), r=(tqt,), w=(tqt,))
                P.add("dve", lambda e, qs=qs, tqq=tqq: e.tensor_tensor(out=tA[:, qs], in0=tA[:, qs], in1=tqq[:], op=ALU.mult), r=("tA", tqt), w=("tA",))
                P.add("act", lambda e, qs=qs: e.activation(out=tA[:, qs], in_=tA[:, qs], func=AF.Identity, bias=par8["lnx_b"][:, hp:hp + 1], scale=par8["lnx_g"][:, hp:hp + 1]),
                      r=("tA",), w=("tA",))
                P.add("pool", lambda e, qs=qs: e.tensor_tensor(out=tA[:, qs], in0=tA[:, qs], in1=tB[:, qs], op=ALU.add), r=("tA", "tB"), w=("tA",))
                gbank = 4 + q % 4
                P.add("pe", lambda e, qs=qs, gbank=gbank: e.matmul(PS[gbank][:], g2a[:, hc], lo_g[:, qs], start=True, stop=False), r=LWW + ("lo_g",), w=(psb(gbank),))
                P.add("pe", lambda e, qs=qs, gbank=gbank: e.matmul(PS[gbank][:], a2g[64:96, hc], lo_a[64:96, qs], start=False, stop=True), r=LWW + ("lo_a",), w=(psb(gbank),))
                P.add("dve", lambda e, qs=qs, gbank=gbank: e.tensor_tensor(out=mixo[:, qs], in0=PS[gbank][:], in1=tA[:, qs], op=ALU.mult), r=(psb(gbank), "tA", "mixo"), w=("mixo",))
            P.add("sp", lambda e: e.dma_start(out=mixT_d[s, hp], in_=mixo[:]), r=("mixo",), w=(("mixd", hp),), chan="mixst")

        if do_rwkv:
            project(G_WFWB, to_raw)
            shift(lo_w[:], "lo_w", om_c[:, 24:25], mue_c[:, 24:25], muo_c[:, 24:25], post=AF.Tanh)
            project(G_AG1, to_raw)
            shift(lo_a[0:64, :], "lo_a", muag[:, 1:2], muag[:, 2:3], muag[:, 3:4], np_=64, p0=0, post=AF.Copy)
            shift(lo_a[64:96, :], "lo_a", muag[:, 1:2], muag[:, 2:3], muag[:, 3:4], np_=32, p0=64, post=AF.Sigmoid)
            project(G_G0, to_raw)
            shift(lo_g[:], "lo_g", mug0[:, 1:2], mug0[:, 2:3], mug0[:, 3:4], post=AF.Sigmoid)
            chk(20)
            for hp in range(8):
                do_hp(hp)
                chk(51)
        else:
            for hp in range(8):
                P.add("pool", lambda e: e.memset(mixo[:], 0.0), r=("mixo",), w=("mixo",))
                P.add("sp", lambda e, hp=hp: e.dma_start(out=mixT_d[s, hp], in_=mixo[:]), r=("mixo",), w=(("mixd", hp),), chan="mixst")

        def do_conv(cc):
            def cons_C(q, ps, ptok):
                P.add("act", lambda e: e.activation(out=tA[:, q * 512:(q + 1) * 512], in_=ps, func=AF.Copy), r=(ptok,), w=("tA",))
            project(G_CONV + 8 + cc, cons_C)
            P.add("pool", zero_pads, r=("rawp",), w=("rawp",))

            def cons_X(q, ps, ptok):
                P.add("dve", lambda e: e.tensor_tensor(out=rawp[:, 1 + q * 512:1 + (q + 1) * 512], in0=ps, in1=tA[:, q * 512:(q + 1) * 512], op=ALU.mult),
                      r=(ptok, "tA"), w=("rawp",))
            project(G_CONV + 16 + cc, cons_X)

            def cons_B(q, ps, ptok):
                P.add("act", lambda e: e.activation(out=tB[:, q * 512:(q + 1) * 512], in_=ps, func=AF.Copy), r=(ptok,), w=("tB",))
            project(G_CONV + cc, cons_B)
            cw = [par8["cw%d" % j][:, cc:cc + 1] for j in range(3)]
            P.add("act", lambda e: e.activation(out=tA[:], in_=rawp[:, 1:T + 1], func=AF.Identity, scale=cw[1]), r=("rawp", "tA"), w=("tA",))
            P.add("dve", lambda e: e.scalar_tensor_tensor(out=tA[:], in0=rawp[:, 0:T], scalar=cw[0], in1=tA[:], op0=ALU.mult, op1=ALU.add), r=("rawp", "tA"), w=("tA",))
            P.add("dve", lambda e: e.scalar_tensor_tensor(out=tA[:], in0=rawp[:, 2:T + 2], scalar=cw[2], in1=tA[:], op0=ALU.mult, op1=ALU.add), r=("rawp", "tA"), w=("tA",))
            P.add("dve", lambda e: e.tensor_tensor(out=mixo[:], in0=tA[:], in1=tB[:], op=ALU.mult), r=("tA", "tB", "mixo"), w=("mixo",))
            P.add("sp", lambda e: e.dma_start(out=mixT_d[s, 8 + cc], in_=mixo[:]), r=("mixo",), w=(("mixd", 8 + cc),), chan="mixst")
        for cc in range(8):
            do_conv(cc)

        P.barrier()
        if stage == 6:
            raise StopIteration

        apos[0] = 0
        mx = carve(KC * 512 * 2, BF16, [128, KC, 512])
        xr = carve(4 * D * 4, F32, [128, 4, D])
        wo = [carve(8 * 512 * 2, BF16, [128, 8, 512]) for _ in range(2)]
        xn2 = carve(D * 4, F32, [128, D])
        junk2 = carve(D * 2, BF16, [128, D])
        NWGU = 3
        wgu = [[carve(KC * 128 * 2, BF16, [128, KC, 128]) for _ in range(NWGU)] for _ in range(2)]
        actT = carve(FC * 512 * 2, BF16, [128, FC, 512])
        wdt = [carve(11 * 512 * 2, BF16, [128, 11, 512]) for _ in range(2)]
        sgt = [carve(512 * 4, F32, [128, 512]) for _ in range(2)]
        dgm = carve(128 * 4, F32, [128, 128])
        nfb = carve(D * 4, F32, [128, D])
        gtb = carve(D * 4, F32, [128, D])
        P.add("sp", lambda e: e.dma_start(out=nfb[:], in_=norm_f_g.partition_broadcast(128)), w=("nfb",), chan="nfb")

        def build_gtb(chunk0):
            for kq in range(4):
                bank = 4 + kq % 2
                for k4 in range(4):
                    kc = kq * 4 + k4
                    P.add("dve", lambda e, kc=kc: e.tensor_scalar(out=dgm[:], in0=id_f[:], scalar1=modT[:, s, chunk0 + kc:chunk0 + kc + 1], scalar2=None, op0=ALU.mult),
                          r=("dgm",), w=("dgm",))
                    P.add("pe", lambda e, k4=k4, bank=bank: e.matmul(PS[bank][:, k4 * 128:(k4 + 1) * 128], ones_f[:], dgm[:], start=True, stop=True), r=("dgm",), w=(psb(bank),))
                P.add("act", lambda e, kq=kq, bank=bank: e.activation(out=gtb[:, kq * 512:(kq + 1) * 512], in_=PS[bank][:], func=AF.Copy), r=(psb(bank),), w=("gtb",))

        wo_i = {"i": 0}
        wgu_i = {"i": 0}
        wd_i = {"i": 0}
        MXT = tuple(("mx", kc) for kc in range(KC))

        def do_tile(tt):
            tsl = slice(tt * 512, (tt + 1) * 512)
            for kc in range(KC):
                P.add("sp", lambda e, kc=kc: e.dma_start(out=mx[:, kc, :], in_=mixT_d[s, kc, :, tsl]), r=(("mixd", kc),), w=(("mx", kc),), chan="mxl")
            for j in range(4):
                P.add("sp", lambda e, j=j: e.dma_start(out=xr[:, j, :], in_=xs[s, tt * 512 + j * 128:tt * 512 + (j + 1) * 128, :]), w=(("xr", j),), chan="xrl")
            if stage == 70:
                raise StopIteration
            build_gtb(32)
            if stage == 71:
                raise StopIteration
            for dg in range(4):
                dsl = slice(dg * 512, (dg + 1) * 512)
                for half in range(2):
                    slot = wo_i["i"] % 2
                    wo_i["i"] += 1
                    P.add("sp", lambda e, slot=slot, dg=dg, half=half: e.dma_start(
                        out=wo[slot][:].rearrange("p a b -> p (a b)"), in_=wout_b[dg][:, half * 8 * 512:(half + 1) * 8 * 512]),
                        r=(("scr", "wout", dg, half),), w=(("wo", slot),), chan="wo%d" % slot)
                    for k8 in range(8):
                        kc = half * 8 + k8
                        for j in range(4):
                            P.add("pe", lambda e, j=j, k8=k8, kc=kc, slot=slot: e.matmul(PS[j][:], mx[:, kc, j * 128:(j + 1) * 128], wo[slot][:, k8, :],
                                                                                       start=(kc == 0), stop=(kc == KC - 1)),
                                  r=(("mx", kc), ("wo", slot)), w=(psb(j),))
                for j in range(4):
                    P.add("dve", lambda e, j=j, dsl=dsl: e.tensor_tensor(out=sgt[j % 2][:], in0=PS[j][:], in1=gtb[:, dsl], op=ALU.mult), r=(psb(j), "gtb"), w=(("sgt", j % 2),))
                    P.add("pool", lambda e, j=j, dsl=dsl: e.tensor_tensor(out=xr[:, j, dsl], in0=xr[:, j, dsl], in1=sgt[j % 2][:], op=ALU.add), r=(("sgt", j % 2), ("xr", j)), w=(("xr", j),))
            if stage == 7:
                raise StopIteration
            for j in range(4):
                norm_transpose(xr[:, j, :], ("xr", j), (s1c if stage == 84 else s2c)[:, s, :], modT[:, s, 0:16] if stage == 84 else modT[:, s, 48:64], mx, slice(j * 128, (j + 1) * 128), (xn2[:], "xn2", junk2[:]),
                               lambda kc: ("mx", kc), bank0=6)
            if stage in (8, 81, 82, 83, 84):
                raise StopIteration
            for fb in range(FC):
                slot = wgu_i["i"] % NWGU
                wgu_i["i"] += 1
                P.add("sp", lambda e, fb=fb, slot=slot: e.dma_start(out=wgu[0][slot][:].rearrange("p a b -> p (a b)"), in_=wg_b[fb]),
                      r=(("scr", "wg", fb),), w=(("wgu", 0, slot),), chan="wgu%d" % slot)
                P.add("sp", lambda e, fb=fb, slot=slot: e.dma_start(out=wgu[1][slot][:].rearrange("p a b -> p (a b)"), in_=wu_b[fb]),
                      r=(("scr", "wu", fb),), w=(("wgu", 1, slot),), chan="wgu%d" % slot)
                gb = 4 + 2 * (fb % 2)
                for w_ in range(2):
                    for kc in range(KC):
                        P.add("pe", lambda e, w_=w_, kc=kc, slot=slot, gb=gb: e.matmul(PS[gb + w_][:], wgu[w_][slot][:, kc, :], mx[:, kc, :], start=(kc == 0), stop=(kc == KC - 1)),
                              r=(("wgu", w_, slot), ("mx", kc)), w=(psb(gb + w_),))
                P.add("act", lambda e, fb=fb, gb=gb: e.activation(out=sgt[fb % 2][:], in_=PS[gb][:], func=AF.Silu), r=(psb(gb),), w=(("sgt", fb % 2),))
                P.add("dve", lambda e, fb=fb, gb=gb: e.tensor_tensor(out=actT[:, fb, :], in0=PS[gb + 1][:], in1=sgt[fb % 2][:], op=ALU.mult),
                      r=(psb(gb + 1), ("sgt", fb % 2)), w=(("actT", fb),))
            if stage == 9:
                raise StopIteration
            build_gtb(80)
            for dg in range(4):
                dsl = slice(dg * 512, (dg + 1) * 512)
                for fq in range(4):
                    slot = wd_i["i"] % 2
                    wd_i["i"] += 1
                    P.add("sp", lambda e, slot=slot, dg=dg, fq=fq: e.dma_start(
                        out=wdt[slot][:].rearrange("p a b -> p (a b)"), in_=wd_b[dg, fq]),
                        r=(("scr", "wd", dg, fq),), w=(("wdt", slot),), chan="wd%d" % slot)
                    for f11 in range(11):
                        fc = fq * 11 + f11
                        for j in range(4):
                            P.add("pe", lambda e, j=j, f11=f11, fc=fc, slot=slot: e.matmul(PS[j][:], actT[:, fc, j * 128:(j + 1) * 128], wdt[slot][:, f11, :],
                                                                                         start=(fc == 0), stop=(fc == FC - 1)),
                                  r=(("actT", fc), ("wdt", slot)), w=(psb(j),))
                for j in range(4):
                    P.add("dve", lambda e, j=j, dsl=dsl: e.tensor_tensor(out=sgt[j % 2][:], in0=PS[j][:], in1=gtb[:, dsl], op=ALU.mult), r=(psb(j), "gtb"), w=(("sgt", j % 2),))
                    P.add("pool", lambda e, j=j, dsl=dsl: e.tensor_tensor(out=xr[:, j, dsl], in0=xr[:, j, dsl], in1=sgt[j % 2][:], op=ALU.add), r=(("sgt", j % 2), ("xr", j)), w=(("xr", j),))
            if stage == 10:
                raise StopIteration
            for j in range(4):
                P.add("act", lambda e, j=j: e.activation(out=junk2[:], in_=xr[:, j, :], func=AF.Square, accum_out=stat[:, 2:3]), r=(("xr", j),), w=("stat", "junk"))
                rstd_from_ss(stat[:, 2:3], stat[:, 3:4], 1.0 / D, 0)
                P.add("dve", lambda e, j=j: e.scalar_tensor_tensor(out=xr[:, j, :], in0=xr[:, j, :], scalar=stat[:, 3:4], in1=nfb[:], op0=ALU.mult, op1=ALU.mult),
                      r=(("xr", j), "stat", "nfb"), w=(("xr", j),))
                P.add("sp", lambda e, j=j: e.dma_start(out=ys[s, tt * 512 + j * 128:tt * 512 + (j + 1) * 128, :], in_=xr[:, j, :]), r=(("xr", j),), w=(("ysd", j),), chan="yst")
        for tt in range(NQ):
            do_tile(tt)
        P.barrier()

    try:
        for s in range(NSEQ):
            do_seq(s)
    except StopIteration:
        pass

    P.finish()
    P.emit(nc, es)
    es.close()
    return nc


_W_NAMES = ["w_ada", "b_ada", "norm1_g", "w_in", "mu_shift", "w0_decay", "w2_decay", "a0", "a2", "g2_gate", "k_k", "k_a", "r_k",
            "lnx_g", "lnx_b", "conv_w", "w_out", "norm2_g", "w_ffn_gate", "w_ffn_up", "w_ffn_down"]


def _weights_map(inputs):
    m = {}
    for nm in _W_NAMES:
        a = np.asarray(inputs[nm], dtype=np.float32)
        a = a[0]
        if nm == "r_k":
            a = a.reshape(-1)
        m[nm] = np.ascontiguousarray(a)
    m["norm_f_g"] = np.ascontiguousarray(np.asarray(inputs["norm_f_g"], dtype=np.float32))
    return m


def kernel(**inputs):
    x_prompt = np.asarray(inputs["x_prompt"], dtype=np.float32)
    x_sample = np.asarray(inputs["x_sample"], dtype=np.float32)
    c_prompt = np.asarray(inputs["c_prompt"], dtype=np.float32)
    c_sample = np.asarray(inputs["c_sample"], dtype=np.float32)
    NB, T = x_prompt.shape[0], x_prompt.shape[1]
    NS = x_sample.shape[0]
    ntot = NB + NS
    NSEQ = 3
    ncore = 8

    def getx(i):
        return x_prompt[i] if i < NB else x_sample[i - NB]

    def getc(i):
        return c_prompt[i] if i < NB else c_sample[i - NB]

    wm = _weights_map(inputs)
    in_maps = []
    assign = []
    for c in range(ncore):
        ids = [c, c + 8, c + 16 if c + 16 < ntot else c]
        assign.append(ids)
        m = dict(wm)
        m["xs"] = np.stack([getx(i) for i in ids])
        m["cs"] = np.stack([getc(i) for i in ids])
        in_maps.append(m)
    nc = build_program(NSEQ, T)
    res = run_bass_kernel_spmd(nc, in_maps, core_ids=list(range(ncore)))
    y_p = np.empty_like(x_prompt)
    y_s = np.empty_like(x_sample)
    for c in range(ncore):
        ysc = res.results[c]["ys"]
        for slot, i in enumerate(assign[c]):
            if slot == 2 and c + 16 >= ntot:
                continue
            if i < NB:
                y_p[i] = ysc[slot]
            else:
                y_s[i - NB] = ysc[slot]
    return (y_p, y_s)
```

```python
import math
from contextlib import ExitStack
import numpy as np
import concourse.bass as bass
import concourse.mybir as mybir
from concourse.bass_utils import run_bass_kernel_spmd

F32 = mybir.dt.float32
BF16 = mybir.dt.bfloat16
AF = mybir.ActivationFunctionType
ALU = mybir.AluOpType
AX = mybir.AxisListType

D = 2048
KC = 16
RW = 1024
RWC = 3424
IN_COLS = 6496
DFF = 5632
FC = 44
NMOD = 6
RMS_EPS = 1e-6
LNX_EPS = 64e-5
C0 = -math.exp(-0.5)

GROUPS = []
for i in range(8):
    GROUPS.append([(i * 128, 128, 0)])
for i in range(8):
    GROUPS.append([(1024 + i * 128, 128, 0)])
for i in range(8):
    GROUPS.append([(2048 + i * 128, 128, 0)])
G_WFWB = len(GROUPS); GROUPS.append([(3072, 128, 0)])
G_AG1 = len(GROUPS); GROUPS.append([(3200, 64, 0), (3392, 32, 64), (3392, 32, 96)])
G_G0 = len(GROUPS); GROUPS.append([(3264, 128, 0)])
G_CONV = len(GROUPS)
for j in range(3):
    for i in range(8):
        GROUPS.append([(RWC + j * 1024 + i * 128, 128, 0)])
NG = len(GROUPS)


class Prog:
    ENGS = ("pe", "act", "dve", "pool", "sp")

    def __init__(self):
        self.ops = []
        self.last_w = {}
        self.readers = {}
        self.chan_cnt = {}

    def add(self, eng, fn, r=(), w=(), chan=None):
        i = len(self.ops)
        deps = set()
        for t in r:
            lw = self.last_w.get(t)
            if lw is not None:
                deps.add(lw)
        for t in w:
            lw = self.last_w.get(t)
            if lw is not None:
                deps.add(lw)
            for rd in self.readers.get(t, ()):
                deps.add(rd)
        for t in r:
            self.readers.setdefault(t, []).append(i)
        for t in w:
            self.last_w[t] = i
            self.readers[t] = []
        cval = None
        if chan is not None:
            self.chan_cnt[chan] = self.chan_cnt.get(chan, 0) + 1
            cval = 16 * self.chan_cnt[chan]
        import sys as _s
        fr = _s._getframe(1)
        self.ops.append(dict(eng=eng, fn=fn, deps=deps, chan=chan, cval=cval, sig=False, seq=0, tag=fr.f_lineno))
        return i

    def barrier(self):
        last = {}
        for i, op in enumerate(self.ops):
            if op["fn"] is None:
                continue
            if op["chan"] is not None:
                last[("c", op["chan"])] = i
            else:
                last[("e", op["eng"])] = i
        deps = set(last.values())
        for e in self.ENGS:
            self.ops.append(dict(eng=e, fn=None, deps=set(deps), chan=None, cval=None, sig=False, seq=0))
        self.last_w = {}
        self.readers = {}

    def finish(self):
        self.barrier()

    def simulate(self):
        import bisect
        ops = self.ops
        for op in ops:
            for d in op["deps"]:
                if ops[d]["chan"] is None:
                    ops[d]["sig"] = True
        cnt = {e: 0 for e in self.ENGS}
        for op in ops:
            if op["chan"] is None and op["sig"] and op["fn"] is not None:
                cnt[op["eng"]] += 1
                op["seq"] = cnt[op["eng"]]
        chan_ops = {}
        for i, op in enumerate(ops):
            op["idx"] = i
            if op["chan"] is not None:
                chan_ops.setdefault(op["chan"], []).append(i)
        per = {e: [op for op in ops if op["eng"] == e] for e in self.ENGS}
        pos = {e: 0 for e in self.ENGS}
        sem = {}
        progress = True
        while progress:
            progress = False
            for e in self.ENGS:
                while pos[e] < len(per[e]):
                    op = per[e][pos[e]]
                    ok = True
                    for d in op["deps"]:
                        dop = ops[d]
                        if dop["chan"] is not None:
                            key = ("c", dop["chan"])
                            if str(dop["chan"]).startswith("pcs"):
                                val = dop["cval"]
                            else:
                                val = 16 * bisect.bisect_left(chan_ops[dop["chan"]], op["idx"])
                        else:
                            if dop["eng"] == e and e == "pe":
                                continue
                            if dop["fn"] is None:
                                print("DEP ON NONE OP", op["idx"], d)
                            key = ("e", dop["eng"]); val = dop["seq"]
                        if sem.get(key, 0) < val:
                            ok = False
                            blk = (key, val, sem.get(key, 0), d)
                            break
                    if not ok:
                        op["blk"] = blk
                        break
                    if op["fn"] is not None:
                        if op["chan"] is not None:
                            sem[("c", op["chan"])] = sem.get(("c", op["chan"]), 0) + 16
                        elif op["sig"]:
                            sem[("e", e)] = sem.get(("e", e), 0) + 1
                    pos[e] += 1
                    progress = True
        stuck = {e: (pos[e], len(per[e])) for e in self.ENGS if pos[e] < len(per[e])}
        if stuck:
            print("DEADLOCK", stuck)
            for e in stuck:
                op = per[e][pos[e]]
                print(e, "op idx", op["idx"], "blocked on", op.get("blk"), "tag", op.get("tag"))
        else:
            print("simulate: no deadlock;", {e: len(per[e]) for e in self.ENGS})
        return not stuck

    def emit(self, nc, es):
        import os
        if os.environ.get("KSIM"):
            self.simulate()
        ops = self.ops
        for op in ops:
            for d in op["deps"]:
                if ops[d]["chan"] is None and not (ops[d]["eng"] == "pe" and op["eng"] == "pe"):
                    ops[d]["sig"] = True
        cnt = {e: 0 for e in self.ENGS}
        for op in ops:
            if op["chan"] is None and op["sig"]:
                cnt[op["eng"]] += 1
                op["seq"] = cnt[op["eng"]]
        sems = {}
        for e in self.ENGS:
            sems[("e", e)] = es.enter_context(nc.semaphore("sem_" + e))
        for c in self.chan_cnt:
            sems[("c", c)] = es.enter_context(nc.semaphore("ch_" + str(c)))
        block = es.enter_context(nc.Block())
        for i, op in enumerate(ops):
            op["idx"] = i
        per = {e: [op for op in ops if op["eng"] == e] for e in self.ENGS}
        import bisect
        chan_ops = {}
        for i, op in enumerate(ops):
            if op["chan"] is not None:
                chan_ops.setdefault(op["chan"], []).append(i)

        def run(eng_name, e):
            waited = {}
            for op in per[eng_name]:
                need = {}
                for d in op["deps"]:
                    dop = ops[d]
                    if dop["chan"] is not None:
                        key = ("c", dop["chan"])
                        if str(dop["chan"]).startswith("pcs"):
                            val = dop["cval"]
                        else:
                            lst = chan_ops[dop["chan"]]
                            k = bisect.bisect_left(lst, op["idx"])
                            val = 16 * k
                    else:
                        if dop["eng"] == eng_name and eng_name == "pe":
                            continue
                        key = ("e", dop["eng"]); val = dop["seq"]
                    if need.get(key, 0) < val:
                        need[key] = val
                for key, val in need.items():
                    if waited.get(key, 0) >= val:
                        continue
                    e.wait_ge(sems[key], val)
                    waited[key] = val
                if op["fn"] is None:
                    continue
                ins = op["fn"](e)
                if op["chan"] is not None:
                    ins.then_inc(sems[("c", op["chan"])], 16)
                elif op["sig"]:
                    ins.then_inc(sems[("e", eng_name)], 1)

        @block.tensor
        def _(e):
            run("pe", e)

        @block.scalar
        def _(e):
            run("act", e)

        @block.vector
        def _(e):
            run("dve", e)

        @block.gpsimd
        def _(e):
            run("pool", e)

        @block.sync
        def _(e):
            run("sp", e)


def build_program(NSEQ, T, do_rwkv=True, stage=99):
    NT = T // 128
    NQ = T // 512
    assert T % 512 == 0
    nc = bass.Bass("TRN2", target_bir_lowering=False)
    P = Prog()
    es = ExitStack()

    def din(name, shape):
        return nc.dram_tensor(name, list(shape), F32, kind="ExternalInput").ap()

    xs = din("xs", [NSEQ, T, D])
    cs = din("cs", [NSEQ, D])
    w_ada = din("w_ada", [D, NMOD * D])
    b_ada = din("b_ada", [NMOD * D])
    norm1_g = din("norm1_g", [D])
    w_in = din("w_in", [D, IN_COLS])
    mu_shift = din("mu_shift", [RWC])
    w0_decay = din("w0_decay", [2, RW])
    w2_decay = din("w2_decay", [2, 64, RW])
    a0 = din("a0", [RW])
    a2 = din("a2", [64, RW])
    g2_gate = din("g2_gate", [160, RW])
    k_k = din("k_k", [RW])
    k_a = din("k_a", [RW])
    r_k = din("r_k", [RW])
    lnx_g = din("lnx_g", [RW])
    lnx_b = din("lnx_b", [RW])
    conv_w = din("conv_w", [3, RW])
    w_out = din("w_out", [D, D])
    norm2_g = din("norm2_g", [D])
    w_ffn_gate = din("w_ffn_gate", [D, DFF])
    w_ffn_up = din("w_ffn_up", [D, DFF])
    w_ffn_down = din("w_ffn_down", [DFF, D])
    norm_f_g = din("norm_f_g", [D])
    ys = nc.dram_tensor("ys", [NSEQ, T, D], F32, kind="ExternalOutput").ap()

    wd_b = nc.dram_tensor("wd_b", [4, 4, 128, 11 * 512], BF16).ap()
    win_b = nc.dram_tensor("win_b", [NG, 128, KC * 128], BF16).ap()
    wg_b = nc.dram_tensor("wg_b", [FC, 128, KC * 128], BF16).ap()
    wu_b = nc.dram_tensor("wu_b", [FC, 128, KC * 128], BF16).ap()
    wout_b = nc.dram_tensor("wout_b", [4, 128, KC * 512], BF16).ap()
    mixT_d = nc.dram_tensor("mixT_d", [NSEQ, 16, 128, T], BF16).ap()

    def sb(name, shape, dt):
        return es.enter_context(nc.sbuf_tensor(name, list(shape), dt))

    id_f = sb("id_f", [128, 128], F32)
    id_b = sb("id_b", [128, 128], BF16)
    ones_f = sb("ones_f", [128, 128], F32)
    blk_f = sb("blk_f", [128, 128], F32)
    MA = [sb("MA%d" % d, [128, 512], F32) for d in range(2)]
    MX = [sb("MX%d" % d, [128, 256], F32) for d in range(2)]
    TRI = [sb("TRI%d" % d, [128, 256], F32) for d in range(2)]
    n1g = sb("n1g", [128, 16], F32)
    n2g = sb("n2g", [128, 16], F32)
    bada = sb("bada", [128, 96], F32)
    modT = sb("modT", [128, NSEQ, 96], F32)
    s1c = sb("s1c", [128, NSEQ, 16], F32)
    s2c = sb("s2c", [128, NSEQ, 16], F32)
    mu_c = sb("mu_c", [128, 27], F32)
    om_c = sb("om_c", [128, 27], F32)
    mue_c = sb("mue_c", [128, 27], F32)
    muo_c = sb("muo_c", [128, 27], F32)
    muag = sb("muag", [128, 4], F32)
    par8 = {nm: sb("p_" + nm, [128, 8], F32) for nm in ("a0", "k_k", "k_a", "r_k", "lnx_g", "lnx_b", "omka", "cw0", "cw1", "cw2")}
    evn = sb("evn", [128, 1], F32)
    mug0 = sb("mug0", [128, 4], F32)
    odd = sb("odd", [128, 1], F32)
    stat = sb("stat", [128, 8], F32)
    epsc = sb("epsc", [128, 2], F32)

    ARENA = 49800
    arena = sb("arena", [128, ARENA], F32)
    apos = [0]

    def carve(nbytes, dt, shape):
        n32 = (nbytes + 3) // 4
        n32 = (n32 + 7) // 8 * 8
        a = arena[:, apos[0]:apos[0] + n32]
        apos[0] += n32
        assert apos[0] <= ARENA, (apos[0], ARENA)
        if dt == BF16:
            a = a.bitcast(BF16)
        v = a
        if len(shape) == 3:
            v = a[:, 0:shape[1] * shape[2]].rearrange("p (a b) -> p a b", b=shape[2])
        elif len(shape) == 2:
            v = a[:, 0:shape[1]]
        return v

    PS = [es.enter_context(nc.psum_tensor("ps%d" % i, [128, 512], F32)) for i in range(8)]

    def psb(i):
        return ("ps", i)

    def pool_op(fn, r=(), w=()):
        return P.add("pool", fn, r, w)

    def mk_mask(ap, kind, tok="const"):
        mt = ("mask", id(ap), kind, len(P.ops))
        pool_op(lambda e: e.memset(ap, 1.0), w=(mt,))

        def f(e):
            if kind == "SL":
                return e.affine_select(out=ap, in_=ap, pattern=[[-1, 128]], compare_op=ALU.is_gt, fill=0.0, base=0, channel_multiplier=1)
            if kind == "SU":
                return e.affine_select(out=ap, in_=ap, pattern=[[1, 128]], compare_op=ALU.is_gt, fill=0.0, base=0, channel_multiplier=-1)
            if kind == "IU":
                return e.affine_select(out=ap, in_=ap, pattern=[[1, 128]], compare_op=ALU.is_ge, fill=0.0, base=0, channel_multiplier=-1)
            if kind == "IL":
                return e.affine_select(out=ap, in_=ap, pattern=[[-1, 128]], compare_op=ALU.is_ge, fill=0.0, base=0, channel_multiplier=1)
        pool_op(f, r=(mt,), w=(mt, tok))

    pool_op(lambda e: e.memset(id_f[:], 0.0), w=("id_f0",))
    pool_op(lambda e: e.affine_select(out=id_f[:], in_=id_f[:], pattern=[[-1, 128]], compare_op=ALU.not_equal, fill=1.0, base=0, channel_multiplier=1),
            r=("id_f0",), w=("id_f0", "const"))

    def f_ident(e):
        e.memset(ones_f[:], 1.0)
        e.memset(epsc[:, 0:1], RMS_EPS)
        return e.memset(epsc[:, 1:2], LNX_EPS)
    pool_op(f_ident, w=("const_b",))
    pool_op(lambda e: e.memset(blk_f[:], 0.0), w=("blk0",))

    def f_blk(e):
        e.memset(blk_f[0:64, 0:64], 1.0)
        return e.memset(blk_f[64:128, 64:128], 1.0)
    pool_op(f_blk, r=("blk0",), w=("blk0", "const_c"))
    pool_op(lambda e: e.tensor_copy(out=id_b[:], in_=id_f[:]), r=("const",), w=("const2",))
    for d, (ks, ki, kx) in enumerate((("SU", "IU", "SL"), ("SL", "IL", "SU"))):
        mk_mask(MA[d][:, 0:128], ks); mk_mask(MA[d][:, 128:256], ks)
        mk_mask(MA[d][:, 256:384], ki); mk_mask(MA[d][:, 384:512], ki)
        mk_mask(MX[d][:, 0:128], kx); mk_mask(MX[d][:, 128:256], kx)
    mk_mask(TRI[0][:, 0:128], "IU", tok=("tri", 0)); mk_mask(TRI[0][:, 128:256], "SU", tok=("tri", 0))
    mk_mask(TRI[1][:, 0:128], "IL", tok=("tri", 1)); mk_mask(TRI[1][:, 128:256], "SL", tok=("tri", 1))
    pool_op(lambda e: e.tensor_scalar(out=TRI[0][0:64, :], in0=TRI[0][0:64, :], scalar1=-1.0, scalar2=None, op0=ALU.add), r=(("tri", 0),), w=(("tri", 0),))
    pool_op(lambda e: e.tensor_scalar(out=TRI[1][64:128, :], in0=TRI[1][64:128, :], scalar1=-1.0, scalar2=None, op0=ALU.add), r=(("tri", 1),), w=(("tri", 1),))
    pool_op(lambda e: e.tensor_scalar(out=TRI[0][:], in0=TRI[0][:], scalar1=C0, scalar2=None, op0=ALU.mult), r=(("tri", 0),), w=(("tri", 0),))
    pool_op(lambda e: e.tensor_scalar(out=TRI[1][:], in0=TRI[1][:], scalar1=C0, scalar2=None, op0=ALU.mult), r=(("tri", 1),), w=(("tri", 1), "const3"))

    def f_par(e):
        idv = id_f[:].rearrange("p (a b) -> p a b", b=2)
        e.tensor_reduce(out=evn[:], in_=idv[:, :, 0], axis=AX.X, op=ALU.add)
        return e.tensor_reduce(out=odd[:], in_=idv[:, :, 1], axis=AX.X, op=ALU.add)
    P.add("dve", f_par, r=("const",), w=("const4",))

    def early(k):
        if stage == k:
            P.finish()
            P.emit(nc, es)
            es.close()
            return True
        return False
    if early(0):
        return nc
    pcn = {"i": 0}
    PCD = 4

    def precast(dst, src, tok, chan):
        k = pcn["i"] % PCD
        pcn["i"] += 1
        P.add("pool", lambda e: e.dma_start(out=dst, in_=src), r=(), w=(tok, ("pcring", k)), chan="pcs%d" % k)

    for g, parts in enumerate(GROUPS):
        dstg = win_b[g].rearrange("p (kc m) -> p kc m", m=128)
        for (c0, wd, d0) in parts:
            src = w_in[:, c0:c0 + wd].rearrange("(kc p) m -> p kc m", p=128)
            precast(dstg[:, :, d0:d0 + wd], src, ("scr", "win", g, d0), "pc_win")

    if early(1):
        return nc
    stg = sb("stg", [128, 128], F32)

    def load_cols(vec, n, dst_cols, tag):
        rows = n // 128
        rem = n - rows * 128
        nr = rows + (1 if rem else 0)
        P.add("pool", lambda e: e.memset(stg[:], 0.0), w=("stg",))
        if rows:
            P.add("sp", lambda e: e.dma_start(out=stg[0:rows, :], in_=vec[0:rows * 128].rearrange("(r c) -> r c", c=128)),
                  w=("stg",), chan="misc")
        if rem:
            P.add("sp", lambda e: e.dma_start(out=stg[rows:rows + 1, 0:rem], in_=vec[rows * 128:n].rearrange("(r c) -> r c", r=1)),
                  w=("stg",), chan="misc")
        P.add("pe", lambda e: e.transpose(PS[0][:, 0:128], stg[:], id_f[:]), r=("stg", "const"), w=(psb(0),))
        P.add("dve", lambda e: e.tensor_copy(out=dst_cols, in_=PS[0][:, 0:nr]), r=(psb(0),), w=("par", tag))

    load_cols(norm1_g, D, n1g[:], "n1g")
    load_cols(norm2_g, D, n2g[:], "n2g")
    load_cols(b_ada, NMOD * D, bada[:], "bada")
    load_cols(mu_shift, RWC, mu_c[:], "mu")
    for nm, v in (("a0", a0), ("k_k", k_k), ("k_a", k_a), ("r_k", r_k), ("lnx_g", lnx_g), ("lnx_b", lnx_b)):
        load_cols(v, RW, par8[nm][:], nm)
    for j in range(3):
        load_cols(conv_w[j], RW, par8["cw%d" % j][:], "cw%d" % j)
    P.add("pool", lambda e: e.memset(muag[:], 0.0), w=("muag",))
    P.add("sp", lambda e: e.dma_start(out=muag[0:64, 0:1], in_=mu_shift[3200:3264].rearrange("(p o) -> p o", o=1), allow_slow_non_contiguous=True),
          w=("muag",), chan="c_muag")
    P.add("sp", lambda e: e.dma_start(out=muag[64:96, 0:1], in_=mu_shift[3392:3424].rearrange("(p o) -> p o", o=1), allow_slow_non_contiguous=True),
          w=("muag",), chan="c_muag")

    P.add("sp", lambda e: e.dma_start(out=mug0[:, 0:1], in_=mu_shift[3264:3392].rearrange("(p o) -> p o", o=1), allow_slow_non_contiguous=True),
          w=("mug0",), chan="c_mug0")

    def f_mu(e):
        e.tensor_scalar(out=mug0[:, 1:2], in0=mug0[:, 0:1], scalar1=-1.0, scalar2=1.0, op0=ALU.mult, op1=ALU.add)
        e.tensor_scalar(out=mug0[:, 2:3], in0=mug0[:, 0:1], scalar1=evn[:, 0:1], scalar2=None, op0=ALU.mult)
        e.tensor_scalar(out=mug0[:, 3:4], in0=mug0[:, 0:1], scalar1=odd[:, 0:1], scalar2=None, op0=ALU.mult)
        e.tensor_scalar(out=om_c[:], in0=mu_c[:], scalar1=-1.0, scalar2=1.0, op0=ALU.mult, op1=ALU.add)
        e.tensor_scalar(out=mue_c[:], in0=mu_c[:], scalar1=evn[:, 0:1], scalar2=None, op0=ALU.mult)
        e.tensor_scalar(out=muo_c[:], in0=mu_c[:], scalar1=odd[:, 0:1], scalar2=None, op0=ALU.mult)
        e.tensor_scalar(out=muag[:, 1:2], in0=muag[:, 0:1], scalar1=-1.0, scalar2=1.0, op0=ALU.mult, op1=ALU.add)
        e.tensor_scalar(out=muag[:, 2:3], in0=muag[:, 0:1], scalar1=evn[:, 0:1], scalar2=None, op0=ALU.mult)
        e.tensor_scalar(out=muag[:, 3:4], in0=muag[:, 0:1], scalar1=odd[:, 0:1], scalar2=None, op0=ALU.mult)
        return e.tensor_scalar(out=par8["omka"][:], in0=par8["k_a"][:], scalar1=-1.0, scalar2=1.0, op0=ALU.mult, op1=ALU.add)
    P.add("dve", f_mu, r=(("par", "mu"), ("par", "k_a"), "muag", "mug0", "const3", "const4"), w=("mud",))

    if early(2):
        return nc
    apos[0] = 0
    cT = carve(KC * NSEQ * 2, BF16, [128, KC, NSEQ])
    crow = carve(D * 4, F32, [128, D])
    WA_G = 512
    wa = [carve(KC * WA_G * 2, BF16, [128, KC, WA_G]) for _ in range(2)]
    P.add("sp", lambda e: e.dma_start(out=crow[0:NSEQ, :], in_=cs), w=("crow",), chan="c_crow")
    for kc in range(KC):
        P.add("pe", lambda e, kc=kc: e.transpose(PS[1][:, kc * NSEQ:(kc + 1) * NSEQ], crow[0:NSEQ, kc * 128:(kc + 1) * 128], id_f[0:NSEQ, 0:NSEQ]),
              r=("crow", "const"), w=(psb(1),))
    P.add("act", lambda e: e.activation(out=cT[:].rearrange("p a b -> p (a b)"), in_=PS[1][:, 0:KC * NSEQ], func=AF.Silu), r=(psb(1),), w=("cT",))
    NWA = NMOD * D // WA_G
    for gi in range(NWA):
        slot = gi % 2
        P.add("pool", lambda e, gi=gi, slot=slot: e.dma_start(
            out=wa[slot][:], in_=w_ada[:, gi * WA_G:(gi + 1) * WA_G].rearrange("(kc p) m -> p kc m", p=128)),
            w=(("wa", slot),), chan="wa%d" % slot)
        bank = 2 + (gi % 2)
        for jj in range(WA_G // 128):
            for kc in range(KC):
                P.add("pe", lambda e, jj=jj, kc=kc, slot=slot, bank=bank: e.matmul(
                    PS[bank][:, jj * NSEQ:(jj + 1) * NSEQ], wa[slot][:, kc, jj * 128:(jj + 1) * 128], cT[:, kc, :],
                    start=(kc == 0), stop=(kc == KC - 1)), r=(("wa", slot), "cT"), w=(psb(bank),))
        nj = WA_G // 128
        for s in range(NSEQ):
            P.add("dve", lambda e, gi=gi, s=s, bank=bank, nj=nj: e.tensor_tensor(
                out=modT[:, s, gi * nj:(gi + 1) * nj],
                in0=PS[bank][:, 0:nj * NSEQ].rearrange("p (j s) -> p j s", s=NSEQ)[:, :, s],
                in1=bada[:, gi * nj:(gi + 1) * nj], op=ALU.add), r=(psb(bank), ("par", "bada")), w=("modT",))

    def f_s12(e):
        for s in range(NSEQ):
            e.scalar_tensor_tensor(out=s1c[:, s, :], in0=modT[:, s, 16:32], scalar=1.0, in1=n1g[:], op0=ALU.add, op1=ALU.mult)
            r_ = e.scalar_tensor_tensor(out=s2c[:, s, :], in0=modT[:, s, 64:80], scalar=1.0, in1=n2g[:], op0=ALU.add, op1=ALU.mult)
        return r_
    P.add("dve", f_s12, r=("modT", ("par", "n1g"), ("par", "n2g")), w=("s12",))

    if early(3):
        return nc
    for dg in range(4):
        dst = wout_b[dg].rearrange("p (kc n) -> p kc n", n=512)
        for half in range(2):
            src = w_out[half * 1024:(half + 1) * 1024, dg * 512:(dg + 1) * 512].rearrange("(kc p) n -> p kc n", p=128)
            precast(dst[:, half * 8:(half + 1) * 8, :], src, ("scr", "wout", dg, half), "pc_wout")
    if early(31):
        return nc
    import os as _os
    for fb in range(0 if _os.environ.get("SKIPGU") is None else FC, FC):
        for (wsrc, wdst, nm) in ((w_ffn_gate, wg_b, "wg"), (w_ffn_up, wu_b, "wu")):
            src = wsrc[:, fb * 128:(fb + 1) * 128].rearrange("(kc p) m -> p kc m", p=128)
            precast(wdst[fb].rearrange("p (kc m) -> p kc m", m=128), src, ("scr", nm, fb), "pc_" + nm)
    if early(32):
        return nc
    for dg in range(4):
        for fq in range(4):
            src = w_ffn_down[fq * 1408:(fq + 1) * 1408, dg * 512:(dg + 1) * 512].rearrange("(fc p) n -> p fc n", p=128)
            precast(wd_b[dg, fq].rearrange("p (fc n) -> p fc n", n=512), src, ("scr", "wd", dg, fq), "pc_wd")

    P.barrier()
    if early(4):
        return nc

    rr = {"i": 0}

    def evac_eng():
        import os
        if os.environ.get("EVAC"):
            return os.environ["EVAC"]
        return "dve"

    def rstd_from_ss(ss_col, out_col, scale, eps_idx):
        P.add("act", lambda e: e.activation(out=out_col, in_=ss_col, func=AF.Sqrt, bias=epsc[:, eps_idx:eps_idx + 1], scale=scale),
              r=("stat",), w=("stat",))
        P.add("dve", lambda e: e.reciprocal(out=out_col, in_=out_col), r=("stat",), w=("stat",))

    def norm_transpose(src_row, rtok, s_cols, sh_cols, dstT, dtoks, ttok, w_tok_fn, bank0=4):
        xn, xn_tok, junk = ttok
        P.add("act", lambda e: e.activation(out=junk, in_=src_row, func=AF.Square, accum_out=stat[:, 0:1]), r=(rtok,), w=("stat", "junk"))
        rstd_from_ss(stat[:, 0:1], stat[:, 1:2], 1.0 / D, 0)
        if stage == 81 and bank0 == 6:
            return
        P.add("dve", lambda e: e.tensor_scalar(out=xn, in0=src_row, scalar1=stat[:, 1:2], scalar2=None, op0=ALU.mult), r=(rtok, "stat"), w=(xn_tok,))
        if stage == 82 and bank0 == 6:
            return
        for kq in range(4):
            bank = bank0 + (kq % 2)
            for k4 in range(4):
                kc = kq * 4 + k4
                P.add("pe", lambda e, kc=kc, k4=k4, bank=bank: e.transpose(PS[bank][:, k4 * 128:(k4 + 1) * 128], xn[:, kc * 128:(kc + 1) * 128], id_f[:]),
                      r=(xn_tok,), w=(psb(bank),))
            if stage == 83 and bank0 == 6:
                continue
            for k4 in range(4):
                kc = kq * 4 + k4
                eng = evac_eng()
                if eng == "dve":
                    P.add("dve", lambda e, kc=kc, k4=k4, bank=bank: e.tensor_scalar(
                        out=dstT[:, kc, dtoks], in0=PS[bank][:, k4 * 128:(k4 + 1) * 128], scalar1=s_cols[:, kc:kc + 1], scalar2=sh_cols[:, kc:kc + 1],
                        op0=ALU.mult, op1=ALU.add), r=(psb(bank),), w=(w_tok_fn(kc),))
                else:
                    P.add("act", lambda e, kc=kc, k4=k4, bank=bank: e.activation(
                        out=dstT[:, kc, dtoks], in_=PS[bank][:, k4 * 128:(k4 + 1) * 128], func=AF.Identity,
                        bias=sh_cols[:, kc:kc + 1], scale=s_cols[:, kc:kc + 1]), r=(psb(bank),), w=(w_tok_fn(kc),))

    def chk(n):
        if stage == n:
            raise StopIteration

    import os as _os2
    for _i in range(int(_os2.environ.get("PEPAD", "0"))):
        P.add("pe", lambda e: e.matmul(PS[7][:, 0:128], id_b[:], id_b[:], start=True, stop=True), w=(psb(7),))

    def do_seq(s):
        apos[0] = 0
        hn1T = carve(KC * T * 2, BF16, [128, KC, T])
        a_mark = apos[0]
        xrow = [carve(D * 4, F32, [128, D]) for _ in range(2)]
        xn = carve(D * 4, F32, [128, D])
        junk = carve(D * 2, BF16, [128, D])
        for ti in range(NT):
            slot = ti % 2
            P.add("sp", lambda e, ti=ti, slot=slot: e.dma_start(out=xrow[slot][:], in_=xs[s, ti * 128:(ti + 1) * 128, :]),
                  w=(("xrow", slot),), chan="xr%d" % slot)
            norm_transpose(xrow[slot][:], ("xrow", slot), s1c[:, s, :], modT[:, s, 0:16], hn1T, slice(ti * 128, (ti + 1) * 128),
                           (xn[:], "xn", junk[:]), lambda kc, ti=ti: ("hn1T", ti // 4))
        P.barrier()
        if stage == 5:
            raise StopIteration

        apos[0] = a_mark
        rawp = carve((T + 2) * 4, F32, [128, T + 2])
        tA = carve(T * 4, F32, [128, T])
        tB = carve(T * 4, F32, [128, T])
        yacc = carve(T * 4, F32, [128, T])
        r_bf = carve(T * 2, BF16, [128, T])
        k_bf = carve(T * 2, BF16, [128, T])
        v_bf = carve(T * 2, BF16, [128, T])
        ka_bf = carve(T * 2, BF16, [128, T])
        nb_bf = carve(T * 2, BF16, [128, T])
        mixo = carve(T * 2, BF16, [128, T])
        lo_w = carve(T * 2, BF16, [128, T])
        lo_a = carve(T * 2, BF16, [128, T])
        lo_g = carve(T * 2, BF16, [128, T])
        w2b = carve(RW * 2, BF16, [128, RW])
        a2g = carve(RW * 2, BF16, [128, RW])
        g2a = carve(RW * 2, BF16, [128, RW])
        w0b = carve(256 * 4, F32, [128, 256])
        LW = [tA[:, 0:NT * 128].rearrange("p (a b) -> p a b", b=128), rawp[:, 0:NT * 128].rearrange("p (a b) -> p a b", b=128)]
        LWT = ["tA", "rawp"]
        vpad = carve(NT * 256 * 2, BF16, [128, NT, 256])
        win_t = [carve(KC * 128 * 2, BF16, [128, KC, 128]) for _ in range(2)]
        UB = {}
        for d in range(2):
            UB[d] = dict(
                E1=[carve(256 * 4, F32, [128, 256]) for _ in range(2)], Em=[carve(128 * 4, F32, [128, 128]) for _ in range(2)],
                fm=[carve(512 * 2, BF16, [128, 4, 128]) for _ in range(2)],
                rh=[carve(256 * 2, BF16, [128, 2, 128]) for _ in range(2)],
                kh=[carve(256 * 2, BF16, [128, 2, 128]) for _ in range(2)],
                tk=[carve(768 * 2, BF16, [128, 3, 256]) for _ in range(2)],
                A=carve(1024 * 2, BF16, [128, 2, 512]),
                X0=carve(256 * 2, BF16, [128, 2, 128]),
                XZ=[carve(512 * 2, BF16, [128, 4, 128]) for _ in range(2)],
                Pm=[carve(256 * 2, BF16, [128, 2, 128]) for _ in range(2)],
                akv=carve(256 * 2, BF16, [128, 256]), wtT=carve(128 * 2, BF16, [128, 128]),
                upad=carve(256 * 2, BF16, [128, 256]),
            )
        HN = [carve(128 * 4, F32, [128, 128]) for _ in range(2)]
        HB = [carve(128 * 2, BF16, [128, 128]) for _ in range(2)]
        GC = [carve(4 * 4, F32, [128, 4]) for _ in range(2)]
        tq = [carve(512 * 4, F32, [128, 512]) for _ in range(2)]

        def zero_pads(e):
            e.memset(rawp[:, 0:1], 0.0)
            return e.memset(rawp[:, T + 1:T + 2], 0.0)

        def zero_padded(e):
            e.memset(vpad[:].rearrange("p a b -> p (a b)"), 0.0)
            for d in range(2):
                for sl in range(2):
                    e.memset(UB[d]["tk"][sl][:].rearrange("p a b -> p (a b)"), 0.0)
                    e.memset(UB[d]["rh"][sl][:].rearrange("p a b -> p (a b)"), 0.0)
                    e.memset(UB[d]["kh"][sl][:].rearrange("p a b -> p (a b)"), 0.0)
                e.memset(UB[d]["akv"][:], 0.0)
                r_ = e.memset(UB[d]["upad"][:], 0.0)
            return r_
        P.add("pool", zero_padded, w=("vpad",) + tuple(("ub", d, nm) for d in range(2) for nm in ("akv", "upad"))
              + tuple(("ub", d, nm, sl) for d in range(2) for nm in ("tk", "fmh") for sl in range(2)))

        P.add("pool", lambda e: e.dma_start(out=w2b[0:64, :], in_=w2_decay[0]), w=(("lw_w", 0),), chan="lora")
        P.add("pool", lambda e: e.dma_start(out=w2b[64:128, :], in_=w2_decay[1]), w=(("lw_w", 1),), chan="lora")
        P.add("pool", lambda e: e.dma_start(out=a2g[0:64, :], in_=a2), w=(("lw_w", 2),), chan="lora")
        P.add("pool", lambda e: e.dma_start(out=a2g[64:96, :], in_=g2_gate[128:160, :]), w=(("lw_w", 3),), chan="lora")
        P.add("pool", lambda e: e.dma_start(out=g2a[:], in_=g2_gate[0:128, :]), w=(("lw_w", 4),), chan="lora")
        LWW = tuple(("lw_w", i) for i in range(5))

        wslot = {"i": 0}

        def project(g, consume):
            slot = wslot["i"] % 2
            wslot["i"] += 1
            rt = tuple(("scr", "win", g, d0) for (_, _, d0) in GROUPS[g])
            P.add("sp", lambda e: e.dma_start(out=win_t[slot][:].rearrange("p a b -> p (a b)"), in_=win_b[g]),
                  r=rt, w=(("win_t", slot),), chan="win%d" % slot)
            for q in range(NQ):
                bank = q % 4
                for kc in range(KC):
                    P.add("pe", lambda e, kc=kc, q=q, bank=bank: e.matmul(PS[bank][:], win_t[slot][:, kc, :], hn1T[:, kc, q * 512:(q + 1) * 512],
                                                                         start=(kc == 0), stop=(kc == KC - 1)),
                          r=(("win_t", slot),), w=(psb(bank),))
                consume(q, PS[bank][:], psb(bank))

        def to_raw(q, ps, ptok):
            eng = evac_eng()
            dst = rawp[:, 1 + q * 512:1 + (q + 1) * 512]
            if eng == "dve":
                P.add("dve", lambda e: e.tensor_copy(out=dst, in_=ps), r=(ptok,), w=("rawp",))
            else:
                P.add("act", lambda e: e.activation(out=dst, in_=ps, func=AF.Copy), r=(ptok,), w=("rawp",))

        def shift(dst, dst_tok, om, mue, muo, np_=128, p0=0, post=None):
            ps_ = slice(p0, p0 + np_)
            tmp = tA[ps_, :]
            P.add("pool", zero_pads, r=("rawp",), w=("rawp",))
            P.add("act", lambda e: e.activation(out=tmp, in_=rawp[ps_, 1:T + 1], func=AF.Identity, scale=om[ps_, :]), r=("rawp",), w=("tA",))
            P.add("dve", lambda e: e.scalar_tensor_tensor(out=tmp, in0=rawp[ps_, 0:T], scalar=mue[ps_, :], in1=tmp, op0=ALU.mult, op1=ALU.add),
                  r=("rawp", "tA"), w=("tA",))
            if post is None:
                P.add("dve", lambda e: e.scalar_tensor_tensor(out=dst, in0=rawp[ps_, 2:T + 2], scalar=muo[ps_, :], in1=tmp, op0=ALU.mult, op1=ALU.add),
                      r=("rawp", "tA"), w=(dst_tok,))
            else:
                P.add("dve", lambda e: e.scalar_tensor_tensor(out=tmp, in0=rawp[ps_, 2:T + 2], scalar=muo[ps_, :], in1=tmp, op0=ALU.mult, op1=ALU.add),
                      r=("rawp", "tA"), w=("tA",))
                P.add("act", lambda e: e.activation(out=dst, in_=tmp, func=post), r=("tA",), w=(dst_tok,))

        def do_hp(hp):
            hc = slice(hp * 128, (hp + 1) * 128)
            project(hp, to_raw)
            shift(r_bf[:], "r_bf", om_c[:, hp:hp + 1], mue_c[:, hp:hp + 1], muo_c[:, hp:hp + 1])
            project(16 + hp, to_raw)
            shift(v_bf[:], "v_bf", om_c[:, 16 + hp:17 + hp], mue_c[:, 16 + hp:17 + hp], muo_c[:, 16 + hp:17 + hp])
            project(8 + hp, to_raw)
            shift(tB[:], "tB", om_c[:, 8 + hp:9 + hp], mue_c[:, 8 + hp:9 + hp], muo_c[:, 8 + hp:9 + hp])
            chk(21)
            for q in range(NQ):
                bank = 4 + q % 2
                P.add("pe", lambda e, q=q, bank=bank: e.matmul(PS[bank][:], a2g[0:64, hc], lo_a[0:64, q * 512:(q + 1) * 512], start=True, stop=True),
                      r=LWW + ("lo_a",), w=(psb(bank),))
                P.add("act", lambda e, q=q, bank=bank: e.activation(out=yacc[:, q * 512:(q + 1) * 512], in_=PS[bank][:], func=AF.Sigmoid,
                                                                      bias=par8["a0"][:, hp:hp + 1]), r=(psb(bank),), w=("yacc",))
            P.add("dve", lambda e: e.tensor_scalar(out=tA[:], in0=tB[:], scalar1=par8["k_k"][:, hp:hp + 1], scalar2=None, op0=ALU.mult),
                  r=("tB",), w=("tA",))
            for q in range(NQ):
                qs = slice(q * 512, (q + 1) * 512)
                bank = 4 + q % 2
                tqq = tq[q % 2]
                tqt = ("tq", q % 2)
                P.add("pool", lambda e, qs=qs, tqq=tqq: e.tensor_tensor(out=tqq[:], in0=tA[:, qs], in1=tA[:, qs], op=ALU.mult), r=("tA",), w=(tqt,))
                P.add("pe", lambda e, tqq=tqq, bank=bank: e.matmul(PS[bank][:], blk_f[:], tqq[:], start=True, stop=True), r=(tqt,), w=(psb(bank),))
                P.add("act", lambda e, tqq=tqq, bank=bank: e.activation(out=tqq[:], in_=PS[bank][:], func=AF.Sqrt), r=(psb(bank),), w=(tqt,))
                P.add("dve", lambda e, tqq=tqq: e.tensor_scalar(out=tqq[:], in0=tqq[:], scalar1=1e-12, scalar2=None, op0=ALU.max), r=(tqt,), w=(tqt,))
                P.add("dve", lambda e, tqq=tqq: e.reciprocal(out=tqq[:], in_=tqq[:]), r=(tqt,), w=(tqt,))
                P.add("dve", lambda e, qs=qs, tqq=tqq: e.tensor_tensor(out=tA[:, qs], in0=tA[:, qs], in1=tqq[:], op=ALU.mult), r=("tA", tqt), w=("tA",))
            P.add("act", lambda e: e.activation(out=ka_bf[:], in_=tA[:], func=AF.Copy), r=("tA",), w=("ka_bf",))
            P.add("dve", lambda e: e.scalar_tensor_tensor(out=nb_bf[:], in0=tA[:], scalar=-1.0, in1=yacc[:], op0=ALU.mult, op1=ALU.mult),
                  r=("tA", "yacc"), w=("nb_bf",))
            P.add("dve", lambda e: e.tensor_scalar(out=yacc[:], in0=yacc[:], scalar1=par8["k_a"][:, hp:hp + 1], scalar2=par8["omka"][:, hp:hp + 1],
                                                   op0=ALU.mult, op1=ALU.add), r=("yacc",), w=("yacc",))
            P.add("pool", lambda e: e.tensor_tensor(out=k_bf[:], in0=tB[:], in1=yacc[:], op=ALU.mult), r=("tB", "yacc"), w=("k_bf",))
            chk(22)
            P.add("dve", lambda e: e.scalar_tensor_tensor(out=tA[:], in0=r_bf[:], scalar=par8["r_k"][:, hp:hp + 1], in1=k_bf[:], op0=ALU.mult, op1=ALU.mult),
                  r=("r_bf", "k_bf", "tA"), w=("tA",))
            for q in range(NQ):
                qs = slice(q * 512, (q + 1) * 512)
                bank = 4 + q % 2
                P.add("pe", lambda e, qs=qs, bank=bank: e.matmul(PS[bank][:], blk_f[:], tA[:, qs], start=True, stop=True), r=("tA",), w=(psb(bank),))
                P.add("dve", lambda e, qs=qs, bank=bank: e.tensor_tensor(out=tB[:, qs], in0=PS[bank][:], in1=v_bf[:, qs], op=ALU.mult),
                      r=(psb(bank), "v_bf", "tB"), w=("tB",))
            chk(23)
            for ti in range(NT):
                bank = 4 + ti % 2
                pst = PS[bank][:].bitcast(BF16)
                P.add("pe", lambda e, ti=ti, pst=pst: e.transpose(pst[:, 0:128], v_bf[:, ti * 128:(ti + 1) * 128], id_b[:]), r=("v_bf",), w=(psb(bank),))
                eng = evac_eng()
                if eng == "dve":
                    P.add("dve", lambda e, ti=ti, pst=pst: e.tensor_copy(out=vpad[:, ti, 0:64], in_=pst[:, 0:64]), r=(psb(bank),), w=("vpad",))
                    P.add("dve", lambda e, ti=ti, pst=pst: e.tensor_copy(out=vpad[:, ti, 192:256], in_=pst[:, 64:128]), r=(psb(bank),), w=("vpad",))
                else:
                    P.add("act", lambda e, ti=ti, pst=pst: e.activation(out=vpad[:, ti, 0:64], in_=pst[:, 0:64], func=AF.Copy), r=(psb(bank),), w=("vpad",))
                    P.add("act", lambda e, ti=ti, pst=pst: e.activation(out=vpad[:, ti, 192:256], in_=pst[:, 64:128], func=AF.Copy), r=(psb(bank),), w=("vpad",))
            chk(24)
            for d in range(2):
                P.add("sp", lambda e, d=d: e.dma_start(out=w0b[:, d * 128:(d + 1) * 128], in_=w0_decay[d, hc].partition_broadcast(128)),
                      w=(("w0b", d),), chan="w0b")
            for d in range(2):
                dr = slice(d * 64, (d + 1) * 64)
                for t4 in range(NT // 4):
                    bank = 4 + t4 % 2
                    for j in range(4):
                        ti = t4 * 4 + j
                        P.add("pe", lambda e, ti=ti, j=j, bank=bank, dr=dr: e.matmul(PS[bank][:, j * 128:(j + 1) * 128], lo_w[dr, ti * 128:(ti + 1) * 128], w2b[dr, hc],
                                                                                   start=True, stop=True), r=("lo_w",) + LWW, w=(psb(bank),))
                    for j in range(4):
                        ti = t4 * 4 + j
                        P.add("dve", lambda e, ti=ti, j=j, bank=bank, d=d: e.tensor_tensor(out=LW[d][:, ti, :], in0=PS[bank][:, j * 128:(j + 1) * 128],
                                                                                        in1=w0b[:, d * 128:(d + 1) * 128], op=ALU.add),
                              r=(psb(bank), ("w0b", d)), w=(LWT[d],))
                P.add("act", lambda e, d=d: e.activation(out=LW[d][:].rearrange("p a b -> p (a b)"), in_=LW[d][:].rearrange("p a b -> p (a b)"), func=AF.Sigmoid),
                      r=(LWT[d],), w=(LWT[d],))

            chk(25)
            def zero_state(e):
                for d in range(2):
                    e.memset(HN[d][:], 0.0)
                    e.memset(GC[d][:], 1.0)
                    r_ = e.memset(HB[d][:], 0.0)
                return r_
            P.add("pool", zero_state, w=(("HN", 0), ("HN", 1), ("HB", 0), ("HB", 1), ("GC", 0, 0), ("GC", 0, 1), ("GC", 1, 0), ("GC", 1, 1)))

            def unit_stages(d, step):
                ti = step if d == 0 else NT - 1 - step
                U = UB[d]
                E1 = U["E1"][step % 2]
                E1p = U["E1"][(step - 1) % 2]
                e1t = ("ub", d, "E1", step % 2)
                e1pt = ("ub", d, "E1", (step - 1) % 2)
                kb = ("ub", d)
                b0, b1, b2, b3 = 4 * d, 4 * d + 1, 4 * d + 2, 4 * d + 3
                ts_ = slice(ti * 128, (ti + 1) * 128)
                endcol = 127 if d == 0 else 0
                startcol = 128 + (0 if d == 0 else 127)
                st = []
                sl = step % 2
                fm = U["fm"][sl]
                tkk = U["tk"][sl]
                Em = U["Em"][sl]
                rhh = U["rh"][sl]
                khh = U["kh"][sl]
                FMT = kb + ("fm", sl)
                TKT = kb + ("tk", sl)
                FHT = kb + ("fmh", sl)
                EMT = kb + ("Em", sl)
                gcol = 2 * sl
                GCT = ("GC", d, sl)

                def s_prep():
                    P.add("pe", lambda e: e.matmul(PS[b3][:, 0:256], LW[d][:, ti, :], TRI[d][:], start=True, stop=True), r=(LWT[d],), w=(psb(b3),))
                    P.add("act", lambda e: e.activation(out=E1[:], in_=PS[b3][:, 0:256], func=AF.Exp), r=(psb(b3),), w=(e1t,))
                    P.add("act", lambda e: e.activation(out=Em[:], in_=PS[b3][:, 0:128], func=AF.Exp, scale=-1.0), r=(psb(b3),), w=(EMT,))
                    if step > 0:
                        P.add("dve", lambda e: e.reciprocal(out=GC[d][:, gcol + 1:gcol + 2], in_=E1[:, startcol:startcol + 1]), r=(e1t,), w=(("GCt", d, sl),))
                        P.add("dve", lambda e: e.tensor_tensor(out=GC[d][:, gcol:gcol + 1], in0=E1p[:, endcol:endcol + 1], in1=GC[d][:, gcol + 1:gcol + 2], op=ALU.mult),
                              r=(e1pt, ("GCt", d, sl)), w=(GCT,))
                    P.add("dve", lambda e: e.tensor_tensor(out=fm[:, 0, :], in0=r_bf[:, ts_], in1=E1[:, 0:128], op=ALU.mult), r=("r_bf", e1t), w=(FMT,))
                    P.add("pool", lambda e: e.tensor_tensor(out=fm[:, 1, :], in0=k_bf[:, ts_], in1=Em[:], op=ALU.mult), r=("k_bf", EMT), w=(FMT,))
                    P.add("pool", lambda e: e.tensor_tensor(out=fm[:, 2, :], in0=nb_bf[:, ts_], in1=Em[:], op=ALU.mult), r=("nb_bf", EMT), w=(FMT,))
                    P.add("dve", lambda e: e.tensor_tensor(out=fm[:, 3, :], in0=ka_bf[:, ts_], in1=E1[:, 128:256], op=ALU.mult), r=("ka_bf", e1t), w=(FMT,))
                    for h in range(2):
                        hs = slice(h * 64, (h + 1) * 64)
                        P.add("pool", lambda e, h=h, hs=hs: e.tensor_copy(out=rhh[hs, h, :], in_=fm[hs, 0, :]), r=(FMT,), w=(FHT,))
                        P.add("pool", lambda e, h=h, hs=hs: e.tensor_copy(out=khh[hs, h, :], in_=fm[hs, 3, :]), r=(FMT,), w=(FHT,))
                    pst = PS[b3][:].bitcast(BF16)
                    for m in range(3):
                        P.add("pe", lambda e, m=m: e.transpose(pst[:, 512 + m * 128:512 + (m + 1) * 128], fm[:, 1 + m, :], id_b[:]), r=(FMT,), w=(psb(b3),))
                    for m in range(3):
                        eng = evac_eng()
                        for h in range(2):
                            src = pst[:, 512 + m * 128 + h * 64:512 + m * 128 + (h + 1) * 64]
                            dst = tkk[:, m, h * 192:h * 192 + 64]
                            if eng == "dve":
                                P.add("dve", lambda e, src=src, dst=dst: e.tensor_copy(out=dst, in_=src), r=(psb(b3),), w=(TKT,))
                            else:
                                P.add("act", lambda e, src=src, dst=dst: e.activation(out=dst, in_=src, func=AF.Copy), r=(psb(b3),), w=(TKT,))
                st.append(s_prep)

                def s_A():
                    for h in range(2):
                        hs = slice(h * 64, (h + 1) * 64)
                        bank = b0 + h
                        for j, (li, rsrc) in enumerate(((2, "kh"), (1, "kh"), (1, "rh"), (2, "rh"))):
                            P.add("pe", lambda e, j=j, li=li, rsrc=rsrc, h=h, bank=bank: e.matmul(PS[bank][:, j * 128:(j + 1) * 128], fm[:, li, :], (khh if rsrc == "kh" else rhh)[:, h, :], start=True, stop=True),
                                  r=(FMT, FHT), w=(psb(bank),))
                        P.add("pe", lambda e, h=h: e.matmul(PS[b2][:, h * 128:(h + 1) * 128], khh[:, h, :], fm[:, 2, :], start=True, stop=True),
                              r=(FMT, FHT), w=(psb(b2),))
                    if step == 0 and d == 0:
                        chk(201)
                    for h in range(2):
                        P.add("dve", lambda e, h=h: e.tensor_tensor(out=U["A"][:, h, :], in0=PS[b0 + h][:], in1=MA[d][:], op=ALU.mult), r=(psb(b0 + h),), w=(kb + ("A",),))
                    P.add("dve", lambda e: e.tensor_tensor(out=U["X0"][:].rearrange("p a b -> p (a b)"), in0=PS[b2][:, 0:256], in1=MX[d][:], op=ALU.mult),
                          r=(psb(b2),), w=(kb + ("X0",),))
                    if step == 0 and d == 0:
                        chk(202)
                    for h in range(2):
                        P.add("pool", lambda e, h=h: e.tensor_tensor(out=U["Pm"][0][:, h, :], in0=U["A"][:, h, 0:128], in1=id_b[:], op=ALU.add),
                              r=(kb + ("A",),), w=(kb + ("P", 0, h),))
                    if step == 0 and d == 0:
                        chk(203)
                    for h in range(2):
                        P.add("pe", lambda e, h=h: e.matmul(PS[b3][:, 256:384], U["A"][:, h, 128:256], vpad[:, ti, h * 128:(h + 1) * 128], start=(h == 0), stop=(h == 1)),
                              r=(kb + ("A",), "vpad"), w=(psb(b3),))
                    P.add("act", lambda e: e.activation(out=U["akv"][:, 0:64], in_=PS[b3][:, 256:320], func=AF.Copy), r=(psb(b3),), w=(kb + ("akv",),))
                    P.add("act", lambda e: e.activation(out=U["akv"][:, 192:256], in_=PS[b3][:, 320:384], func=AF.Copy), r=(psb(b3),), w=(kb + ("akv",),))
                st.append(s_A)

                def mk_level(lev):
                    cur = lev % 2
                    nxt = (lev + 1) % 2
                    XZn = U["XZ"][nxt]
                    Pc = U["Pm"][cur]
                    Pn = U["Pm"][nxt]
                    wcols = 256 if lev < 5 else 128

                    def s_a():
                        for h in range(2):
                            sbk = b0 + h
                            if lev == 0:
                                Xh = U["X0"][:, h, :]
                                Zh = U["A"][:, h, 0:128]
                                rt = (kb + ("X0",), kb + ("A",))
                            else:
                                Xh = U["XZ"][cur][:, 2 * h, :]
                                Zh = U["XZ"][cur][:, 2 * h + 1, :]
                                rt = (kb + ("XZ", cur, h),)
                            P.add("pe", lambda e, Xh=Xh, Zh=Zh, sbk=sbk: e.matmul(PS[sbk][:, 0:128], Zh, Xh, start=True, stop=True), r=rt, w=(psb(sbk),))
                            if lev < 5:
                                P.add("pe", lambda e, Xh=Xh, Zh=Zh, sbk=sbk: e.matmul(PS[sbk][:, 128:256], Xh, Zh, start=True, stop=True), r=rt, w=(psb(sbk),))
                        for h in range(2):
                            sbk = b0 + h
                            dst = XZn[:, 2 * h:2 * h + 2, :].rearrange("p a b -> p (a b)")[:, 0:wcols]
                            if h == 0:
                                P.add("act", lambda e, dst=dst, sbk=sbk: e.activation(out=dst, in_=PS[sbk][:, 0:wcols], func=AF.Copy), r=(psb(sbk),), w=(kb + ("XZ", nxt, h),))
                            else:
                                P.add("dve", lambda e, dst=dst, sbk=sbk: e.tensor_copy(out=dst, in_=PS[sbk][:, 0:wcols]), r=(psb(sbk),), w=(kb + ("XZ", nxt, h),))

                    def s_b():
                        for h in range(2):
                            P.add("pe", lambda e, h=h: e.matmul(PS[b2][:, h * 128:(h + 1) * 128], XZn[:, 2 * h, :], Pc[:, h, :], start=True, stop=True),
                                  r=(kb + ("XZ", nxt, h), kb + ("P", cur, h)), w=(psb(b2),))
                        for h in range(2):
                            P.add("dve", lambda e, h=h: e.tensor_tensor(out=Pn[:, h, :], in0=PS[b2][:, h * 128:(h + 1) * 128], in1=Pc[:, h, :], op=ALU.add),
                                  r=(psb(b2), kb + ("P", cur, h)), w=(kb + ("P", nxt, h),))
                    return s_a, s_b
                lv = [mk_level(lev) for lev in range(6)]
                st.append(lv[0][0])
                for lev in range(1, 6):
                    st.append(lv[lev][0])
                    st.append(lv[lev - 1][1])
                st.append(lv[5][1])

                def s_tail():
                    TT = U["Pm"][0]
                    for h in range(2):
                        P.add("pe", lambda e, h=h: e.matmul(PS[b3][:, 256:384], TT[:, h, :], U["akv"][:, h * 128:(h + 1) * 128], start=(h == 0), stop=False),
                              r=(kb + ("P", 0, h), kb + ("akv",)), w=(psb(b3),))
                    for h in range(2):
                        P.add("pe", lambda e, h=h: e.matmul(PS[b1][:, 128:256], tkk[:, 2, h * 128:(h + 1) * 128], TT[:, h, :], start=(h == 0), stop=(h == 1)),
                              r=(kb + ("P", 0, h), TKT), w=(psb(b1),))
                    P.add("act", lambda e: e.activation(out=U["wtT"][:], in_=PS[b1][:, 128:256], func=AF.Copy), r=(psb(b1),), w=(kb + ("wtT",),))
                    P.add("act", lambda e: e.activation(out=HB[d][:], in_=HN[d][:], func=AF.Identity, scale=GC[d][:, gcol:gcol + 1]), r=(("HN", d), GCT), w=(("HB", d),))
                    P.add("pe", lambda e: e.matmul(PS[b3][:, 256:384], U["wtT"][:], HB[d][:], start=False, stop=True), r=(kb + ("wtT",), ("HB", d)), w=(psb(b3),))
                    P.add("dve", lambda e: e.tensor_copy(out=U["upad"][:, 0:64], in_=PS[b3][:, 256:320]), r=(psb(b3),), w=(kb + ("upad",),))
                    P.add("dve", lambda e: e.tensor_copy(out=U["upad"][:, 192:256], in_=PS[b3][:, 320:384]), r=(psb(b3),), w=(kb + ("upad",),))
                    yb = PS[b0][:, 0:128]
                    P.add("pe", lambda e: e.matmul(yb, HB[d][:], fm[:, 0, :], start=True, stop=False), r=(("HB", d), FMT), w=(psb(b0),))
                    for h in range(2):
                        P.add("pe", lambda e, h=h: e.matmul(yb, vpad[:, ti, h * 128:(h + 1) * 128], U["A"][:, h, 256:384], start=False, stop=False),
                              r=("vpad", kb + ("A",)), w=(psb(b0),))
                    for h in range(2):
                        P.add("pe", lambda e, h=h: e.matmul(yb, U["upad"][:, h * 128:(h + 1) * 128], U["A"][:, h, 384:512], start=False, stop=(h == 1)),
                              r=(kb + ("upad",), kb + ("A",)), w=(psb(b0),))
                    if step < NT // 2:
                        P.add("act", lambda e: e.activation(out=yacc[:, ts_], in_=yb, func=AF.Copy), r=(psb(b0),), w=("yacc",))
                    else:
                        P.add("dve", lambda e: e.tensor_tensor(out=yacc[:, ts_], in0=yb, in1=yacc[:, ts_], op=ALU.add), r=(psb(b0), "yacc"), w=("yacc",))
                    hb_ = PS[b1][:, 0:128]
                    for h in range(2):
                        P.add("pe", lambda e, h=h: e.matmul(hb_, tkk[:, 0, h * 128:(h + 1) * 128], vpad[:, ti, h * 128:(h + 1) * 128], start=(h == 0), stop=False),
                              r=(TKT, "vpad"), w=(psb(b1),))
                    for h in range(2):
                        P.add("pe", lambda e, h=h: e.matmul(hb_, tkk[:, 1, h * 128:(h + 1) * 128], U["upad"][:, h * 128:(h + 1) * 128], start=False, stop=(h == 1)),
                              r=(TKT, kb + ("upad",)), w=(psb(b1),))
                    P.add("dve", lambda e: e.scalar_tensor_tensor(out=HN[d][:], in0=HN[d][:], scalar=GC[d][:, gcol:gcol + 1], in1=hb_, op0=ALU.mult, op1=ALU.add),
                          r=(psb(b1), ("HN", d), GCT), w=(("HN", d),))
                st.append(s_tail)
                return st

            units = [[unit_stages(0, step), unit_stages(1, step)] for step in range(NT)]
            for ch in units[0]:
                ch[0]()
            for step in range(NT):
                for ch in units[step]:
                    ch[1]()
                if step + 1 < NT:
                    for ch in units[step + 1]:
                        ch[0]()
                for si in range(2, len(units[step][0])):
                    for ch in units[step]:
                        ch[si]()
                chk(140 + step)
            chk(50)
            for q in range(NQ):
                qs = slice(q * 512, (q + 1) * 512)
                bank = q % 4
                tqq = tq[q % 2]
                tqt = ("tq", q % 2)
                P.add("pe", lambda e, qs=qs, bank=bank: e.matmul(PS[bank][:], blk_f[:], yacc[:, qs], start=True, stop=True), r=("yacc",), w=(psb(bank),))
                P.add("dve", lambda e, qs=qs, bank=bank: e.scalar_tensor_tensor(out=tA[:, qs], in0=PS[bank][:], scalar=-1.0 / 64, in1=yacc[:, qs], op0=ALU.mult, op1=ALU.add),
                      r=(psb(bank), "yacc"), w=("tA",))
                P.add("pool", lambda e, qs=qs, tqq=tqq: e.tensor_tensor(out=tqq[:], in0=tA[:, qs], in1=tA[:, qs], op=ALU.mult), r=("tA",), w=(tqt,))
                P.add("pe", lambda e, tqq=tqq, bank=bank: e.matmul(PS[bank][:], blk_f[:], tqq[:], start=True, stop=True), r=(tqt,), w=(psb(bank),))
                P.add("act", lambda e, tqq=tqq, bank=bank: e.activation(out=tqq[:], in_=PS[bank][:], func=AF.Sqrt, bias=epsc[:, 1:2], scale=1.0 / 64),
                      r=(psb(bank),), w=(tqt,))
                P.add("dve", lambda e, tqq=tqq: e.reciprocal(out=tqq[:], in_=tqq[:]), r=(tqt,), w=(tqt,))
                P.add("dve", lambda e, qs=qs, tqq=tqq: e.tensor_tensor(out=tA[:, qs], in0=tA[:, qs], in1=tqq[:], op=ALU.mult), r=("tA", tqt), w=("tA",))
                P.add("act", lambda e, qs=qs: e.activation(out=tA[:, qs], in_=tA[:, qs], func=AF.Identity, bias=par8["lnx_b"][:, hp:hp + 1], scale=par8["lnx_g"][:, hp:hp + 1]),
                      r=("tA",), w=("tA",))
                P.add("pool", lambda e, qs=qs: e.tensor_tensor(out=tA[:, qs], in0=tA[:, qs], in1=tB[:, qs], op=ALU.add), r=("tA", "tB"), w=("tA",))
                gbank = 4 + q % 4
                P.add("pe", lambda e, qs=qs, gbank=gbank: e.matmul(PS[gbank][:], g2a[:, hc], lo_g[:, qs], start=True, stop=False), r=LWW + ("lo_g",), w=(psb(gbank),))
                P.add("pe", lambda e, qs=qs, gbank=gbank: e.matmul(PS[gbank][:], a2g[64:96, hc], lo_a[64:96, qs], start=False, stop=True), r=LWW + ("lo_a",), w=(psb(gbank),))
                P.add("dve", lambda e, qs=qs, gbank=gbank: e.tensor_tensor(out=mixo[:, qs], in0=PS[gbank][:], in1=tA[:, qs], op=ALU.mult), r=(psb(gbank), "tA", "mixo"), w=("mixo",))
            P.add("sp", lambda e: e.dma_start(out=mixT_d[s, hp], in_=mixo[:]), r=("mixo",), w=(("mixd", hp),), chan="mixst")

        if do_rwkv:
            project(G_WFWB, to_raw)
            shift(lo_w[:], "lo_w", om_c[:, 24:25], mue_c[:, 24:25], muo_c[:, 24:25], post=AF.Tanh)
            project(G_AG1, to_raw)
            shift(lo_a[0:64, :], "lo_a", muag[:, 1:2], muag[:, 2:3], muag[:, 3:4], np_=64, p0=0, post=AF.Copy)
            shift(lo_a[64:96, :], "lo_a", muag[:, 1:2], muag[:, 2:3], muag[:, 3:4], np_=32, p0=64, post=AF.Sigmoid)
            project(G_G0, to_raw)
            shift(lo_g[:], "lo_g", mug0[:, 1:2], mug0[:, 2:3], mug0[:, 3:4], post=AF.Sigmoid)
            chk(20)
            for hp in range(8):
                do_hp(hp)
                chk(51)
        else:
            for hp in range(8):
                P.add("pool", lambda e: e.memset(mixo[:], 0.0), r=("mixo",), w=("mixo",))
                P.add("sp", lambda e, hp=hp: e.dma_start(out=mixT_d[s, hp], in_=mixo[:]), r=("mixo",), w=(("mixd", hp),), chan="mixst")

        def do_conv(cc):
            def cons_C(q, ps, ptok):
                P.add("act", lambda e: e.activation(out=tA[:, q * 512:(q + 1) * 512], in_=ps, func=AF.Copy), r=(ptok,), w=("tA",))
            project(G_CONV + 8 + cc, cons_C)
            P.add("pool", zero_pads, r=("rawp",), w=("rawp",))

            def cons_X(q, ps, ptok):
                P.add("dve", lambda e: e.tensor_tensor(out=rawp[:, 1 + q * 512:1 + (q + 1) * 512], in0=ps, in1=tA[:, q * 512:(q + 1) * 512], op=ALU.mult),
                      r=(ptok, "tA"), w=("rawp",))
            project(G_CONV + 16 + cc, cons_X)

            def cons_B(q, ps, ptok):
                P.add("act", lambda e: e.activation(out=tB[:, q * 512:(q + 1) * 512], in_=ps, func=AF.Copy), r=(ptok,), w=("tB",))
            project(G_CONV + cc, cons_B)
            cw = [par8["cw%d" % j][:, cc:cc + 1] for j in range(3)]
            P.add("act", lambda e: e.activation(out=tA[:], in_=rawp[:, 1:T + 1], func=AF.Identity, scale=cw[1]), r=("rawp", "tA"), w=("tA",))
            P.add("dve", lambda e: e.scalar_tensor_tensor(out=tA[:], in0=rawp[:, 0:T], scalar=cw[0], in1=tA[:], op0=ALU.mult, op1=ALU.add), r=("rawp", "tA"), w=("tA",))
            P.add("dve", lambda e: e.scalar_tensor_tensor(out=tA[:], in0=rawp[:, 2:T + 2], scalar=cw[2], in1=tA[:], op0=ALU.mult, op1=ALU.add), r=("rawp", "tA"), w=("tA",))
            P.add("dve", lambda e: e.tensor_tensor(out=mixo[:], in0=tA[:], in1=tB[:], op=ALU.mult), r=("tA", "tB", "mixo"), w=("mixo",))
            P.add("sp", lambda e: e.dma_start(out=mixT_d[s, 8 + cc], in_=mixo[:]), r=("mixo",), w=(("mixd", 8 + cc),), chan="mixst")
        for cc in range(8):
            do_conv(cc)

        P.barrier()
        if stage == 6:
            raise StopIteration

        apos[0] = 0
        mx = carve(KC * 512 * 2, BF16, [128, KC, 512])
        xr = carve(4 * D * 4, F32, [128, 4, D])
        wo = [carve(8 * 512 * 2, BF16, [128, 8, 512]) for _ in range(2)]
        xn2 = carve(D * 4, F32, [128, D])
        junk2 = carve(D * 2, BF16, [128, D])
        NWGU = 3
        wgu = [[carve(KC * 128 * 2, BF16, [128, KC, 128]) for _ in range(NWGU)] for _ in range(2)]
        actT = carve(FC * 512 * 2, BF16, [128, FC, 512])
        wdt = [carve(11 * 512 * 2, BF16, [128, 11, 512]) for _ in range(2)]
        sgt = [carve(512 * 4, F32, [128, 512]) for _ in range(2)]
        dgm = carve(128 * 4, F32, [128, 128])
        nfb = carve(D * 4, F32, [128, D])
        gtb = carve(D * 4, F32, [128, D])
        P.add("sp", lambda e: e.dma_start(out=nfb[:], in_=norm_f_g.partition_broadcast(128)), w=("nfb",), chan="nfb")

        def build_gtb(chunk0):
            for kq in range(4):
                bank = 4 + kq % 2
                for k4 in range(4):
                    kc = kq * 4 + k4
                    P.add("dve", lambda e, kc=kc: e.tensor_scalar(out=dgm[:], in0=id_f[:], scalar1=modT[:, s, chunk0 + kc:chunk0 + kc + 1], scalar2=None, op0=ALU.mult),
                          r=("dgm",), w=("dgm",))
                    P.add("pe", lambda e, k4=k4, bank=bank: e.matmul(PS[bank][:, k4 * 128:(k4 + 1) * 128], ones_f[:], dgm[:], start=True, stop=True), r=("dgm",), w=(psb(bank),))
                P.add("act", lambda e, kq=kq, bank=bank: e.activation(out=gtb[:, kq * 512:(kq + 1) * 512], in_=PS[bank][:], func=AF.Copy), r=(psb(bank),), w=("gtb",))

        wo_i = {"i": 0}
        wgu_i = {"i": 0}
        wd_i = {"i": 0}
        MXT = tuple(("mx", kc) for kc in range(KC))

        def do_tile(tt):
            tsl = slice(tt * 512, (tt + 1) * 512)
            for kc in range(KC):
                P.add("sp", lambda e, kc=kc: e.dma_start(out=mx[:, kc, :], in_=mixT_d[s, kc, :, tsl]), r=(("mixd", kc),), w=(("mx", kc),), chan="mxl")
            for j in range(4):
                P.add("sp", lambda e, j=j: e.dma_start(out=xr[:, j, :], in_=xs[s, tt * 512 + j * 128:tt * 512 + (j + 1) * 128, :]), w=(("xr", j),), chan="xrl")
            if stage == 70:
                raise StopIteration
            build_gtb(32)
            if stage == 71:
                raise StopIteration
            for dg in range(4):
                dsl = slice(dg * 512, (dg + 1) * 512)
                for half in range(2):
                    slot = wo_i["i"] % 2
                    wo_i["i"] += 1
                    P.add("sp", lambda e, slot=slot, dg=dg, half=half: e.dma_start(
                        out=wo[slot][:].rearrange("p a b -> p (a b)"), in_=wout_b[dg][:, half * 8 * 512:(half + 1) * 8 * 512]),
                        r=(("scr", "wout", dg, half),), w=(("wo", slot),), chan="wo%d" % slot)
                    for k8 in range(8):
                        kc = half * 8 + k8
                        for j in range(4):
                            P.add("pe", lambda e, j=j, k8=k8, kc=kc, slot=slot: e.matmul(PS[j][:], mx[:, kc, j * 128:(j + 1) * 128], wo[slot][:, k8, :],
                                                                                       start=(kc == 0), stop=(kc == KC - 1)),
                                  r=(("mx", kc), ("wo", slot)), w=(psb(j),))
                for j in range(4):
                    P.add("dve", lambda e, j=j, dsl=dsl: e.tensor_tensor(out=sgt[j % 2][:], in0=PS[j][:], in1=gtb[:, dsl], op=ALU.mult), r=(psb(j), "gtb"), w=(("sgt", j % 2),))
                    P.add("pool", lambda e, j=j, dsl=dsl: e.tensor_tensor(out=xr[:, j, dsl], in0=xr[:, j, dsl], in1=sgt[j % 2][:], op=ALU.add), r=(("sgt", j % 2), ("xr", j)), w=(("xr", j),))
            if stage == 7:
                raise StopIteration
            for j in range(4):
                norm_transpose(xr[:, j, :], ("xr", j), (s1c if stage == 84 else s2c)[:, s, :], modT[:, s, 0:16] if stage == 84 else modT[:, s, 48:64], mx, slice(j * 128, (j + 1) * 128), (xn2[:], "xn2", junk2[:]),
                               lambda kc: ("mx", kc), bank0=6)
            if stage in (8, 81, 82, 83, 84):
                raise StopIteration
            for fb in range(FC):
                slot = wgu_i["i"] % NWGU
                wgu_i["i"] += 1
                P.add("sp", lambda e, fb=fb, slot=slot: e.dma_start(out=wgu[0][slot][:].rearrange("p a b -> p (a b)"), in_=wg_b[fb]),
                      r=(("scr", "wg", fb),), w=(("wgu", 0, slot),), chan="wgu%d" % slot)
                P.add("sp", lambda e, fb=fb, slot=slot: e.dma_start(out=wgu[1][slot][:].rearrange("p a b -> p (a b)"), in_=wu_b[fb]),
                      r=(("scr", "wu", fb),), w=(("wgu", 1, slot),), chan="wgu%d" % slot)
                gb = 4 + 2 * (fb % 2)
                for w_ in range(2):
                    for kc in range(KC):
                        P.add("pe", lambda e, w_=w_, kc=kc, slot=slot, gb=gb: e.matmul(PS[gb + w_][:], wgu[w_][slot][:, kc, :], mx[:, kc, :], start=(kc == 0), stop=(kc == KC - 1)),
                              r=(("wgu", w_, slot), ("mx", kc)), w=(psb(gb + w_),))
                P.add("act", lambda e, fb=fb, gb=gb: e.activation(out=sgt[fb % 2][:], in_=PS[gb][:], func=AF.Silu), r=(psb(gb),), w=(("sgt", fb % 2),))
                P.add("dve", lambda e, fb=fb, gb=gb: e.tensor_tensor(out=actT[:, fb, :], in0=PS[gb + 1][:], in1=sgt[fb % 2][:], op=ALU.mult),
                      r=(psb(gb + 1), ("sgt", fb % 2)), w=(("actT", fb),))
            if stage == 9:
                raise StopIteration
            build_gtb(80)
            for dg in range(4):
                dsl = slice(dg * 512, (dg + 1) * 512)
                for fq in range(4):
                    slot = wd_i["i"] % 2
                    wd_i["i"] += 1
                    P.add("sp", lambda e, slot=slot, dg=dg, fq=fq: e.dma_start(
                        out=wdt[slot][:].rearrange("p a b -> p (a b)"), in_=wd_b[dg, fq]),
                        r=(("scr", "wd", dg, fq),), w=(("wdt", slot),), chan="wd%d" % slot)
                    for f11 in range(11):
                        fc = fq * 11 + f11
                        for j in range(4):
                            P.add("pe", lambda e, j=j, f11=f11, fc=fc, slot=slot: e.matmul(PS[j][:], actT[:, fc, j * 128:(j + 1) * 128], wdt[slot][:, f11, :],
                                                                                         start=(fc == 0), stop=(fc == FC - 1)),
                                  r=(("actT", fc), ("wdt", slot)), w=(psb(j),))
                for j in range(4):
                    P.add("dve", lambda e, j=j, dsl=dsl: e.tensor_tensor(out=sgt[j % 2][:], in0=PS[j][:], in1=gtb[:, dsl], op=ALU.mult), r=(psb(j), "gtb"), w=(("sgt", j % 2),))
                    P.add("pool", lambda e, j=j, dsl=dsl: e.tensor_tensor(out=xr[:, j, dsl], in0=xr[:, j, dsl], in1=sgt[j % 2][:], op=ALU.add), r=(("sgt", j % 2), ("xr", j)), w=(("xr", j),))
            if stage == 10:
                raise StopIteration
            for j in range(4):
                P.add("act", lambda e, j=j: e.activation(out=junk2[:], in_=xr[:, j, :], func=AF.Square, accum_out=stat[:, 2:3]), r=(("xr", j),), w=("stat", "junk"))
                rstd_from_ss(stat[:, 2:3], stat[:, 3:4], 1.0 / D, 0)
                P.add("dve", lambda e, j=j: e.scalar_tensor_tensor(out=xr[:, j, :], in0=xr[:, j, :], scalar=stat[:, 3:4], in1=nfb[:], op0=ALU.mult, op1=ALU.mult),
                      r=(("xr", j), "stat", "nfb"), w=(("xr", j),))
                P.add("sp", lambda e, j=j: e.dma_start(out=ys[s, tt * 512 + j * 128:tt * 512 + (j + 1) * 128, :], in_=xr[:, j, :]), r=(("xr", j),), w=(("ysd", j),), chan="yst")
        for tt in range(NQ):
            do_tile(tt)
        P.barrier()

    try:
        for s in range(NSEQ):
            do_seq(s)
    except StopIteration:
        pass

    P.finish()
    P.emit(nc, es)
    es.close()
    return nc


_W_NAMES = ["w_ada", "b_ada", "norm1_g", "w_in", "mu_shift", "w0_decay", "w2_decay", "a0", "a2", "g2_gate", "k_k", "k_a", "r_k",
            "lnx_g", "lnx_b", "conv_w", "w_out", "norm2_g", "w_ffn_gate", "w_ffn_up", "w_ffn_down"]


def _weights_map(inputs):
    m = {}
    for nm in _W_NAMES:
        a = np.asarray(inputs[nm], dtype=np.float32)
        a = a[0]
        if nm == "r_k":
            a = a.reshape(-1)
        m[nm] = np.ascontiguousarray(a)
    m["norm_f_g"] = np.ascontiguousarray(np.asarray(inputs["norm_f_g"], dtype=np.float32))
    return m


def kernel(**inputs):
    x_prompt = np.asarray(inputs["x_prompt"], dtype=np.float32)
    x_sample = np.asarray(inputs["x_sample"], dtype=np.float32)
    c_prompt = np.asarray(inputs["c_prompt"], dtype=np.float32)
    c_sample = np.asarray(inputs["c_sample"], dtype=np.float32)
    NB, T = x_prompt.shape[0], x_prompt.shape[1]
    NS = x_sample.shape[0]
    ntot = NB + NS
    NSEQ = 3
    ncore = 8

    def getx(i):
        return x_prompt[i] if i < NB else x_sample[i - NB]

    def getc(i):
        return c_prompt[i] if i < NB else c_sample[i - NB]

    wm = _weights_map(inputs)
    in_maps = []
    assign = []
    for c in range(ncore):
        ids = [c, c + 8, c + 16 if c + 16 < ntot else c]
        assign.append(ids)
        m = dict(wm)
        m["xs"] = np.stack([getx(i) for i in ids])
        m["cs"] = np.stack([getc(i) for i in ids])
        in_maps.append(m)
    nc = build_program(NSEQ, T)
    res = run_bass_kernel_spmd(nc, in_maps, core_ids=list(range(ncore)))
    y_p = np.empty_like(x_prompt)
    y_s = np.empty_like(x_sample)
    for c in range(ncore):
        ysc = res.results[c]["ys"]
        for slot, i in enumerate(assign[c]):
            if slot == 2 and c + 16 >= ntot:
                continue
            if i < NB:
                y_p[i] = ysc[slot]
            else:
                y_s[i - NB] = ysc[slot]
    return (y_p, y_s)
```
